# Optimizing a Trainium2 kernel written in Bass

```python
import math, functools
import jax, jax.numpy as jnp
from jax import lax
import numpy as np

D_MODEL = 1024
BATCH = 4
SEQ = 4096
DEPTH = 2

GRID_W = 64
CTX_LEN = 256
N_MIXERS = 2
N_A_LAYERS = (DEPTH + 1) // 2
N_B_LAYERS = DEPTH // 2
D_FF = 2816
N_MOD = 9
EPS = 1e-6
ROPE_BASE = 10000.0
Q_BLOCK = 128
NEG_INF = -1e30

A_HEADS = 8
A_HEAD_DIM = 64
A_V_DIM = 2 * A_HEAD_DIM
A_QKV_COLS = 3 * A_HEADS * A_V_DIM
A_WIDTH = A_HEADS * A_V_DIM

B_Q_HEADS = 16
B_KV_HEADS = 4
B_HEAD_DIM = 64
B_GROUP = B_Q_HEADS // B_KV_HEADS
B_QKV_COLS = (B_Q_HEADS + 2 * B_KV_HEADS) * B_HEAD_DIM
B_WIDTH = B_Q_HEADS * B_HEAD_DIM
WINDOW = 128
BAND_BLOCK = 128

kernel_name = "hybrid_diffattn_swagqa_macaron_dit"


def rms(x):
    xf = x.astype(jnp.float32)
    return (xf * lax.rsqrt(jnp.mean(xf * xf, axis=-1, keepdims=True) + EPS)).astype(x.dtype)


def rms_gain(x, g):
    return rms(x) * g


def modulate(h, shift, scale):
    return h * (1 + scale) + shift


def swiglu(h, wi, wo):
    g, u = jnp.split(h @ wi, 2, axis=-1)
    return (jax.nn.silu(g) * u) @ wo


def axial_rope_tables(rows, head_dim):
    nf = head_dim // 4
    inv = ROPE_BASE ** (-jnp.arange(nf, dtype=jnp.float32) / nf)
    row = jnp.broadcast_to(jnp.arange(rows, dtype=jnp.float32)[:, None], (rows, GRID_W)).reshape(-1)
    col = jnp.broadcast_to(jnp.arange(GRID_W, dtype=jnp.float32)[None, :], (rows, GRID_W)).reshape(-1)
    ang = jnp.stack([row[:, None] * inv, col[:, None] * inv], axis=1)
    ang = jnp.stack([ang, ang], axis=2).reshape(rows * GRID_W, head_dim)
    return jnp.cos(ang), jnp.sin(ang)


def apply_rope(x, cos, sin):
    dh = x.shape[-1]
    xr = x.reshape(x.shape[:-1] + (2, 2, dh // 4))
    rot = jnp.stack([-xr[..., 1, :], xr[..., 0, :]], axis=-2).reshape(x.shape)
    return (x * cos + rot * sin).astype(x.dtype)


def diff_attention(h_lat, h_ctx, w_qkv, w_o, q_gain, k_gain, lam_vec, subln_gain, lam_init, cos, sin, ctx_out):
    B, L, _ = h_lat.shape

    def project(h):
        q, k, v = jnp.split(h @ w_qkv, [A_WIDTH, 2 * A_WIDTH], axis=-1)
        q = rms_gain(q.reshape(B, -1, A_HEADS, 2, A_HEAD_DIM), q_gain)
        k = rms_gain(k.reshape(B, -1, A_HEADS, 2, A_HEAD_DIM), k_gain)
        return q, k, v.reshape(B, -1, A_HEADS, A_V_DIM)

    q_l, k_l, v_l = project(h_lat)
    q_c, k_c, v_c = project(h_ctx)
    rc, rs = cos[:, None, None, :], sin[:, None, None, :]
    q_l, k_l = apply_rope(q_l, rc, rs), apply_rope(k_l, rc, rs)

    lv = lam_vec.astype(jnp.float32)
    lam = jnp.exp(jnp.sum(lv[0] * lv[1])) - jnp.exp(jnp.sum(lv[2] * lv[3])) + lam_init
    scale = A_HEAD_DIM ** -0.5

    def attend(q, k, v):
        s = jnp.einsum('bqhcd,bkhcd->bhcqk', q, k).astype(jnp.float32) * scale
        p = jax.nn.softmax(s, axis=-1)
        a = (p[:, :, 0] - lam * p[:, :, 1]).astype(v.dtype)
        return jnp.einsum('bhqk,bkhe->bqhe', a, v)

    def finish(o):
        o = rms_gain(o, subln_gain) * (1.0 - lam_init)
        return o.reshape(o.shape[0], o.shape[1], A_WIDTH) @ w_o

    k_all = jnp.concatenate([k_l, k_c], axis=1)
    v_all = jnp.concatenate([v_l, v_c], axis=1)
    nb = L // Q_BLOCK
    qb = q_l.reshape(B, nb, Q_BLOCK, A_HEADS, 2, A_HEAD_DIM).transpose(1, 0, 2, 3, 4, 5)
    o_l = lax.map(lambda qblk: attend(qblk, k_all, v_all), qb)
    o_l = o_l.transpose(1, 0, 2, 3, 4).reshape(B, L, A_HEADS, A_V_DIM)
    out_l = finish(o_l)
    out_c = finish(attend(q_c, k_c, v_c)) if ctx_out else None
    return out_l, out_c


def window_gqa(h_lat, h_ctx, w_qkv, w_o, q_gain, k_gain, sink, cos, sin, ctx_out):
    B, L, _ = h_lat.shape
    nb = L // BAND_BLOCK
    BB = BAND_BLOCK

    def project(h):
        q, k, v = jnp.split(h @ w_qkv, [B_WIDTH, B_WIDTH + B_KV_HEADS * B_HEAD_DIM], axis=-1)
        q = rms_gain(q.reshape(B, -1, B_KV_HEADS, B_GROUP, B_HEAD_DIM), q_gain)
        k = rms_gain(k.reshape(B, -1, B_KV_HEADS, B_HEAD_DIM), k_gain)
        return q, k, v.reshape(B, -1, B_KV_HEADS, B_HEAD_DIM)

    q_l, k_l, v_l = project(h_lat)
    q_c, k_c, v_c = project(h_ctx)
    q_l = apply_rope(q_l, cos[:, None, None, :], sin[:, None, None, :])
    k_l = apply_rope(k_l, cos[:, None, :], sin[:, None, :])
    scale = B_HEAD_DIM ** -0.5
    sink_f = sink.astype(jnp.float32).reshape(B_KV_HEADS, B_GROUP)

    def band(t):
        tb = jnp.pad(t, ((0, 0), (BB, BB), (0, 0), (0, 0))).reshape(B, nb + 2, BB, B_KV_HEADS, B_HEAD_DIM)
        return jnp.concatenate([tb[:, :-2], tb[:, 1:-1], tb[:, 2:]], axis=2)

    k_w, v_w = band(k_l), band(v_l)
    qb = q_l.reshape(B, nb, BB, B_KV_HEADS, B_GROUP, B_HEAD_DIM)
    s_w = jnp.einsum('bnqhgd,bnkhd->bnhgqk', qb, k_w).astype(jnp.float32) * scale
    s_c = jnp.einsum('bnqhgd,bkhd->bnhgqk', qb, k_c).astype(jnp.float32) * scale
    qi = jnp.arange(BB)
    kj = jnp.arange(3 * BB)
    rel = kj[None, :] - BB - qi[:, None]
    kpos = jnp.arange(nb)[:, None] * BB - BB + kj[None, :]
    valid = (jnp.abs(rel) <= WINDOW)[None] & ((kpos >= 0) & (kpos < L))[:, None, :]
    s_w = jnp.where(valid[None, :, None, None], s_w, NEG_INF)
    sink_b = jnp.broadcast_to(sink_f[None, None, :, :, None, None], s_w.shape[:-1] + (1,))
    n_ctx = k_c.shape[1]
    p = jax.nn.softmax(jnp.concatenate([s_w, s_c, sink_b], axis=-1), axis=-1)
    p_w = p[..., :3 * BB].astype(v_w.dtype)
    p_c = p[..., 3 * BB:3 * BB + n_ctx].astype(v_c.dtype)
    o = jnp.einsum('bnhgqk,bnkhd->bnqhgd', p_w, v_w) + jnp.einsum('bnhgqk,bkhd->bnqhgd', p_c, v_c)
    out_l = o.reshape(B, L, B_WIDTH) @ w_o
    out_c = None
    if ctx_out:
        s = jnp.einsum('bqhgd,bkhd->bhgqk', q_c, k_c).astype(jnp.float32) * scale
        sb = jnp.broadcast_to(sink_f[None, :, :, None, None], s.shape[:-1] + (1,))
        pc = jax.nn.softmax(jnp.concatenate([s, sb], axis=-1), axis=-1)[..., :n_ctx].astype(v_c.dtype)
        oc = jnp.einsum('bhgqk,bkhd->bqhgd', pc, v_c)
        out_c = oc.reshape(B, n_ctx, B_WIDTH) @ w_o
    return out_l, out_c


def layer(x, xc, mod_l, mod_c, pre_wi, pre_wo, post_wi, post_wo, mixer, ctx_out):
    sh1, sc1, g1, sh2, sc2, g2, sh3, sc3, g3 = jnp.split(mod_l, N_MOD, axis=-1)
    ch1, cc1, cg1, ch2, cc2, cg2, ch3, cc3, cg3 = jnp.split(mod_c, N_MOD, axis=-1)

    def ffn_step(t, sh, sc, g, wi, wo):
        return t + 0.5 * g * swiglu(modulate(rms(t), sh, sc), wi, wo)

    x = ffn_step(x, sh1, sc1, g1, pre_wi, pre_wo)
    xc = ffn_step(xc, ch1, cc1, cg1, pre_wi, pre_wo)
    o_l, o_c = mixer(h_lat=modulate(rms(x), sh2, sc2), h_ctx=modulate(rms(xc), ch2, cc2), ctx_out=ctx_out)
    x = x + g2 * o_l
    x = ffn_step(x, sh3, sc3, g3, post_wi, post_wo)
    if ctx_out:
        xc = xc + cg2 * o_c
        xc = ffn_step(xc, ch3, cc3, cg3, post_wi, post_wo)
    return x, xc


def setup_inputs(seed: int = 0) -> dict:
    key = jax.random.key(seed)
    ks = jax.random.split(key, 24)
    f32 = jnp.float32
    D = D_MODEL

    def nrm(k, shape, s):
        return jax.random.normal(k, shape, f32) * s

    return {
        "x": nrm(ks[0], (BATCH, SEQ, D), 1.0),
        "c": nrm(ks[1], (BATCH, D), 1.0),
        "ctx": nrm(ks[2], (BATCH, CTX_LEN, D), 1.0),
        "c_ctx": nrm(ks[3], (D,), 1.0),
        "ada_w": nrm(ks[4], (DEPTH, D, N_MOD * D), 0.5 * D ** -0.5),
        "ada_b": nrm(ks[5], (DEPTH, N_MOD * D), 0.02),
        "ffn_pre_wi": nrm(ks[6], (DEPTH, D, 2 * D_FF), D ** -0.5),
        "ffn_pre_wo": nrm(ks[7], (DEPTH, D_FF, D), D_FF ** -0.5),
        "ffn_post_wi": nrm(ks[8], (DEPTH, D, 2 * D_FF), D ** -0.5),
        "ffn_post_wo": nrm(ks[9], (DEPTH, D_FF, D), D_FF ** -0.5),
        "a_w_qkv": nrm(ks[10], (N_A_LAYERS, D, A_QKV_COLS), D ** -0.5),
        "a_w_o": nrm(ks[11], (N_A_LAYERS, A_WIDTH, D), A_WIDTH ** -0.5),
        "a_q_gain": 1.0 + nrm(ks[12], (N_A_LAYERS, A_HEAD_DIM), 0.1),
        "a_k_gain": 1.0 + nrm(ks[13], (N_A_LAYERS, A_HEAD_DIM), 0.1),
        "a_lambda": nrm(ks[14], (N_A_LAYERS, 4, A_HEAD_DIM), 0.1),
        "a_subln_gain": 1.0 + nrm(ks[15], (N_A_LAYERS, A_V_DIM), 0.1),
        "b_w_qkv": nrm(ks[16], (N_B_LAYERS, D, B_QKV_COLS), D ** -0.5),
        "b_w_o": nrm(ks[17], (N_B_LAYERS, B_WIDTH, D), B_WIDTH ** -0.5),
        "b_q_gain": 1.0 + nrm(ks[18], (N_B_LAYERS, B_HEAD_DIM), 0.1),
        "b_k_gain": 1.0 + nrm(ks[19], (N_B_LAYERS, B_HEAD_DIM), 0.1),
        "b_sink": nrm(ks[20], (N_B_LAYERS, B_Q_HEADS), 0.5),
    }


def reference(x, c, ctx, c_ctx, ada_w, ada_b, ffn_pre_wi, ffn_pre_wo, ffn_post_wi, ffn_post_wo,
              a_w_qkv, a_w_o, a_q_gain, a_k_gain, a_lambda, a_subln_gain,
              b_w_qkv, b_w_o, b_q_gain, b_k_gain, b_sink):
    n_tok = x.shape[1]
    rows = n_tok // GRID_W
    cos, sin = axial_rope_tables(rows, A_HEAD_DIM)
    xc = ctx
    for i in range(DEPTH):
        mod_l = (jax.nn.silu(c) @ ada_w[i] + ada_b[i])[:, None, :]
        mod_c = jax.nn.silu(c_ctx) @ ada_w[i] + ada_b[i]
        ctx_out = i < DEPTH - 1
        j = i // N_MIXERS
        if i % N_MIXERS == 0:
            mixer = functools.partial(diff_attention, w_qkv=a_w_qkv[j], w_o=a_w_o[j], q_gain=a_q_gain[j],
                                      k_gain=a_k_gain[j], lam_vec=a_lambda[j], subln_gain=a_subln_gain[j],
                                      lam_init=0.8 - 0.6 * math.exp(-0.3 * i), cos=cos, sin=sin)
        else:
            mixer = functools.partial(window_gqa, w_qkv=b_w_qkv[j], w_o=b_w_o[j], q_gain=b_q_gain[j],
                                      k_gain=b_k_gain[j], sink=b_sink[j], cos=cos, sin=sin)
        x, xc = layer(x, xc, mod_l, mod_c, ffn_pre_wi[i], ffn_pre_wo[i], ffn_post_wi[i], ffn_post_wo[i],
                      mixer, ctx_out)
    return x
```

```python
import contextlib
import math
import numpy as np
import ml_dtypes
import concourse.bass as bass
import concourse.mybir as mybir
from concourse.bass_utils import run_bass_kernel_spmd

F32 = mybir.dt.float32
BF16 = mybir.dt.bfloat16
AF = mybir.ActivationFunctionType
ALU = mybir.AluOpType
AX = mybir.AxisListType

D = 1024
FF = 2816
NFC = 22
SEQ = 4096
NCTX = 256
OWN = 2048
HALO = 128
XR = OWN + HALO + NCTX
NKA = SEQ + NCTX
EPS = 1e-6
LAM_INIT0 = 0.8 - 0.6 * math.exp(-0.3 * 0)

ENGS = ("pe", "act", "dve", "pool", "sp")


class Buf:
    __slots__ = ("name", "writers", "readers", "sem", "dma_count", "group", "persist")

    def __init__(self, name, group=False, persist=False):
        self.name = name
        self.writers = []
        self.readers = {}
        self.sem = None
        self.dma_count = 0
        self.group = group
        self.persist = persist


class Op:
    __slots__ = ("eng", "fn", "deps", "is_dma", "signal", "semval", "buf", "prog")

    def __init__(self, eng, fn, is_dma, prog):
        self.eng = eng
        self.fn = fn
        self.deps = []
        self.is_dma = is_dma
        self.signal = False
        self.semval = None
        self.buf = None
        self.prog = prog


class Prog:
    def __init__(self, nc, name):
        self.nc = nc
        self.name = name
        self.ops = {e: [] for e in ENGS}
        self.all_ops = []

    def _key(self, op):
        return ("dma", id(op)) if op.is_dma else op.eng

    def _live(self, d):
        return d.prog is self or (d.is_dma and d.buf.persist)

    def add(self, eng, fn, reads=(), writes=(), dma_dst=None, after=()):
        op = Op(eng, fn, dma_dst is not None, self)
        op.buf = dma_dst
        deps = list(after)
        for b in reads:
            for w in b.writers:
                if w.is_dma or op.is_dma or w.eng != eng or eng != "pe":
                    deps.append(w)
        for b in writes:
            for r in b.readers.values():
                if r.is_dma or op.is_dma or r.eng != eng or eng != "pe":
                    deps.append(r)
            if not b.group:
                for w in b.writers:
                    if w.is_dma or op.is_dma or w.eng != eng or eng != "pe":
                        deps.append(w)
        seen = set()
        for d in deps:
            if id(d) not in seen and d is not op and self._live(d):
                seen.add(id(d))
                op.deps.append(d)
                d.signal = True
        for b in writes:
            if b.group:
                b.writers.append(op)
            else:
                b.writers = [op]
                b.readers = {}
        for b in reads:
            b.readers[self._key(op)] = op
        if op.is_dma:
            dma_dst.dma_count += 1
            op.semval = 16 * dma_dst.dma_count
        self.ops[eng].append(op)
        self.all_ops.append(op)
        return op

    def emit(self, C):
        nc = self.nc
        for e in ENGS:
            comp = [o for o in self.ops[e] if not o.is_dma]
            if comp:
                comp[-1].signal = True
            c = 0
            for op in comp:
                if op.signal:
                    c += 1
                    op.semval = c
        dma_bufs = []
        for op in self.all_ops:
            if op.is_dma and (not op.buf.persist) and op.buf not in dma_bufs:
                dma_bufs.append(op.buf)
        k = C.phase_idx
        cur, other = C.semsets[k % 2], C.semsets[(k + 1) % 2]
        assert len(dma_bufs) <= len(cur["dma"]), (self.name, len(dma_bufs))
        esem = cur["eng"]
        bar = C.bar
        bar_target = len(ENGS) * (k + 1)
        for i, b in enumerate(dma_bufs):
            b.sem = cur["dma"][i]
        with nc.Block() as block:

            def run(eng_name):
                def body(eng):
                    if eng_name == "pool" and k >= 1:
                        for sname in ENGS:
                            eng.sem_clear(other["eng"][sname])
                        for sm in other["dma"]:
                            eng.sem_clear(sm)
                    waited = {}
                    last_comp = None
                    my_dma = {}
                    for op in self.ops[eng_name]:
                        need = {}
                        for d in op.deps:
                            sem = d.buf.sem if d.is_dma else esem[d.eng]
                            kk = id(sem)
                            if kk not in need or need[kk][1] < d.semval:
                                need[kk] = (sem, d.semval)
                        for kk, (sem, v) in need.items():
                            if waited.get(kk, 0) < v:
                                eng.wait_ge(sem, v)
                                waited[kk] = v
                        ins = op.fn(eng)
                        if op.is_dma:
                            ins.then_inc(op.buf.sem, 16)
                            if not op.buf.persist:
                                my_dma[id(op.buf)] = (op.buf, op.semval)
                        else:
                            last_comp = op
                            if op.signal:
                                ins.then_inc(esem[eng_name], 1)
                    for b, v in my_dma.values():
                        eng.wait_ge(b.sem, v)
                    if last_comp is not None:
                        eng.wait_ge(esem[eng_name], last_comp.semval)
                    eng.sem_inc(bar, 1)
                    eng.wait_ge(bar, bar_target)
                return body

            block.tensor(run("pe"))
            block.scalar(run("act"))
            block.vector(run("dve"))
            block.gpsimd(run("pool"))
            block.sync(run("sp"))
        C.phase_idx += 1
        for b in dma_bufs:
            b.dma_count = 0
            b.sem = None
            b.writers = []
            b.readers = {}


class Ctx:
    pass


def mm(C, out, lhsT, rhs, start, stop, reads, writes, **kw):
    return C.P.add("pe", lambda e: e.matmul(out, lhsT, rhs, start=start, stop=stop, **kw), reads, writes)


def act(C, out, in_, func, reads, writes, **kw):
    C.P.add("act", lambda e: e.activation(out=out, in_=in_, func=func, **kw), reads, writes)


def stt(C, out, in0, scalar, in1, op0, op1, reads, writes, eng="dve"):
    C.P.add(eng, lambda e: e.scalar_tensor_tensor(out=out, in0=in0, scalar=scalar, in1=in1, op0=op0, op1=op1), reads, writes)


def tt(C, out, in0, in1, op, reads, writes, eng="dve"):
    C.P.add(eng, lambda e: e.tensor_tensor(out=out, in0=in0, in1=in1, op=op), reads, writes)


def ts(C, out, in0, s1, s2, op0, op1, reads, writes, eng="dve"):
    if s2 is None:
        C.P.add(eng, lambda e: e.tensor_scalar(out=out, in0=in0, scalar1=s1, scalar2=None, op0=op0), reads, writes)
    else:
        C.P.add(eng, lambda e: e.tensor_scalar(out=out, in0=in0, scalar1=s1, scalar2=s2, op0=op0, op1=op1), reads, writes)


def cp(C, out, in_, reads, writes, eng="dve"):
    C.P.add(eng, lambda e: e.tensor_copy(out=out, in_=in_), reads, writes)


def dma(C, eng, out, in_, reads, writes, dst, after=()):
    return C.P.add(eng, lambda e: e.dma_start(out=out, in_=in_), reads, writes, dma_dst=dst, after=after)


def pipeline(stages, n):
    ns = len(stages)
    for s in range(n + ns - 1):
        for k in reversed(range(ns)):
            c = s - k
            if 0 <= c < n:
                stages[k](c)


class Stream:
    def __init__(self, C, tens, name):
        self.C = C
        self.tens = tens
        self.bufs = [Buf("%s%d" % (name, i)) for i in range(len(tens))]
        self.i = 0

    def load(self, src, src_bufs, view=None, eng="sp"):
        s = self.i % len(self.tens)
        self.i += 1
        t = self.tens[s]
        dst = view(t) if view is not None else t
        dma(self.C, eng, dst, src, list(src_bufs), [self.bufs[s]], self.bufs[s])
        return t, self.bufs[s]


def _perm(h):
    if h == 0:
        own = np.arange(0, 2048)
        halo = np.arange(2048, 2176)
        rest = np.arange(2176, 4096)
    else:
        own = np.arange(2048, 4096)
        halo = np.arange(1920, 2048)
        rest = np.arange(0, 1920)
    return np.concatenate([own, halo, rest])


def _rope_tables(pos):
    nf = 16
    inv = (10000.0 ** (-np.arange(nf, dtype=np.float32) / nf)).astype(np.float32)
    row = (pos // 64).astype(np.float32)
    col = (pos % 64).astype(np.float32)
    ang = np.zeros((64, pos.shape[0]), np.float32)
    for d in range(64):
        a = d // 32
        f = d % 16
        ang[d] = (row if a == 0 else col) * inv[f]
    ang = np.concatenate([ang, ang], 0)
    return np.cos(ang).astype(np.float32), np.sin(ang).astype(np.float32)


def _chunk_w(W, cols):
    return np.ascontiguousarray(W[:, cols].reshape(8, 128, len(cols)).transpose(1, 0, 2))


def prep_shared(inp):
    f = np.float32
    out = {}
    ada_w = np.asarray(inp["ada_w"], f)
    out["ada_h"] = np.ascontiguousarray(
        ada_w.reshape(2, 8, 128, 72, 128).transpose(0, 3, 2, 1, 4)).reshape(144, 128, 1024)
    ada_b = np.asarray(inp["ada_b"], f)
    out["ada_bT"] = np.ascontiguousarray(ada_b.reshape(2, 72, 128).transpose(2, 0, 1)).reshape(128, 144)
    wi_all, wo_all = [], []
    for l in range(2):
        for (wi_name, wo_name) in (("ffn_pre_wi", "ffn_pre_wo"), ("ffn_post_wi", "ffn_post_wo")):
            wi = np.asarray(inp[wi_name][l], f)
            wo = np.asarray(inp[wo_name][l], f)
            g = wi[:, :FF].reshape(8, 128, NFC, 128)
            u = wi[:, FF:].reshape(8, 128, NFC, 128)
            gu = np.stack([g, u], axis=3)
            wi_all.append(np.ascontiguousarray(gu.transpose(2, 1, 0, 3, 4)).reshape(NFC * 128, 2048))
            wo_all.append(np.ascontiguousarray(wo.reshape(NFC, 128, 8, 128).transpose(2, 1, 0, 3)).reshape(8 * 128, FF))
    out["wi_h"] = np.stack(wi_all)
    out["wo_h"] = np.stack(wo_all)
    wa = np.asarray(inp["a_w_qkv"][0], f)
    chunks = [1024 + h * 128 for h in range(8)] + [h * 128 for h in range(8)]
    qk = [_chunk_w(wa, np.arange(c0, c0 + 128)) for c0 in chunks]
    out["wqkA_h"] = np.ascontiguousarray(
        np.stack([np.stack([qk[2 * p], qk[2 * p + 1]], axis=1) for p in range(8)])).reshape(8 * 128, 2048)
    out["wvA_h"] = np.ascontiguousarray(
        np.stack([_chunk_w(wa, np.arange(2048 + pc * 256, 2048 + (pc + 1) * 256)) for pc in range(4)])).reshape(4 * 128, 2048)
    woa = np.asarray(inp["a_w_o"][0], f)
    out["woA_h"] = np.ascontiguousarray(woa.reshape(8, 128, 8, 128).transpose(2, 1, 0, 3)).reshape(8 * 128, 1024)
    wb = np.asarray(inp["b_w_qkv"][0], f)
    colsB = []
    for kvh in range(4):
        c = np.arange(1024 + kvh * 64, 1024 + (kvh + 1) * 64)
        colsB.append(np.concatenate([c, c]))
    for j in range(8):
        colsB.append(np.arange(j * 128, (j + 1) * 128))
    qkb = [_chunk_w(wb, c) for c in colsB]
    out["wqkB_h"] = np.ascontiguousarray(
        np.stack([np.stack([qkb[2 * p], qkb[2 * p + 1]], axis=1) for p in range(6)])).reshape(6 * 128, 2048)
    out["wvB_h"] = _chunk_w(wb, np.arange(1280, 1536)).reshape(128, 2048)
    wob = np.asarray(inp["b_w_o"][0], f)
    out["woB_h"] = np.ascontiguousarray(wob.reshape(8, 128, 8, 128).transpose(2, 1, 0, 3)).reshape(8 * 128, 1024)
    sp = np.zeros((128, 32), f)
    sp[:, 0] = np.tile(np.asarray(inp["a_q_gain"][0], f), 2)
    sp[:, 1] = np.tile(np.asarray(inp["a_k_gain"][0], f), 2)
    sp[:, 2] = np.tile(np.asarray(inp["b_q_gain"][0], f), 2)
    sp[:, 3] = np.tile(np.asarray(inp["b_k_gain"][0], f), 2)
    sp[:, 4] = np.asarray(inp["a_subln_gain"][0], f)
    sp[:, 5:21] = np.asarray(inp["b_sink"][0], f)[None, :]
    out["smallp"] = sp
    lam = np.asarray(inp["a_lambda"][0], f)
    out["lamv"] = np.concatenate([lam[0], lam[2], lam[1], lam[3]])[None, :].astype(f)
    rm = np.zeros((128, 128), f)
    for m in range(128):
        if (m % 32) < 16:
            rm[m + 16, m] = -1.0
        else:
            rm[m - 16, m] = 1.0
    out["rmat"] = rm
    return out


def prep_core(inp, b, h):
    f = np.float32
    out = {}
    perm = _perm(h)
    x = np.asarray(inp["x"], f)
    out["xT"] = np.ascontiguousarray(x[b][perm].T)
    out["ctxT"] = np.ascontiguousarray(np.asarray(inp["ctx"], f)[b].T)
    cv = np.stack([np.asarray(inp["c"], f)[b], np.asarray(inp["c_ctx"], f)], axis=1)
    out["cvec"] = np.ascontiguousarray(cv.reshape(8, 128, 2).transpose(1, 0, 2)).reshape(128, 16)
    cos, sin = _rope_tables(perm)
    out["cos_t"] = cos
    out["sin_t"] = sin
    cs_hc = np.zeros((128, 2, HALO + NCTX), f)
    cs_hc[:, 0, :HALO] = cos[:, OWN:OWN + HALO]
    cs_hc[:, 0, HALO:] = 1.0
    cs_hc[:, 1, :HALO] = sin[:, OWN:OWN + HALO]
    out["cs_hc"] = cs_hc
    jj = np.arange(128)[:, None]
    ii = np.arange(128)[None, :]
    mprev = (jj >= ii).astype(f)
    mnext = (jj <= ii).astype(f)
    out["masks"] = np.ascontiguousarray(
        np.stack([mprev, mnext, mprev * (1.0 if h == 1 else 0.0), mnext * (1.0 if h == 0 else 0.0)], axis=1)).reshape(128, 512)
    return out


SHARED_SHAPES = {
    "ada_h": [144, 128, 1024], "ada_bT": [128, 144], "wi_h": [4, NFC * 128, 2048], "wo_h": [4, 1024, FF],
    "wqkA_h": [1024, 2048], "wvA_h": [512, 2048], "woA_h": [1024, 1024], "wqkB_h": [768, 2048],
    "wvB_h": [128, 2048], "woB_h": [1024, 1024], "smallp": [128, 32], "lamv": [1, 256], "rmat": [128, 128],
}
CORE_SHAPES = {
    "xT": [D, SEQ], "ctxT": [D, NCTX], "cs_hc": [128, 2, HALO + NCTX], "cvec": [128, 16], "cos_t": [128, SEQ], "sin_t": [128, SEQ], "masks": [128, 512],
}


def finish(C):
    nc = C.nc
    nc.all_engine_barrier()
    nc.clear_and_free_semaphores(C.all_sems)
    nc.all_engine_barrier()


def build(stop_after=None, debug=False):
    nc = bass.Bass("TRN2", target_bir_lowering=False)
    C = Ctx()
    C.nc = nc
    I = {}
    for k, shp in list(SHARED_SHAPES.items()) + list(CORE_SHAPES.items()):
        I[k] = nc.dram_tensor(k, shp, F32, kind="ExternalInput").ap()
    yT = nc.dram_tensor("yT", [D, OWN], F32, kind="ExternalOutput").ap()
    dk = "ExternalOutput" if debug else "Internal"
    Wb = {}
    for k in ("wi_h", "wo_h", "wqkA_h", "wvA_h", "woA_h", "wqkB_h", "wvB_h", "woB_h"):
        Wb[k] = nc.dram_tensor(k + "_b", SHARED_SHAPES[k], BF16).ap()
    kT_scr = nc.dram_tensor("kT_scr", [8, 128, NKA], BF16, kind=dk).ap()
    v_scr = nc.dram_tensor("v_scr", [NKA, D], BF16, kind=dk).ap()
    qT_scr = nc.dram_tensor("qT_scr", [8, 128, XR], BF16, kind=dk).ap()
    kB_scr = nc.dram_tensor("kB_scr", [4, 128, XR], BF16, kind=dk).ap()
    vB_scr = nc.dram_tensor("vB_scr", [XR, 256], BF16, kind=dk).ap()
    qB_scr = nc.dram_tensor("qB_scr", [8, 128, OWN], BF16, kind=dk).ap()
    if debug:
        xdbg = nc.dram_tensor("xdbg", [128, 8, XR], F32, kind="ExternalOutput").ap()
        moddbg = nc.dram_tensor("moddbg", [128, 288], F32, kind="ExternalOutput").ap()
        aodbg = nc.dram_tensor("aodbg", [128, 8, XR], BF16, kind="ExternalOutput").ap()

    with contextlib.ExitStack() as gs:
        sb = lambda name, shape, dt: gs.enter_context(nc.sbuf_tensor(name, shape, dt))
        x_sb = sb("x_sb", [128, 8, XR], F32)
        mod = sb("mod", [128, 2, 72, 2], F32)
        adab = sb("adab", [128, 2, 72], F32)
        smallp = sb("smallp_s", [128, 32], F32)
        csil = sb("csil", [128, 8, 2], F32)
        ones_bf = sb("ones_bf", [128, 128], BF16)
        bd_bf = sb("bd_bf", [128, 128], BF16)
        rm_bf = sb("rm_bf", [128, 128], BF16)
        ones_f = sb("ones_f", [128, 128], F32)
        neglam = sb("neglam", [128, 2], F32)
        gain2 = sb("gain2", [128, 1], F32)
        esink = sb("esink", [128, 16], F32)
        masks_bf = sb("masks_bf", [128, 4, 128], BF16)
        RA_BYTES = 65 * 1024
        RC_BYTES = 46 * 1024
        regA = sb("regA", [128, RA_BYTES // 4], F32)
        regC = sb("regC", [128, RC_BYTES // 4], F32)
        xtmp = sb("xtmp", [128, 8, 512], F32)

        def carve(reg, off, shape, dt):
            n = int(np.prod(shape))
            bpe = 4 if dt == F32 else 2
            assert off % 4 == 0 and (n * bpe) % 4 == 0
            v = reg[:, off // 4:(off + n * bpe) // 4]
            if dt != F32:
                v = v.bitcast(dt)
            if len(shape) == 2:
                v = v.rearrange("p (a b) -> p a b", a=shape[0])
            elif len(shape) == 3:
                v = v.rearrange("p (a b c) -> p a b c", a=shape[0], b=shape[1])
            return v, off + n * bpe

        C.all_sems = []

        def new_sem(name):
            h = nc.alloc_semaphore(name=name)
            C.all_sems.append(h)
            return h
        NDMA_SEMS = 24
        C.semsets = [dict(eng={e: new_sem("s%d_%s" % (i, e)) for e in ENGS}, dma=[new_sem("d%d_%d" % (i, j)) for j in range(NDMA_SEMS)])
                     for i in range(2)]
        C.bar = new_sem("bar")
        C.phase_idx = 0
        castB = {}

        def cast_buf(name):
            b = Buf(name, persist=True, group=True)
            b.sem = new_sem("c_" + name)
            castB[name] = b
            return b

        C.P = Prog(nc, "p0")
        with contextlib.ExitStack() as ps:
            psb = lambda name, shape, dt: ps.enter_context(nc.sbuf_tensor(name, shape, dt))
            banks = [ps.enter_context(nc.psum_tensor("b0_%d" % i, [128, 512], F32)) for i in range(8)]
            bankB = [Buf("bank%d" % i) for i in range(8)]
            C.cast_queue = None

            def cast(key, idx_slices, tag):
                for i, sl in enumerate(idx_slices):
                    b = cast_buf("%s_%d" % (tag, i))
                    src = I[key]
                    dst = Wb[key]
                    for s in sl[:-1]:
                        src = src[s]
                        dst = dst[s]
                    r0, r1 = sl[-1]
                    ncol = src.shape[-1]
                    cols = [(0, ncol)] if ncol <= 2048 else [(0, 1408), (1408, 2816)]
                    rstep = 704 if ncol <= 2048 else 512
                    for (c0, c1) in cols:
                        for ra in range(r0, r1, rstep):
                            rb = min(r1, ra + rstep)

                            def issue(after=(), src=src, dst=dst, ra=ra, rb=rb, c0=c0, c1=c1, b=b):
                                dma(C, "pool", dst[ra:rb, c0:c1], src[ra:rb, c0:c1], [], [b], b, after=after)
                            if C.cast_queue is None:
                                issue()
                            else:
                                C.cast_queue.append(issue)

            C.wi_bounds = {0: [0, 2, 11, NFC], 1: [0, 11, NFC], 2: [0, 11, NFC], 3: [0, 11, NFC]}
            def ffn_cast(f):
                bnd = C.wi_bounds[f]
                cast("wi_h", [(f, (bnd[i] * 128, bnd[i + 1] * 128)) for i in range(len(bnd) - 1)], "wi%d" % f)
                cast("wo_h", [(f, (0, 1024))], "wo%d" % f)
            ffn_cast(0)

            def early_casts():
                cast("wqkA_h", [((0, 1024),)], "wqkA")
                cast("wvA_h", [((0, 512),)], "wvA")

            def deferred_casts():
                cast("woA_h", [((0, 1024),)], "woA")
                ffn_cast(1)
                ffn_cast(2)
                cast("wqkB_h", [((0, 768),)], "wqkB")
                cast("wvB_h", [((0, 128),)], "wvB")
                cast("woB_h", [((0, 1024),)], "woB")
                ffn_cast(3)

            def wiB2(f, j):
                bnd = C.wi_bounds[f]
                for i in range(len(bnd) - 1):
                    if bnd[i] <= j < bnd[i + 1]:
                        return [castB["wi%d_%d" % (f, i)]]

            def woB_(f, dc):
                return [castB["wo%d_0" % f]]
            C.wiB2 = wiB2
            C.woB_ = woB_

            t_cv = xtmp[:, 0, 0:16]
            t_lam = xtmp[0:1, 1, 0:256]
            t_lam2 = xtmp[0:1, 2, 0:8]
            t_rm = xtmp[:, 3, 0:128]
            t_mk = xtmp[:, 4, 0:512]
            Bs = {n: Buf(n) for n in ["cv", "lam", "lam2", "rm", "mk", "smallp", "adab", "csil", "consts", "neglam", "gain2", "esink", "masksbf", "mod"]}
            dma(C, "sp", t_cv, I["cvec"], [], [Bs["cv"]], Bs["cv"])
            dma(C, "sp", smallp[:], I["smallp"], [], [Bs["smallp"]], Bs["smallp"])
            dma(C, "sp", adab[:].rearrange("p l j -> p (l j)"), I["ada_bT"], [], [Bs["adab"]], Bs["adab"])
            dma(C, "sp", t_lam, I["lamv"], [], [Bs["lam"]], Bs["lam"])
            dma(C, "sp", t_rm, I["rmat"], [], [Bs["rm"]], Bs["rm"])
            dma(C, "sp", t_mk, I["masks"], [], [Bs["mk"]], Bs["mk"])
            C.P.add("dve", lambda e: e.memset(ones_bf[:], 1.0), [], [Bs["consts"]])
            C.P.add("dve", lambda e: e.memset(ones_f[:], 1.0), [], [Bs["consts"]])
            C.P.add("dve", lambda e: e.memset(bd_bf[:], 0.0), [], [Bs["consts"]])
            C.P.add("dve", lambda e: e.memset(bd_bf[0:64, 0:64], 1.0), [], [Bs["consts"]])
            C.P.add("dve", lambda e: e.memset(bd_bf[64:128, 64:128], 1.0), [], [Bs["consts"]])
            cp(C, rm_bf[:], t_rm, [Bs["rm"]], [Bs["consts"]])
            cp(C, masks_bf[:].rearrange("p a b -> p (a b)"), t_mk, [Bs["mk"]], [Bs["masksbf"]])
            act(C, csil[:].rearrange("p a b -> p (a b)"), t_cv, AF.Silu, [Bs["cv"]], [Bs["csil"]])
            tt(C, t_lam[:, 0:128], t_lam[:, 0:128], t_lam[:, 128:256], ALU.mult, [Bs["lam"]], [Bs["lam"]])
            C.P.add("dve", lambda e: e.tensor_reduce(out=t_lam2[:, 0:2], in_=t_lam[:, 0:128].rearrange("p (a b) -> p a b", a=2), axis=AX.X, op=ALU.add),
                    [Bs["lam"]], [Bs["lam2"]])
            act(C, t_lam2[:, 2:4], t_lam2[:, 0:2], AF.Exp, [Bs["lam2"]], [Bs["lam2"]])
            tt(C, t_lam2[:, 4:5], t_lam2[:, 2:3], t_lam2[:, 3:4], ALU.subtract, [Bs["lam2"]], [Bs["lam2"]])
            ts(C, t_lam2[:, 6:7], t_lam2[:, 4:5], -1.0, -LAM_INIT0, ALU.mult, ALU.add, [Bs["lam2"]], [Bs["lam2"]])
            cp(C, t_lam2[:, 7:8], t_lam2[:, 6:7], [Bs["lam2"]], [Bs["lam2"]])
            mm(C, banks[7][:, 0:2], ones_f[0:1, :], t_lam2[0:1, 6:8], True, True, [Bs["consts"], Bs["lam2"]], [bankB[7]])
            cp(C, neglam[:], banks[7][:, 0:2], [bankB[7]], [Bs["neglam"]])
            ts(C, gain2[:], smallp[:, 4:5], 1.0 - LAM_INIT0, None, ALU.mult, None, [Bs["smallp"]], [Bs["gain2"]])
            act(C, esink[:], smallp[:, 5:21], AF.Exp, [Bs["smallp"]], [Bs["esink"]])
            C.P.emit(C)
        if debug:
            pass

        def alloc_regA(C):
            off = 0
            C.h, off = carve(regA, off, [8, 512], BF16)
            C.G, off = carve(regA, off, [NFC, 512], BF16)
            C.sg = []
            for i in range(2):
                t, off = carve(regA, off, [512], F32)
                C.sg.append(t)
            C.slots = []
            for i in range(4):
                t, off = carve(regA, off, [2048], BF16)
                C.slots.append(t)
            C.woslots = []
            for i in range(2):
                t, off = carve(regA, off, [NFC, 128], BF16)
                C.woslots.append(t)
            C.sq = []
            for i in range(4):
                t, off = carve(regA, off, [512], BF16)
                C.sq.append(t)
            assert off <= RA_BYTES, off
            C.hB = [Buf("h%d" % i) for i in range(8)]
            C.GB = Buf("G")
            C.sgB = [Buf("sg0"), Buf("sg1")]
            C.sqB = [Buf("sq%d" % i) for i in range(4)]
            C.stream = Stream(C, C.slots, "slot")
            C.wostream = Stream(C, C.woslots, "woslot")

        def alloc_regC_proj(C):
            off = 0
            C.lnb, C.rsb, C.tmpf, C.qn, C.t1, C.t2 = [], [], [], [], [], []
            for i in range(2):
                t, off = carve(regC, off, [512], F32); C.lnb.append(t); C.rsb.append(t)
                t, off = carve(regC, off, [512], F32); C.tmpf.append(t); C.t1.append(t)
                t, off = carve(regC, off, [512], BF16); C.qn.append(t)
                t, off = carve(regC, off, [512], F32); C.t2.append(t)
            C.qout, off = carve(regC, off, [8, 512], BF16)
            C.kout, off = carve(regC, off, [8, 512], BF16)
            C.vout, off = carve(regC, off, [4, 1024], BF16)
            C.cs, C.sn = [], []
            for i in range(2):
                t, off = carve(regC, off, [512], F32); C.cs.append(t)
                t, off = carve(regC, off, [512], F32); C.sn.append(t)
            assert off <= RC_BYTES, off
            mk = lambda n: [Buf(n + "0"), Buf(n + "1")]
            C.lnB = mk("ln"); C.rsB = C.lnB
            C.tmpfB = mk("tmpf"); C.t1B = C.tmpfB
            C.qnB, C.t2B = mk("qn"), mk("t2")
            C.qoutB, C.koutB, C.voutB = Buf("qout"), Buf("kout"), Buf("vout")
            C.csB = mk("cs")
            C.cs_stream_i = 0

        def two_h(C):
            h2 = xtmp[:, 4:8, :].rearrange("p a b -> p (a b)").bitcast(BF16).rearrange("p (c t) -> p c t", c=8)
            return [(C.h, C.hB), (h2, [Buf("hx%d" % i) for i in range(8)])]

        def new_banks(C, ps, tag):
            C.banks = [ps.enter_context(nc.psum_tensor("b%s_%d" % (tag, i), [128, 512], F32)) for i in range(8)]
            C.bankB = [Buf("bank%d" % i) for i in range(8)]

        def segs_of(w, ntok):
            return w if isinstance(w, list) else [(0, ntok, w)]

        def modap(l, v, ch, w):
            return mod[:, l, v * 8 + ch, w:w + 1]

        C.modB = Buf("mod")
        C.constB = Buf("const")

        def mod_vector(C, l, v, defer=False, cb=0):
            pb = C.banks[7]
            pB = C.bankB[7]
            for ch in range(8):
                j = v * 8 + ch
                slot, sB = C.stream.load(I["ada_h"][l * 72 + j], [], view=lambda t: t.bitcast(F32))
                w32 = slot.bitcast(F32).rearrange("p (c f) -> p c f", c=8)
                for dc in range(8):
                    mm(C, pb[:, cb + 2 * ch:cb + 2 * ch + 2], w32[:, dc, :], csil[:, dc, :], dc == 0, dc == 7, [sB], [pB])

            def evac():
                dst = mod[:, l, v * 8:(v + 1) * 8, :]
                tt(C, dst, pb[:, cb:cb + 16].rearrange("p (a b) -> p a b", a=8),
                   adab[:, l, v * 8:(v + 1) * 8].unsqueeze(2).broadcast_to([128, 8, 2]), ALU.add, [pB], [C.modB])
                if v in (1, 4, 7):
                    ts(C, dst, dst, 1.0, None, ALU.add, None, [C.modB], [C.modB])
                if v in (2, 8):
                    ts(C, dst, dst, 0.5, None, ALU.mult, None, [C.modB], [C.modB])
            if defer:
                return evac
            evac()
            return None

        def norm_mod(C, xap, xB, ntok, l, vsh, vsc, w, filler=None):
            ssb, ssB = C.banks[6], C.bankB[6]
            for ch in range(8):
                s = ch % 4
                if ch in (1, 4, 6):
                    tt(C, C.sq[s][:, :ntok], xap[:, ch, :], xap[:, ch, :], ALU.mult, [xB], [C.sqB[s]])
                else:
                    act(C, C.sq[s][:, :ntok], xap[:, ch, :], AF.Square, [xB], [C.sqB[s]])
                mm(C, ssb[:, :ntok], ones_bf[:], C.sq[s][:, :ntok], ch == 0, ch == 7, [C.sqB[s]], [ssB])
            after = filler() if filler is not None else None
            act(C, C.lnb[0][:, :ntok], ssb[:, :ntok], AF.Ln, [ssB], [C.lnB[0]], bias=EPS, scale=1.0 / D)
            act(C, C.rsb[0][:, :ntok], C.lnb[0][:, :ntok], AF.Exp, [C.lnB[0]], [C.rsB[0]], scale=-0.5)
            for ch in range(8):
                s = ch % 2
                for (lo, hi, ww) in segs_of(w, ntok):
                    stt(C, C.tmpf[s][:, lo:hi], xap[:, ch, lo:hi], modap(l, vsc, ch, ww), C.rsb[0][:, lo:hi], ALU.mult, ALU.mult,
                        [xB, C.rsB[0], C.modB], [C.tmpfB[s]])
                    act(C, C.h[:, ch, lo:hi], C.tmpf[s][:, lo:hi], AF.Identity, [C.tmpfB[s], C.modB], [C.hB[ch]], bias=modap(l, vsh, ch, ww))
            if after is not None:
                after()

        def ffn(C, xap, xB, ntok, f, l, vg, w):
            ffn_a(C, ntok, f)
            ffn_b(C, xap, xB, ntok, f, l, vg, w)

        def ffn_a(C, ntok, f):
            for j in range(NFC):
                slot, sB = C.stream.load(Wb["wi_h"][f, j * 128:(j + 1) * 128, :], C.wiB2(f, j))
                wv = slot.rearrange("p (c f) -> p c f", c=8)
                bg, bgB = C.banks[2 * (j % 2)], C.bankB[2 * (j % 2)]
                bu, buB = C.banks[2 * (j % 2) + 1], C.bankB[2 * (j % 2) + 1]
                for dc in range(8):
                    mm(C, bg[:, :ntok], wv[:, dc, 0:128], C.h[:, dc, :ntok], dc == 0, dc == 7, [sB, C.hB[dc]], [bgB])
                for dc in range(8):
                    mm(C, bu[:, :ntok], wv[:, dc, 128:256], C.h[:, dc, :ntok], dc == 0, dc == 7, [sB, C.hB[dc]], [buB])
                s = j % 2
                act(C, C.sg[s][:, :ntok], bg[:, :ntok], AF.Silu, [bgB], [C.sgB[s]])
                tt(C, C.G[:, j, :ntok], bu[:, :ntok], C.sg[s][:, :ntok], ALU.mult, [buB, C.sgB[s]], [C.GB])

        def ffn_b(C, xap, xB, ntok, f, l, vg, w):
            for dc in range(8):
                slot, sB = C.wostream.load(Wb["wo_h"][f, dc * 128:(dc + 1) * 128, :], C.woB_(f, dc),
                                           view=lambda t: t.rearrange("p a b -> p (a b)"))
                by, byB = C.banks[4 + dc % 2], C.bankB[4 + dc % 2]
                for j in range(NFC):
                    mm(C, by[:, :ntok], slot[:, j, :], C.G[:, j, :ntok], j == 0, j == NFC - 1, [sB, C.GB], [byB])
                for (lo, hi, ww) in segs_of(w, ntok):
                    stt(C, xap[:, dc, lo:hi], by[:, lo:hi], modap(l, vg, dc, ww), xap[:, dc, lo:hi], ALU.mult, ALU.add,
                        [byB, xB, C.modB], [xB])

        def qk_chunks(C, ntok, chunk_specs, wsrc, wsrcB, rope_off, gcol_fn):
            n = len(chunk_specs)
            state = {}
            if rope_off is not None:
                k = C.cs_stream_i % 2
                C.cs_stream_i += 1
                dma(C, "sp", C.cs[k][:, :ntok], rope_off[0], [], [C.csB[k]], C.csB[k])
                dma(C, "sp", C.sn[k][:, :ntok], rope_off[1], [], [C.csB[k]], C.csB[k])
                cs, sn, csB = C.cs[k], C.sn[k], C.csB[k]

            def proj(c):
                if c % 2 == 0:
                    slot, sB = C.stream.load(wsrc[(c // 2) * 128:(c // 2 + 1) * 128, :], wsrcB)
                    state["slot"] = (slot, sB)
                slot, sB = state["slot"]
                wv = slot.rearrange("p (i c f) -> p i c f", i=2, c=8)
                pb, pB = C.banks[c % 4], C.bankB[c % 4]
                for dc in range(8):
                    mm(C, pb[:, :ntok], wv[:, c % 2, dc, :], C.h[:, dc, :ntok], dc == 0, dc == 7, [sB, C.hB[dc]], [pB])

            def sqr(c):
                s = c % 2
                pb, pB = C.banks[c % 4], C.bankB[c % 4]
                act(C, C.sq[s][:, :ntok], pb[:, :ntok], AF.Square, [pB], [C.sqB[s]])

            def ssmm(c):
                s = c % 2
                mm(C, C.banks[6][:, :ntok], bd_bf[:], C.sq[s][:, :ntok], True, True, [C.sqB[s]], [C.bankB[6]])

            def lnq(c):
                s = c % 2
                pb, pB = C.banks[c % 4], C.bankB[c % 4]
                act(C, C.lnb[s][:, :ntok], C.banks[6][:, :ntok], AF.Ln, [C.bankB[6]], [C.lnB[s]], bias=EPS, scale=1.0 / 64)
                act(C, C.rsb[s][:, :ntok], C.lnb[s][:, :ntok], AF.Exp, [C.lnB[s]], [C.rsB[s]], scale=-0.5)
                dst, dstB = chunk_specs[c]
                gc = gcol_fn(c)
                if rope_off is None:
                    stt(C, dst, pb[:, :ntok], smallp[:, gc:gc + 1], C.rsb[s][:, :ntok], ALU.mult, ALU.mult, [pB, C.rsB[s]], [dstB])
                else:
                    stt(C, C.qn[s][:, :ntok], pb[:, :ntok], smallp[:, gc:gc + 1], C.rsb[s][:, :ntok], ALU.mult, ALU.mult,
                        [pB, C.rsB[s]], [C.qnB[s]])

            def rotmm(c):
                if rope_off is None:
                    return
                s = c % 2
                mm(C, C.banks[7][:, :ntok], rm_bf[:], C.qn[s][:, :ntok], True, True, [C.qnB[s]], [C.bankB[7]])

            def fin(c):
                if rope_off is None:
                    return
                s = c % 2
                dst, dstB = chunk_specs[c]
                tt(C, C.t2[s][:, :ntok], C.banks[7][:, :ntok], sn[:, :ntok], ALU.mult, [C.bankB[7], csB], [C.t2B[s]])
                tt(C, C.t1[s][:, :ntok], C.qn[s][:, :ntok], cs[:, :ntok], ALU.mult, [C.qnB[s], csB], [C.t1B[s]])
                tt(C, dst, C.t1[s][:, :ntok], C.t2[s][:, :ntok], ALU.add, [C.t1B[s], C.t2B[s]], [dstB])

            pipeline([proj, sqr, ssmm, lnq, rotmm, fin], n)

        def v_proj(C, ntok, npieces, wsrc, wsrcB, vdst_fn, vB):
            nsub = ntok // 128
            cnt = 0
            for pc in range(npieces):
                slot, sB = C.stream.load(wsrc[pc * 128:(pc + 1) * 128, :], wsrcB)
                wv = slot.rearrange("p (c f) -> p c f", c=8)
                for sub in range(nsub):
                    pb, pB = C.banks[4 + cnt % 2], C.bankB[4 + cnt % 2]
                    cnt += 1
                    for dc in range(8):
                        mm(C, pb[:, 0:256], C.h[:, dc, sub * 128:(sub + 1) * 128], wv[:, dc, :], dc == 0, dc == 7, [sB, C.hB[dc]], [pB])
                    act(C, vdst_fn(sub, pc), pb[:, 0:256], AF.Copy, [pB], [vB])

        tilesA = []
        for i in range(4):
            tilesA.append(dict(kind="own", xcol=512 * i, ntok=512, src=("x", 512 * i), key=512 * i, rope=512 * i, q=True, w=0))
        HC = HALO + NCTX
        HCSEG = [(0, HALO, 0), (HALO, HC, 1)]
        tilesA.append(dict(kind="hc", xcol=OWN, ntok=HC, src=("hc", OWN), key=None, rope="hc", q=True, w=HCSEG))
        o = OWN + HALO
        while o < SEQ:
            n = min(512, SEQ - o)
            tilesA.append(dict(kind="rest", xcol=None, ntok=n, src=("x", o), key=o, rope=o, q=False, w=0))
            o += n

        pending_mods = [(0, 3), (0, 4), (0, 5), (0, 6), (0, 7), (0, 8)] + [(1, v) for v in range(9)]

        C.P = Prog(nc, "p1")
        with contextlib.ExitStack() as ps:
            new_banks(C, ps, "1")
            alloc_regA(C)
            alloc_regC_proj(C)
            kscrB = Buf("kscr", group=True)
            vscrB = Buf("vscr", group=True)
            qscrB = Buf("qscr", group=True)
            xtmpB = Buf("xtmp")
            allxB = []
            for v in range(3):
                mod_vector(C, 0, v)
            early_casts()
            for ti, T in enumerate(tilesA):
                ntok = T["ntok"]
                if T["xcol"] is not None:
                    xap = x_sb[:, :, T["xcol"]:T["xcol"] + ntok]
                else:
                    xap = xtmp[:, :, :ntok]
                xB = Buf("x_%d" % ti) if T["xcol"] is not None else xtmpB
                allxB.append(xB)
                if T["src"][0] == "x":
                    src = I["xT"][:, T["src"][1]:T["src"][1] + ntok]
                    dma(C, "sp", xap, src.rearrange("(c p) t -> p c t", p=128), [], [xB], xB)
                else:
                    dma(C, "sp", xap[:, :, 0:HALO], I["xT"][:, OWN:OWN + HALO].rearrange("(c p) t -> p c t", p=128), [], [xB], xB)
                    dma(C, "sp", xap[:, :, HALO:HC], I["ctxT"].rearrange("(c p) t -> p c t", p=128), [], [xB], xB)
                w = T["w"]
                def mod_filler(cb=0):
                    if pending_mods:
                        l_, v_ = pending_mods.pop(0)
                        return mod_vector(C, l_, v_, defer=True, cb=cb)
                    return None
                def mod_filler2():
                    e1 = mod_filler(0)
                    e2 = mod_filler(16)
                    return lambda: (e1(), e2())
                norm_mod(C, xap, xB, ntok, 0, 0, 1, w, filler=(mod_filler if ti > 0 else mod_filler2))
                ffn(C, xap, xB, ntok, 0, 0, 2, w)
                norm_mod(C, xap, xB, ntok, 0, 3, 4, w, filler=mod_filler)
                specs = [(C.kout[:, c, :ntok], C.koutB) for c in range(8)]
                if T["q"]:
                    specs += [(C.qout[:, c, :ntok], C.qoutB) for c in range(8)]
                if T["rope"] == "hc":
                    rsrc = (I["cs_hc"][:, 0, :], I["cs_hc"][:, 1, :])
                else:
                    rsrc = (I["cos_t"][:, T["rope"]:T["rope"] + ntok], I["sin_t"][:, T["rope"]:T["rope"] + ntok])
                qk_chunks(C, ntok, specs, Wb["wqkA_h"], [castB["wqkA_0"]], rsrc,
                          lambda c: 1 if c < 8 else 0)
                v_proj(C, ntok, 4, Wb["wvA_h"], [castB["wvA_0"]],
                       lambda sub, pc: C.vout[:, sub, pc * 256:(pc + 1) * 256], C.voutB)
                if T["kind"] == "hc":
                    kparts = [(OWN, 0, HALO), (SEQ, HALO, HC)]
                else:
                    kparts = [(T["key"], 0, ntok)]
                for (k0, lo, hi) in kparts:
                    dma(C, "pool", kT_scr[:, :, k0:k0 + hi - lo].rearrange("h p t -> p h t"), C.kout[:, :, lo:hi], [C.koutB], [kscrB], kscrB)
                    dma(C, "pool", v_scr[k0:k0 + hi - lo, :].rearrange("(s p) f -> p s f", p=128), C.vout[:, lo // 128:hi // 128, :],
                        [C.voutB], [vscrB], vscrB)
                if T["q"]:
                    q0 = T["xcol"]
                    dma(C, "pool", qT_scr[:, :, q0:q0 + ntok].rearrange("h p t -> p h t"), C.qout[:, :, :ntok], [C.qoutB], [qscrB], qscrB)
            while pending_mods:
                mod_vector(C, *pending_mods.pop(0))
            if debug:
                dB = Buf("dbg")
                allx = [Buf("xall")]
                dma(C, "sp", xdbg, x_sb[:], allxB, [dB], dB)
                dma(C, "sp", moddbg, mod[:].rearrange("p l j w -> p (l j w)"), [C.modB], [dB], dB)
            C.P.emit(C)
        if stop_after == 1:
            finish(C)
            return nc

        C.P = Prog(nc, "p2a")
        with contextlib.ExitStack() as ps:
            new_banks(C, ps, "2a")
            off = 0
            Kh, Vh, Qh = [], [], []
            for i in range(2):
                t, off = carve(regA, off, [NKA], BF16); Kh.append(t)
                t, off = carve(regA, off, [NKA // 128, 128], BF16); Vh.append(t)
                t, off = carve(regA, off, [XR], BF16); Qh.append(t)
            pbuf = [[None, None], [None, None]]
            for c in range(2):
                for s in range(2):
                    pbuf[c][s], off = carve(regA, off, [512], BF16)
            ep = []
            for i in range(4):
                t, off = carve(regA, off, [512], F32); ep.append(t)
            eo = [xtmp[:, 0, :], xtmp[:, 1, :]]
            ers = [xtmp[:, 2, :], xtmp[:, 3, :]]
            sqs = [xtmp[:, 4, :].bitcast(BF16)[:, 0:512], xtmp[:, 5, :].bitcast(BF16)[:, 0:512]]
            assert off <= RA_BYTES, off
            attn_out, o2 = carve(regC, 0, [8, XR], BF16)
            assert o2 <= RC_BYTES
            KhB = [Buf("Kh0"), Buf("Kh1")]
            VhB = [Buf("Vh0"), Buf("Vh1")]
            QhB = [Buf("Qh0"), Buf("Qh1")]
            pB = [[Buf("p00"), Buf("p01")], [Buf("p10"), Buf("p11")]]
            epB = [Buf("ep%d" % i) for i in range(4)]
            eoB, ersB, sqsB = [Buf("eo0"), Buf("eo1")], [Buf("ers0"), Buf("ers1")], [Buf("sqs0"), Buf("sqs1")]
            aoB = Buf("attn_out")
            sbank = [[C.banks[0], C.banks[2]], [C.banks[1], C.banks[3]]]
            sbankB = [[C.bankB[0], C.bankB[2]], [C.bankB[1], C.bankB[3]]]
            Ob, ObB = [C.banks[4], C.banks[5]], [C.bankB[4], C.bankB[5]]
            Lb, LbB = [C.banks[6], C.banks[7]], [C.bankB[6], C.bankB[7]]

            allk = list(range(NKA // 128))
            units = []
            for hd in range(8):
                for i in range(4):
                    units.append((hd, 512 * i, 512, allk))
                units.append((hd, OWN, HALO, allk))
                units.append((hd, OWN + HALO, NCTX, [32, 33]))
            items = [(u, i) for u in range(len(units)) for i in range(len(units[u][3]))]
            T = len(items)
            loaded = set()

            def load_head(hd):
                hs = hd % 2
                dma(C, "sp", Kh[hs], kT_scr[hd], [], [KhB[hs]], KhB[hs])
                dma(C, "sp", Vh[hs], v_scr[:, hd * 128:(hd + 1) * 128].rearrange("(c p) e -> p c e", p=128), [], [VhB[hs]], VhB[hs])
                dma(C, "sp", Qh[hs], qT_scr[hd], [], [QhB[hs]], QhB[hs])
                if hd == 1:
                    C.cast_queue = []
                    deferred_casts()

            def qk(t):
                u, i = items[t]
                hd, qcol, nq, kch = units[u]
                hs = hd % 2
                if hd not in loaded:
                    loaded.add(hd)
                    load_head(hd)
                kc = kch[i]
                st_ = t % 2
                for c in range(2):
                    mm(C, sbank[c][st_][:, :nq], Kh[hs][c * 64:(c + 1) * 64, kc * 128:(kc + 1) * 128],
                       Qh[hs][c * 64:(c + 1) * 64, qcol:qcol + nq], True, True, [KhB[hs], QhB[hs]], [sbankB[c][st_]])

            def ex(t):
                u, i = items[t]
                hd, qcol, nq, kch = units[u]
                st_ = t % 2
                for c in range(2):
                    act(C, pbuf[c][st_][:, :nq], sbank[c][st_][:, :nq], AF.Exp, [sbankB[c][st_]], [pB[c][st_]], scale=0.125)

            def pv(t):
                u, i = items[t]
                hd, qcol, nq, kch = units[u]
                hs = hd % 2
                n = len(kch)
                kc = kch[i]
                st_ = t % 2
                for c in range(2):
                    mm(C, Ob[c][:, :nq], Vh[hs][:, kc, :], pbuf[c][st_][:, :nq], i == 0, i == n - 1, [VhB[hs], pB[c][st_]], [ObB[c]])
                    last_pe[0] = mm(C, Lb[c][:, :nq], ones_bf[:], pbuf[c][st_][:, :nq], i == 0, i == n - 1, [pB[c][st_]], [LbB[c]])
                return i == n - 1

            def epi_a(u):
                hd, qcol, nq, kch = units[u]
                k = u % 2
                cp(C, ep[0][:, :nq], Lb[0][:, :nq], [LbB[0]], [epB[0]])
                cp(C, ep[1][:, :nq], Lb[1][:, :nq], [LbB[1]], [epB[1]])
                cp(C, ep[2][:, :nq], Ob[0][:, :nq], [ObB[0]], [epB[2]])
                cp(C, ep[3][:, :nq], Ob[1][:, :nq], [ObB[1]], [epB[3]])
                C.P.add("dve", lambda e: e.reciprocal(out=ep[0][:, :nq], in_=ep[0][:, :nq]), [epB[0]], [epB[0]])
                C.P.add("dve", lambda e: e.reciprocal(out=ep[1][:, :nq], in_=ep[1][:, :nq]), [epB[1]], [epB[1]])
                tt(C, ep[2][:, :nq], ep[2][:, :nq], ep[0][:, :nq], ALU.mult, [epB[2], epB[0]], [epB[2]])
                tt(C, ep[3][:, :nq], ep[3][:, :nq], ep[1][:, :nq], ALU.mult, [epB[3], epB[1]], [epB[3]])
                stt(C, eo[k][:, :nq], ep[3][:, :nq], neglam[:, 0:1], ep[2][:, :nq], ALU.mult, ALU.add, [epB[3], epB[2]], [eoB[k]])
                tt(C, sqs[k][:, :nq], eo[k][:, :nq], eo[k][:, :nq], ALU.mult, [eoB[k]], [sqsB[k]])

            def epi_b(u, bank, bankB_):
                hd, qcol, nq, kch = units[u]
                k = u % 2
                mm(C, bank[:, :nq], ones_bf[:], sqs[k][:, :nq], True, True, [sqsB[k]], [bankB_])
                act(C, ers[k][:, :nq], bank[:, :nq], AF.Ln, [bankB_], [ersB[k]], bias=EPS, scale=1.0 / 128)
                act(C, ers[k][:, :nq], ers[k][:, :nq], AF.Exp, [ersB[k]], [ersB[k]], scale=-0.5)
                stt(C, attn_out[:, hd, qcol:qcol + nq], eo[k][:, :nq], gain2[:, 0:1], ers[k][:, :nq], ALU.mult, ALU.mult,
                    [eoB[k], ersB[k]], [aoB])

            DELAY = 12
            pending = []
            last_pe = [None]
            for s_ in range(T + 2):
                if s_ - 2 >= 0:
                    if pv(s_ - 2):
                        u = items[s_ - 2][0]
                        epi_a(u)
                        pending.append((s_ + DELAY, u))
                        if C.cast_queue:
                            C.cast_queue.pop(0)(after=[last_pe[0]])
                if 0 <= s_ - 1 < T:
                    ex(s_ - 1)
                if s_ < T:
                    qk(s_)
                while pending and pending[0][0] <= s_:
                    _, u = pending.pop(0)
                    st_ = (s_ + 1) % 2
                    epi_b(u, sbank[0][st_], sbankB[0][st_])
            while pending:
                _, u = pending.pop(0)
                epi_b(u, sbank[0][0], sbankB[0][0])
            while C.cast_queue:
                C.cast_queue.pop(0)()
            C.cast_queue = None
            if debug:
                dB = Buf("dbg")
                dma(C, "sp", aodbg, attn_out, [aoB], [dB], dB)
            C.P.emit(C)
        if stop_after == 2:
            finish(C)
            return nc

        def out_proj(C, xap, xB, ntok, col, ao, aoB, wkey, wB, l, w):
            for dc in range(8):
                slot, sB = C.stream.load(Wb[wkey][dc * 128:(dc + 1) * 128, :], wB, view=lambda t: t[:, 0:1024])
                wv = slot[:, 0:1024].rearrange("p (h f) -> p h f", h=8)
                by, byB = C.banks[4 + dc % 2], C.bankB[4 + dc % 2]
                for hd in range(8):
                    mm(C, by[:, :ntok], wv[:, hd, :], ao[:, hd, col:col + ntok], hd == 0, hd == 7, [sB, aoB], [byB])
                for (lo, hi, ww) in segs_of(w, ntok):
                    stt(C, xap[:, dc, lo:hi], by[:, lo:hi], modap(l, 5, dc, ww), xap[:, dc, lo:hi], ALU.mult, ALU.add, [byB, xB, C.modB], [xB])

        tilesB = [dict(xcol=512 * i, ntok=512, w=0) for i in range(4)] + [dict(xcol=OWN, ntok=HC, w=HCSEG)]

        C.P = Prog(nc, "p2b")
        with contextlib.ExitStack() as ps:
            new_banks(C, ps, "2b")
            alloc_regA(C)
            C.lnb = [xtmp[:, 0, :]]
            C.rsb = C.lnb
            C.tmpf = [xtmp[:, 2, :], xtmp[:, 3, :]]
            C.lnB = [Buf("ln")]; C.rsB = C.lnB
            C.tmpfB = [Buf("tmpf0"), Buf("tmpf1")]
            attn_out, _ = carve(regC, 0, [8, XR], BF16)
            aoB = Buf("attn_out")
            allxB = []
            hb = two_h(C)
            tl = []
            for ti, T in enumerate(tilesB):
                ntok, col, w = T["ntok"], T["xcol"], T["w"]
                xB = Buf("x_%d" % ti)
                allxB.append(xB)
                tl.append((x_sb[:, :, col:col + ntok], xB, ntok, col, w))

            def pre2b(ti):
                xap, xB, ntok, col, w = tl[ti]
                C.h, C.hB = hb[ti % 2]
                out_proj(C, xap, xB, ntok, col, attn_out, aoB, "woA_h", [castB["woA_0"]], 0, w)
                norm_mod(C, xap, xB, ntok, 0, 6, 7, w)
            pre2b(0)
            for ti in range(len(tl)):
                xap, xB, ntok, col, w = tl[ti]
                C.h, C.hB = hb[ti % 2]
                ffn_a(C, ntok, 1)
                if ti + 1 < len(tl):
                    pre2b(ti + 1)
                ffn_b(C, xap, xB, ntok, 1, 0, 8, w)
            if debug:
                dB = Buf("dbg")
                dma(C, "sp", xdbg, x_sb[:], allxB, [dB], dB)
            C.P.emit(C)
        if stop_after == 3:
            finish(C)
            return nc

        C.P = Prog(nc, "p3")
        with contextlib.ExitStack() as ps:
            new_banks(C, ps, "3")
            alloc_regA(C)
            alloc_regC_proj(C)
            kscrB = Buf("kBscr", group=True)
            vscrB = Buf("vBscr", group=True)
            qscrB = Buf("qBscr", group=True)
            allxB = []
            for ti, T in enumerate(tilesB):
                ntok, col, w = T["ntok"], T["xcol"], T["w"]
                xap = x_sb[:, :, col:col + ntok]
                xB = Buf("x_%d" % ti)
                allxB.append(xB)
                norm_mod(C, xap, xB, ntok, 1, 0, 1, w)
                ffn(C, xap, xB, ntok, 2, 1, 2, w)
                norm_mod(C, xap, xB, ntok, 1, 3, 4, w)
                isq = col < OWN
                specs = [(C.kout[:, c, :ntok], C.koutB) for c in range(4)]
                if isq:
                    specs += [(C.qout[:, c, :ntok], C.qoutB) for c in range(8)]
                if isq:
                    rsrc = (I["cos_t"][:, col:col + ntok], I["sin_t"][:, col:col + ntok])
                else:
                    rsrc = (I["cs_hc"][:, 0, :], I["cs_hc"][:, 1, :])
                qk_chunks(C, ntok, specs, Wb["wqkB_h"], [castB["wqkB_0"]], rsrc,
                          lambda c: 3 if c < 4 else 2)
                v_proj(C, ntok, 1, Wb["wvB_h"], [castB["wvB_0"]], lambda sub, pc: C.vout[:, sub, 0:256], C.voutB)
                dma(C, "pool", kB_scr[:, :, col:col + ntok].rearrange("h p t -> p h t"), C.kout[:, 0:4, :ntok], [C.koutB], [kscrB], kscrB)
                dma(C, "pool", vB_scr[col:col + ntok, :].rearrange("(s p) f -> p s f", p=128), C.vout[:, :ntok // 128, 0:256], [C.voutB], [vscrB], vscrB)
                if isq:
                    dma(C, "pool", qB_scr[:, :, col:col + ntok].rearrange("h p t -> p h t"), C.qout[:, :, :ntok], [C.qoutB], [qscrB], qscrB)
            if debug:
                dB = Buf("dbg")
                dma(C, "sp", xdbg, x_sb[:], allxB, [dB], dB)
                dma(C, "sp", moddbg, mod[:].rearrange("p l j w -> p (l j w)"), [], [dB], dB)
            C.P.emit(C)
        if stop_after == 4:
            finish(C)
            return nc

        C.P = Prog(nc, "p4a")
        with contextlib.ExitStack() as ps:
            new_banks(C, ps, "4a")
            off = 0
            KB, VB, QB = [], [], []
            NKB = XR // 128
            for i in range(2):
                t, off = carve(regA, off, [XR], BF16); KB.append(t)
                t, off = carve(regA, off, [NKB, 64], BF16); VB.append(t)
                t, off = carve(regA, off, [2, OWN], BF16); QB.append(t)
            pT = []
            for i in range(3):
                t, off = carve(regA, off, [512], BF16); pT.append(t)
            es, off = carve(regA, off, [4, 256], F32)
            lbuf, rbuf = [], []
            for i in range(2):
                t, off = carve(regA, off, [256], F32); lbuf.append(t)
                t, off = carve(regA, off, [256], F32); rbuf.append(t)
            assert off <= RA_BYTES, off
            attn_out, _ = carve(regC, 0, [8, XR], BF16)
            aoB = Buf("attn_out")
            KBB, VBB, QBB = [Buf("KB0"), Buf("KB1")], [Buf("VB0"), Buf("VB1")], [Buf("QB0"), Buf("QB1")]
            pTB = [[Buf("pT%d0" % i), Buf("pT%d1" % i)] for i in range(3)]
            esB = Buf("es")
            lB, rB = [Buf("l0"), Buf("l1")], [Buf("r0"), Buf("r1")]
            se = [C.banks[0], C.banks[1]]; seB = [C.bankB[0], C.bankB[1]]
            so = [C.banks[2], C.banks[3]]; soB = [C.bankB[2], C.bankB[3]]
            OB = [C.banks[4], C.banks[5]]; OBB = [C.bankB[4], C.bankB[5]]
            LB = [C.banks[6], C.banks[7]]; LBB = [C.bankB[6], C.bankB[7]]
            for kvh in range(4):
                for j in range(2):
                    for r in range(2):
                        g = 2 * j + r
                        cp(C, es[r * 64:(r + 1) * 64, kvh, j * 128:(j + 1) * 128],
                           esink[r * 64:(r + 1) * 64, kvh * 4 + g:kvh * 4 + g + 1].broadcast_to([64, 128]), [], [esB])
            unitsB = [(kvh, n) for kvh in range(4) for n in range(16)]

            def klist_of(n):
                return [((n - 1) if n > 0 else 16, 0 if n > 0 else 2), (n, None), ((n + 1) if n < 15 else 16, 1 if n < 15 else 3),
                        (17, None), (18, None)]
            itemsB = [(u, i) for u in range(len(unitsB)) for i in range(5)]
            TB = len(itemsB)
            loadedB = set()

            def load_kvh(kvh):
                hs = kvh % 2
                dma(C, "sp", KB[hs], kB_scr[kvh], [], [KBB[hs]], KBB[hs])
                dma(C, "sp", VB[hs], vB_scr[:, kvh * 64:(kvh + 1) * 64].rearrange("(c p) e -> p c e", p=128), [], [VBB[hs]], VBB[hs])
                dma(C, "sp", QB[hs], qB_scr[2 * kvh:2 * kvh + 2].rearrange("j p t -> p j t"), [], [QBB[hs]], QBB[hs])

            def qkB(t):
                u, i = itemsB[t]
                kvh, n = unitsB[u]
                hs = kvh % 2
                if kvh not in loadedB:
                    loadedB.add(kvh)
                    load_kvh(kvh)
                kc = klist_of(n)[i][0]
                st_ = t % 2
                for j in range(2):
                    mm(C, se[st_][:, j * 128:(j + 1) * 128], KB[hs][0:64, kc * 128:(kc + 1) * 128],
                       QB[hs][0:64, j, n * 128:(n + 1) * 128], True, True, [KBB[hs], QBB[hs]], [seB[st_]])
                    mm(C, so[st_][:, j * 128:(j + 1) * 128], KB[hs][64:128, kc * 128:(kc + 1) * 128],
                       QB[hs][64:128, j, n * 128:(n + 1) * 128], True, True, [KBB[hs], QBB[hs]], [soB[st_]])

            def exB(t):
                u, i = itemsB[t]
                st_ = t % 2
                pt_ = t % 3
                act(C, pT[pt_][:, 0:256], se[st_][:, 0:256], AF.Exp, [seB[st_]], [pTB[pt_][0]], scale=0.125)
                act(C, pT[pt_][:, 256:512], so[st_][:, 0:256], AF.Exp, [soB[st_]], [pTB[pt_][1]], scale=0.125)

            def maskB(t):
                u, i = itemsB[t]
                kvh, n = unitsB[u]
                pt_ = t % 3
                m = klist_of(n)[i][1]
                if m is not None:
                    tt(C, pT[pt_].rearrange("p (a b) -> p a b", a=4), pT[pt_].rearrange("p (a b) -> p a b", a=4),
                       masks_bf[:, m, :].unsqueeze(1).broadcast_to([128, 4, 128]), ALU.mult, list(pTB[pt_]), list(pTB[pt_]), eng="pool")

            def pvB(t):
                u, i = itemsB[t]
                kvh, n = unitsB[u]
                hs = kvh % 2
                ob = u % 2
                kc = klist_of(n)[i][0]
                pt_ = t % 3
                for r in range(2):
                    mm(C, OB[ob][r * 64:(r + 1) * 64, 0:256], VB[hs][:, kc, :], pT[pt_][:, r * 256:(r + 1) * 256],
                       i == 0, i == 4, [VBB[hs], pTB[pt_][r]], [OBB[ob]])
                    mm(C, LB[ob][r * 64:(r + 1) * 64, 0:256], ones_bf[:, 0:64], pT[pt_][:, r * 256:(r + 1) * 256],
                       i == 0, i == 4, [pTB[pt_][r]], [LBB[ob]])
                return i == 4

            def epiB(u):
                kvh, n = unitsB[u]
                ob = u % 2
                tt(C, lbuf[ob], LB[ob][:, 0:256], es[:, kvh, :], ALU.add, [LBB[ob], esB], [lB[ob]])
                C.P.add("dve", lambda e: e.reciprocal(out=rbuf[ob], in_=lbuf[ob]), [lB[ob]], [rB[ob]])
                tt(C, attn_out[:, 2 * kvh:2 * kvh + 2, n * 128:(n + 1) * 128], OB[ob][:, 0:256].rearrange("p (j q) -> p j q", j=2),
                   rbuf[ob].rearrange("p (j q) -> p j q", j=2), ALU.mult, [OBB[ob], rB[ob]], [aoB])

            for s_ in range(TB + 3):
                if 0 <= s_ - 3 < TB:
                    if pvB(s_ - 3):
                        epiB(itemsB[s_ - 3][0])
                if 0 <= s_ - 2 < TB:
                    maskB(s_ - 2)
                if 0 <= s_ - 1 < TB:
                    exB(s_ - 1)
                if s_ < TB:
                    qkB(s_)
            C.P.emit(C)
        if stop_after == 5:
            finish(C)
            return nc

        C.P = Prog(nc, "p4b")
        with contextlib.ExitStack() as ps:
            new_banks(C, ps, "4b")
            alloc_regA(C)
            C.lnb = [xtmp[:, 0, :]]
            C.rsb = C.lnb
            C.tmpf = [xtmp[:, 2, :], xtmp[:, 3, :]]
            C.lnB = [Buf("ln")]; C.rsB = C.lnB
            C.tmpfB = [Buf("tmpf0"), Buf("tmpf1")]
            attn_out, _ = carve(regC, 0, [8, XR], BF16)
            aoB = Buf("attn_out")
            yB = Buf("y", group=True)
            hb = two_h(C)
            tl = []
            for ti in range(4):
                col, ntok = 512 * ti, 512
                tl.append((x_sb[:, :, col:col + ntok], Buf("x_%d" % ti), ntok, col, 0))

            def pre4b(ti):
                xap, xB, ntok, col, w = tl[ti]
                C.h, C.hB = hb[ti % 2]
                out_proj(C, xap, xB, ntok, col, attn_out, aoB, "woB_h", [castB["woB_0"]], 1, 0)
                norm_mod(C, xap, xB, ntok, 1, 6, 7, 0)
            pre4b(0)
            for ti in range(4):
                xap, xB, ntok, col, w = tl[ti]
                C.h, C.hB = hb[ti % 2]
                ffn_a(C, ntok, 3)
                if ti + 1 < 4:
                    pre4b(ti + 1)
                ffn_b(C, xap, xB, ntok, 3, 1, 8, 0)
                dma(C, "sp", yT[:, col:col + ntok].rearrange("(c p) t -> p c t", p=128), xap, [xB], [yB], yB)
            C.P.emit(C)
        finish(C)
    return nc


_CACHE = {}


def kernel(**inputs):
    shared = prep_shared(inputs)
    in_maps = []
    for core in range(8):
        b, h = core // 2, core % 2
        m = dict(shared)
        m.update(prep_core(inputs, b, h))
        in_maps.append(m)
    if "nc" not in _CACHE:
        _CACHE["nc"] = build()
    nc = _CACHE["nc"]
    res = run_bass_kernel_spmd(nc, in_maps, core_ids=list(range(8)))
    out = np.empty((4, SEQ, D), np.float32)
    for core in range(8):
        b, h = core // 2, core % 2
        out[b, h * OWN:(h + 1) * OWN, :] = res.results[core]["yT"].T
    return out
```

```python
import contextlib
import math
import numpy as np
import ml_dtypes
import concourse.bass as bass
import concourse.mybir as mybir
from concourse.bass_utils import run_bass_kernel_spmd

F32 = mybir.dt.float32
BF16 = mybir.dt.bfloat16
AF = mybir.ActivationFunctionType
ALU = mybir.AluOpType
AX = mybir.AxisListType

D = 1024
FF = 2816
NFC = 22
SEQ = 4096
NCTX = 256
OWN = 2048
HALO = 128
XR = OWN + HALO + NCTX
NKA = SEQ + NCTX
EPS = 1e-6
LAM_INIT0 = 0.8 - 0.6 * math.exp(-0.3 * 0)

ENGS = ("pe", "act", "dve", "pool", "sp")


class Buf:
    __slots__ = ("name", "writers", "readers", "sem", "dma_count", "group", "persist")

    def __init__(self, name, group=False, persist=False):
        self.name = name
        self.writers = []
        self.readers = {}
        self.sem = None
        self.dma_count = 0
        self.group = group
        self.persist = persist


class Op:
    __slots__ = ("eng", "fn", "deps", "is_dma", "signal", "semval", "buf", "prog")

    def __init__(self, eng, fn, is_dma, prog):
        self.eng = eng
        self.fn = fn
        self.deps = []
        self.is_dma = is_dma
        self.signal = False
        self.semval = None
        self.buf = None
        self.prog = prog


class Prog:
    def __init__(self, nc, name):
        self.nc = nc
        self.name = name
        self.ops = {e: [] for e in ENGS}
        self.all_ops = []

    def _key(self, op):
        return ("dma", id(op)) if op.is_dma else op.eng

    def _live(self, d):
        return d.prog is self or (d.is_dma and d.buf.persist)

    def add(self, eng, fn, reads=(), writes=(), dma_dst=None, after=()):
        op = Op(eng, fn, dma_dst is not None, self)
        op.buf = dma_dst
        deps = list(after)
        for b in reads:
            for w in b.writers:
                if w.is_dma or op.is_dma or w.eng != eng or eng != "pe":
                    deps.append(w)
        for b in writes:
            for r in b.readers.values():
                if r.is_dma or op.is_dma or r.eng != eng or eng != "pe":
                    deps.append(r)
            if not b.group:
                for w in b.writers:
                    if w.is_dma or op.is_dma or w.eng != eng or eng != "pe":
                        deps.append(w)
        seen = set()
        for d in deps:
            if id(d) not in seen and d is not op and self._live(d):
                seen.add(id(d))
                op.deps.append(d)
                d.signal = True
        for b in writes:
            if b.group:
                b.writers.append(op)
            else:
                b.writers = [op]
                b.readers = {}
        for b in reads:
            b.readers[self._key(op)] = op
        if op.is_dma:
            dma_dst.dma_count += 1
            op.semval = 16 * dma_dst.dma_count
        self.ops[eng].append(op)
        self.all_ops.append(op)
        return op

    def emit(self, C):
        nc = self.nc
        for e in ENGS:
            comp = [o for o in self.ops[e] if not o.is_dma]
            if comp:
                comp[-1].signal = True
            c = 0
            for op in comp:
                if op.signal:
                    c += 1
                    op.semval = c
        dma_bufs = []
        for op in self.all_ops:
            if op.is_dma and (not op.buf.persist) and op.buf not in dma_bufs:
                dma_bufs.append(op.buf)
        k = C.phase_idx
        cur, other = C.semsets[k % 2], C.semsets[(k + 1) % 2]
        assert len(dma_bufs) <= len(cur["dma"]), (self.name, len(dma_bufs))
        esem = cur["eng"]
        bar = C.bar
        bar_target = len(ENGS) * (k + 1)
        for i, b in enumerate(dma_bufs):
            b.sem = cur["dma"][i]
        with nc.Block() as block:

            def run(eng_name):
                def body(eng):
                    if eng_name == "pool" and k >= 1:
                        for sname in ENGS:
                            eng.sem_clear(other["eng"][sname])
                        for sm in other["dma"]:
                            eng.sem_clear(sm)
                    waited = {}
                    last_comp = None
                    my_dma = {}
                    for op in self.ops[eng_name]:
                        need = {}
                        for d in op.deps:
                            sem = d.buf.sem if d.is_dma else esem[d.eng]
                            kk = id(sem)
                            if kk not in need or need[kk][1] < d.semval:
                                need[kk] = (sem, d.semval)
                        for kk, (sem, v) in need.items():
                            if waited.get(kk, 0) < v:
                                eng.wait_ge(sem, v)
                                waited[kk] = v
                        ins = op.fn(eng)
                        if op.is_dma:
                            ins.then_inc(op.buf.sem, 16)
                            if not op.buf.persist:
                                my_dma[id(op.buf)] = (op.buf, op.semval)
                        else:
                            last_comp = op
                            if op.signal:
                                ins.then_inc(esem[eng_name], 1)
                    for b, v in my_dma.values():
                        eng.wait_ge(b.sem, v)
                    if last_comp is not None:
                        eng.wait_ge(esem[eng_name], last_comp.semval)
                    eng.sem_inc(bar, 1)
                    eng.wait_ge(bar, bar_target)
                return body

            block.tensor(run("pe"))
            block.scalar(run("act"))
            block.vector(run("dve"))
            block.gpsimd(run("pool"))
            block.sync(run("sp"))
        C.phase_idx += 1
        for b in dma_bufs:
            b.dma_count = 0
            b.sem = None
            b.writers = []
            b.readers = {}


class Ctx:
    pass


def mm(C, out, lhsT, rhs, start, stop, reads, writes, **kw):
    return C.P.add("pe", lambda e: e.matmul(out, lhsT, rhs, start=start, stop=stop, **kw), reads, writes)


def act(C, out, in_, func, reads, writes, **kw):
    C.P.add("act", lambda e: e.activation(out=out, in_=in_, func=func, **kw), reads, writes)


def stt(C, out, in0, scalar, in1, op0, op1, reads, writes, eng="dve"):
    C.P.add(eng, lambda e: e.scalar_tensor_tensor(out=out, in0=in0, scalar=scalar, in1=in1, op0=op0, op1=op1), reads, writes)


def tt(C, out, in0, in1, op, reads, writes, eng="dve"):
    C.P.add(eng, lambda e: e.tensor_tensor(out=out, in0=in0, in1=in1, op=op), reads, writes)


def ts(C, out, in0, s1, s2, op0, op1, reads, writes, eng="dve"):
    if s2 is None:
        C.P.add(eng, lambda e: e.tensor_scalar(out=out, in0=in0, scalar1=s1, scalar2=None, op0=op0), reads, writes)
    else:
        C.P.add(eng, lambda e: e.tensor_scalar(out=out, in0=in0, scalar1=s1, scalar2=s2, op0=op0, op1=op1), reads, writes)


def cp(C, out, in_, reads, writes, eng="dve"):
    C.P.add(eng, lambda e: e.tensor_copy(out=out, in_=in_), reads, writes)


def dma(C, eng, out, in_, reads, writes, dst, after=()):
    return C.P.add(eng, lambda e: e.dma_start(out=out, in_=in_), reads, writes, dma_dst=dst, after=after)


def pipeline(stages, n):
    ns = len(stages)
    for s in range(n + ns - 1):
        for k in reversed(range(ns)):
            c = s - k
            if 0 <= c < n:
                stages[k](c)


class Stream:
    def __init__(self, C, tens, name):
        self.C = C
        self.tens = tens
        self.bufs = [Buf("%s%d" % (name, i)) for i in range(len(tens))]
        self.i = 0

    def load(self, src, src_bufs, view=None, eng="sp"):
        s = self.i % len(self.tens)
        self.i += 1
        t = self.tens[s]
        dst = view(t) if view is not None else t
        dma(self.C, eng, dst, src, list(src_bufs), [self.bufs[s]], self.bufs[s])
        return t, self.bufs[s]


def _perm(h):
    if h == 0:
        own = np.arange(0, 2048)
        halo = np.arange(2048, 2176)
        rest = np.arange(2176, 4096)
    else:
        own = np.arange(2048, 4096)
        halo = np.arange(1920, 2048)
        rest = np.arange(0, 1920)
    return np.concatenate([own, halo, rest])


def _rope_tables(pos):
    nf = 16
    inv = (10000.0 ** (-np.arange(nf, dtype=np.float32) / nf)).astype(np.float32)
    row = (pos // 64).astype(np.float32)
    col = (pos % 64).astype(np.float32)
    ang = np.zeros((64, pos.shape[0]), np.float32)
    for d in range(64):
        a = d // 32
        f = d % 16
        ang[d] = (row if a == 0 else col) * inv[f]
    ang = np.concatenate([ang, ang], 0)
    return np.cos(ang).astype(np.float32), np.sin(ang).astype(np.float32)


def _chunk_w(W, cols):
    return np.ascontiguousarray(W[:, cols].reshape(8, 128, len(cols)).transpose(1, 0, 2))


def prep_shared(inp):
    f = np.float32
    out = {}
    ada_w = np.asarray(inp["ada_w"], f)
    out["ada_h"] = np.ascontiguousarray(
        ada_w.reshape(2, 8, 128, 72, 128).transpose(0, 3, 2, 1, 4)).reshape(144, 128, 1024)
    ada_b = np.asarray(inp["ada_b"], f)
    out["ada_bT"] = np.ascontiguousarray(ada_b.reshape(2, 72, 128).transpose(2, 0, 1)).reshape(128, 144)
    wi_all, wo_all = [], []
    for l in range(2):
        for (wi_name, wo_name) in (("ffn_pre_wi", "ffn_pre_wo"), ("ffn_post_wi", "ffn_post_wo")):
            wi = np.asarray(inp[wi_name][l], f)
            wo = np.asarray(inp[wo_name][l], f)
            g = wi[:, :FF].reshape(8, 128, NFC, 128)
            u = wi[:, FF:].reshape(8, 128, NFC, 128)
            gu = np.stack([g, u], axis=3)
            wi_all.append(np.ascontiguousarray(gu.transpose(2, 1, 0, 3, 4)).reshape(NFC * 128, 2048))
            wo_all.append(np.ascontiguousarray(wo.reshape(NFC, 128, 8, 128).transpose(2, 1, 0, 3)).reshape(8 * 128, FF))
    out["wi_h"] = np.stack(wi_all)
    out["wo_h"] = np.stack(wo_all)
    wa = np.asarray(inp["a_w_qkv"][0], f)
    chunks = [1024 + h * 128 for h in range(8)] + [h * 128 for h in range(8)]
    qk = [_chunk_w(wa, np.arange(c0, c0 + 128)) for c0 in chunks]
    out["wqkA_h"] = np.ascontiguousarray(
        np.stack([np.stack([qk[2 * p], qk[2 * p + 1]], axis=1) for p in range(8)])).reshape(8 * 128, 2048)
    out["wvA_h"] = np.ascontiguousarray(
        np.stack([_chunk_w(wa, np.arange(2048 + pc * 256, 2048 + (pc + 1) * 256)) for pc in range(4)])).reshape(4 * 128, 2048)
    woa = np.asarray(inp["a_w_o"][0], f)
    out["woA_h"] = np.ascontiguousarray(woa.reshape(8, 128, 8, 128).transpose(2, 1, 0, 3)).reshape(8 * 128, 1024)
    wb = np.asarray(inp["b_w_qkv"][0], f)
    colsB = []
    for kvh in range(4):
        c = np.arange(1024 + kvh * 64, 1024 + (kvh + 1) * 64)
        colsB.append(np.concatenate([c, c]))
    for j in range(8):
        colsB.append(np.arange(j * 128, (j + 1) * 128))
    qkb = [_chunk_w(wb, c) for c in colsB]
    out["wqkB_h"] = np.ascontiguousarray(
        np.stack([np.stack([qkb[2 * p], qkb[2 * p + 1]], axis=1) for p in range(6)])).reshape(6 * 128, 2048)
    out["wvB_h"] = _chunk_w(wb, np.arange(1280, 1536)).reshape(128, 2048)
    wob = np.asarray(inp["b_w_o"][0], f)
    out["woB_h"] = np.ascontiguousarray(wob.reshape(8, 128, 8, 128).transpose(2, 1, 0, 3)).reshape(8 * 128, 1024)
    sp = np.zeros((128, 32), f)
    sp[:, 0] = np.tile(np.asarray(inp["a_q_gain"][0], f), 2)
    sp[:, 1] = np.tile(np.asarray(inp["a_k_gain"][0], f), 2)
    sp[:, 2] = np.tile(np.asarray(inp["b_q_gain"][0], f), 2)
    sp[:, 3] = np.tile(np.asarray(inp["b_k_gain"][0], f), 2)
    sp[:, 4] = np.asarray(inp["a_subln_gain"][0], f)
    sp[:, 5:21] = np.asarray(inp["b_sink"][0], f)[None, :]
    out["smallp"] = sp
    lam = np.asarray(inp["a_lambda"][0], f)
    out["lamv"] = np.concatenate([lam[0], lam[2], lam[1], lam[3]])[None, :].astype(f)
    rm = np.zeros((128, 128), f)
    for m in range(128):
        if (m % 32) < 16:
            rm[m + 16, m] = -1.0
        else:
            rm[m - 16, m] = 1.0
    out["rmat"] = rm
    return out


def prep_core(inp, b, h):
    f = np.float32
    out = {}
    perm = _perm(h)
    x = np.asarray(inp["x"], f)
    out["xT"] = np.ascontiguousarray(x[b][perm].T)
    out["ctxT"] = np.ascontiguousarray(np.asarray(inp["ctx"], f)[b].T)
    cv = np.stack([np.asarray(inp["c"], f)[b], np.asarray(inp["c_ctx"], f)], axis=1)
    out["cvec"] = np.ascontiguousarray(cv.reshape(8, 128, 2).transpose(1, 0, 2)).reshape(128, 16)
    cos, sin = _rope_tables(perm)
    out["cos_t"] = cos
    out["sin_t"] = sin
    cs_hc = np.zeros((128, 2, HALO + NCTX), f)
    cs_hc[:, 0, :HALO] = cos[:, OWN:OWN + HALO]
    cs_hc[:, 0, HALO:] = 1.0
    cs_hc[:, 1, :HALO] = sin[:, OWN:OWN + HALO]
    out["cs_hc"] = cs_hc
    jj = np.arange(128)[:, None]
    ii = np.arange(128)[None, :]
    mprev = (jj >= ii).astype(f)
    mnext = (jj <= ii).astype(f)
    out["masks"] = np.ascontiguousarray(
        np.stack([mprev, mnext, mprev * (1.0 if h == 1 else 0.0), mnext * (1.0 if h == 0 else 0.0)], axis=1)).reshape(128, 512)
    return out


SHARED_SHAPES = {
    "ada_h": [144, 128, 1024], "ada_bT": [128, 144], "wi_h": [4, NFC * 128, 2048], "wo_h": [4, 1024, FF],
    "wqkA_h": [1024, 2048], "wvA_h": [512, 2048], "woA_h": [1024, 1024], "wqkB_h": [768, 2048],
    "wvB_h": [128, 2048], "woB_h": [1024, 1024], "smallp": [128, 32], "lamv": [1, 256], "rmat": [128, 128],
}
CORE_SHAPES = {
    "xT": [D, SEQ], "ctxT": [D, NCTX], "cs_hc": [128, 2, HALO + NCTX], "cvec": [128, 16], "cos_t": [128, SEQ], "sin_t": [128, SEQ], "masks": [128, 512],
}


def finish(C):
    nc = C.nc
    nc.all_engine_barrier()
    nc.clear_and_free_semaphores(C.all_sems)
    nc.all_engine_barrier()


def build(stop_after=None, debug=False):
    nc = bass.Bass("TRN2", target_bir_lowering=False)
    C = Ctx()
    C.nc = nc
    I = {}
    for k, shp in list(SHARED_SHAPES.items()) + list(CORE_SHAPES.items()):
        I[k] = nc.dram_tensor(k, shp, F32, kind="ExternalInput").ap()
    yT = nc.dram_tensor("yT", [D, OWN], F32, kind="ExternalOutput").ap()
    dk = "ExternalOutput" if debug else "Internal"
    Wb = {}
    for k in ("wi_h", "wo_h", "wqkA_h", "wvA_h", "woA_h", "wqkB_h", "wvB_h", "woB_h"):
        Wb[k] = nc.dram_tensor(k + "_b", SHARED_SHAPES[k], BF16).ap()
    kT_scr = nc.dram_tensor("kT_scr", [8, 128, NKA], BF16, kind=dk).ap()
    v_scr = nc.dram_tensor("v_scr", [NKA, D], BF16, kind=dk).ap()
    qT_scr = nc.dram_tensor("qT_scr", [8, 128, XR], BF16, kind=dk).ap()
    kB_scr = nc.dram_tensor("kB_scr", [4, 128, XR], BF16, kind=dk).ap()
    vB_scr = nc.dram_tensor("vB_scr", [XR, 256], BF16, kind=dk).ap()
    qB_scr = nc.dram_tensor("qB_scr", [8, 128, OWN], BF16, kind=dk).ap()
    if debug:
        xdbg = nc.dram_tensor("xdbg", [128, 8, XR], F32, kind="ExternalOutput").ap()
        moddbg = nc.dram_tensor("moddbg", [128, 288], F32, kind="ExternalOutput").ap()
        aodbg = nc.dram_tensor("aodbg", [128, 8, XR], BF16, kind="ExternalOutput").ap()

    with contextlib.ExitStack() as gs:
        sb = lambda name, shape, dt: gs.enter_context(nc.sbuf_tensor(name, shape, dt))
        x_sb = sb("x_sb", [128, 8, XR], F32)
        mod = sb("mod", [128, 2, 72, 2], F32)
        adab = sb("adab", [128, 2, 72], F32)
        smallp = sb("smallp_s", [128, 32], F32)
        csil = sb("csil", [128, 8, 2], F32)
        ones_bf = sb("ones_bf", [128, 128], BF16)
        bd_bf = sb("bd_bf", [128, 128], BF16)
        rm_bf = sb("rm_bf", [128, 128], BF16)
        ones_f = sb("ones_f", [128, 128], F32)
        neglam = sb("neglam", [128, 2], F32)
        gain2 = sb("gain2", [128, 1], F32)
        esink = sb("esink", [128, 16], F32)
        masks_bf = sb("masks_bf", [128, 4, 128], BF16)
        RA_BYTES = 65 * 1024
        RC_BYTES = 46 * 1024
        regA = sb("regA", [128, RA_BYTES // 4], F32)
        regC = sb("regC", [128, RC_BYTES // 4], F32)
        xtmp = sb("xtmp", [128, 8, 512], F32)

        def carve(reg, off, shape, dt):
            n = int(np.prod(shape))
            bpe = 4 if dt == F32 else 2
            assert off % 4 == 0 and (n * bpe) % 4 == 0
            v = reg[:, off // 4:(off + n * bpe) // 4]
            if dt != F32:
                v = v.bitcast(dt)
            if len(shape) == 2:
                v = v.rearrange("p (a b) -> p a b", a=shape[0])
            elif len(shape) == 3:
                v = v.rearrange("p (a b c) -> p a b c", a=shape[0], b=shape[1])
            return v, off + n * bpe

        C.all_sems = []

        def new_sem(name):
            h = nc.alloc_semaphore(name=name)
            C.all_sems.append(h)
            return h
        NDMA_SEMS = 24
        C.semsets = [dict(eng={e: new_sem("s%d_%s" % (i, e)) for e in ENGS}, dma=[new_sem("d%d_%d" % (i, j)) for j in range(NDMA_SEMS)])
                     for i in range(2)]
        C.bar = new_sem("bar")
        C.phase_idx = 0
        castB = {}

        def cast_buf(name):
            b = Buf(name, persist=True, group=True)
            b.sem = new_sem("c_" + name)
            castB[name] = b
            return b

        C.P = Prog(nc, "p0")
        with contextlib.ExitStack() as ps:
            psb = lambda name, shape, dt: ps.enter_context(nc.sbuf_tensor(name, shape, dt))
            banks = [ps.enter_context(nc.psum_tensor("b0_%d" % i, [128, 512], F32)) for i in range(8)]
            bankB = [Buf("bank%d" % i) for i in range(8)]
            C.cast_queue = None

            def cast(key, idx_slices, tag):
                for i, sl in enumerate(idx_slices):
                    b = cast_buf("%s_%d" % (tag, i))
                    src = I[key]
                    dst = Wb[key]
                    for s in sl[:-1]:
                        src = src[s]
                        dst = dst[s]
                    r0, r1 = sl[-1]
                    ncol = src.shape[-1]
                    cols = [(0, ncol)] if ncol <= 2048 else [(0, 1408), (1408, 2816)]
                    rstep = 704 if ncol <= 2048 else 512
                    for (c0, c1) in cols:
                        for ra in range(r0, r1, rstep):
                            rb = min(r1, ra + rstep)

                            def issue(after=(), src=src, dst=dst, ra=ra, rb=rb, c0=c0, c1=c1, b=b):
                                dma(C, "pool", dst[ra:rb, c0:c1], src[ra:rb, c0:c1], [], [b], b, after=after)
                            if C.cast_queue is None:
                                issue()
                            else:
                                C.cast_queue.append(issue)

            C.wi_bounds = {0: [0, 2, 11, NFC], 1: [0, 11, NFC], 2: [0, 11, NFC], 3: [0, 11, NFC]}
            def ffn_cast(f):
                bnd = C.wi_bounds[f]
                cast("wi_h", [(f, (bnd[i] * 128, bnd[i + 1] * 128)) for i in range(len(bnd) - 1)], "wi%d" % f)
                cast("wo_h", [(f, (0, 1024))], "wo%d" % f)
            ffn_cast(0)

            def early_casts():
                cast("wqkA_h", [((0, 1024),)], "wqkA")
                cast("wvA_h", [((0, 512),)], "wvA")

            def deferred_casts():
                cast("woA_h", [((0, 1024),)], "woA")
                ffn_cast(1)
                ffn_cast(2)
                cast("wqkB_h", [((0, 768),)], "wqkB")
                cast("wvB_h", [((0, 128),)], "wvB")
                cast("woB_h", [((0, 1024),)], "woB")
                ffn_cast(3)

            def wiB2(f, j):
                bnd = C.wi_bounds[f]
                for i in range(len(bnd) - 1):
                    if bnd[i] <= j < bnd[i + 1]:
                        return [castB["wi%d_%d" % (f, i)]]

            def woB_(f, dc):
                return [castB["wo%d_0" % f]]
            C.wiB2 = wiB2
            C.woB_ = woB_

            t_cv = xtmp[:, 0, 0:16]
            t_lam = xtmp[0:1, 1, 0:256]
            t_lam2 = xtmp[0:1, 2, 0:8]
            t_rm = xtmp[:, 3, 0:128]
            t_mk = xtmp[:, 4, 0:512]
            Bs = {n: Buf(n) for n in ["cv", "lam", "lam2", "rm", "mk", "smallp", "adab", "csil", "consts", "neglam", "gain2", "esink", "masksbf", "mod"]}
            dma(C, "sp", t_cv, I["cvec"], [], [Bs["cv"]], Bs["cv"])
            dma(C, "sp", smallp[:], I["smallp"], [], [Bs["smallp"]], Bs["smallp"])
            dma(C, "sp", adab[:].rearrange("p l j -> p (l j)"), I["ada_bT"], [], [Bs["adab"]], Bs["adab"])
            dma(C, "sp", t_lam, I["lamv"], [], [Bs["lam"]], Bs["lam"])
            dma(C, "sp", t_rm, I["rmat"], [], [Bs["rm"]], Bs["rm"])
            dma(C, "sp", t_mk, I["masks"], [], [Bs["mk"]], Bs["mk"])
            C.P.add("dve", lambda e: e.memset(ones_bf[:], 1.0), [], [Bs["consts"]])
            C.P.add("dve", lambda e: e.memset(ones_f[:], 1.0), [], [Bs["consts"]])
            C.P.add("dve", lambda e: e.memset(bd_bf[:], 0.0), [], [Bs["consts"]])
            C.P.add("dve", lambda e: e.memset(bd_bf[0:64, 0:64], 1.0), [], [Bs["consts"]])
            C.P.add("dve", lambda e: e.memset(bd_bf[64:128, 64:128], 1.0), [], [Bs["consts"]])
            cp(C, rm_bf[:], t_rm, [Bs["rm"]], [Bs["consts"]])
            cp(C, masks_bf[:].rearrange("p a b -> p (a b)"), t_mk, [Bs["mk"]], [Bs["masksbf"]])
            act(C, csil[:].rearrange("p a b -> p (a b)"), t_cv, AF.Silu, [Bs["cv"]], [Bs["csil"]])
            tt(C, t_lam[:, 0:128], t_lam[:, 0:128], t_lam[:, 128:256], ALU.mult, [Bs["lam"]], [Bs["lam"]])
            C.P.add("dve", lambda e: e.tensor_reduce(out=t_lam2[:, 0:2], in_=t_lam[:, 0:128].rearrange("p (a b) -> p a b", a=2), axis=AX.X, op=ALU.add),
                    [Bs["lam"]], [Bs["lam2"]])
            act(C, t_lam2[:, 2:4], t_lam2[:, 0:2], AF.Exp, [Bs["lam2"]], [Bs["lam2"]])
            tt(C, t_lam2[:, 4:5], t_lam2[:, 2:3], t_lam2[:, 3:4], ALU.subtract, [Bs["lam2"]], [Bs["lam2"]])
            ts(C, t_lam2[:, 6:7], t_lam2[:, 4:5], -1.0, -LAM_INIT0, ALU.mult, ALU.add, [Bs["lam2"]], [Bs["lam2"]])
            cp(C, t_lam2[:, 7:8], t_lam2[:, 6:7], [Bs["lam2"]], [Bs["lam2"]])
            mm(C, banks[7][:, 0:2], ones_f[0:1, :], t_lam2[0:1, 6:8], True, True, [Bs["consts"], Bs["lam2"]], [bankB[7]])
            cp(C, neglam[:], banks[7][:, 0:2], [bankB[7]], [Bs["neglam"]])
            ts(C, gain2[:], smallp[:, 4:5], 1.0 - LAM_INIT0, None, ALU.mult, None, [Bs["smallp"]], [Bs["gain2"]])
            act(C, esink[:], smallp[:, 5:21], AF.Exp, [Bs["smallp"]], [Bs["esink"]])
            C.P.emit(C)
        if debug:
            pass

        def alloc_regA(C):
            off = 0
            C.h, off = carve(regA, off, [8, 512], BF16)
            C.G, off = carve(regA, off, [NFC, 512], BF16)
            C.sg = []
            for i in range(2):
                t, off = carve(regA, off, [512], F32)
                C.sg.append(t)
            C.slots = []
            for i in range(4):
                t, off = carve(regA, off, [2048], BF16)
                C.slots.append(t)
            C.woslots = []
            for i in range(2):
                t, off = carve(regA, off, [NFC, 128], BF16)
                C.woslots.append(t)
            C.sq = []
            for i in range(4):
                t, off = carve(regA, off, [512], BF16)
                C.sq.append(t)
            assert off <= RA_BYTES, off
            C.hB = [Buf("h%d" % i) for i in range(8)]
            C.GB = Buf("G")
            C.sgB = [Buf("sg0"), Buf("sg1")]
            C.sqB = [Buf("sq%d" % i) for i in range(4)]
            C.stream = Stream(C, C.slots, "slot")
            C.wostream = Stream(C, C.woslots, "woslot")

        def alloc_regC_proj(C):
            off = 0
            C.lnb, C.rsb, C.tmpf, C.qn, C.t1, C.t2 = [], [], [], [], [], []
            for i in range(2):
                t, off = carve(regC, off, [512], F32); C.lnb.append(t); C.rsb.append(t)
                t, off = carve(regC, off, [512], F32); C.tmpf.append(t); C.t1.append(t)
                t, off = carve(regC, off, [512], BF16); C.qn.append(t)
                t, off = carve(regC, off, [512], F32); C.t2.append(t)
            C.qout, off = carve(regC, off, [8, 512], BF16)
            C.kout, off = carve(regC, off, [8, 512], BF16)
            C.vout, off = carve(regC, off, [4, 1024], BF16)
            C.cs, C.sn = [], []
            for i in range(2):
                t, off = carve(regC, off, [512], F32); C.cs.append(t)
                t, off = carve(regC, off, [512], F32); C.sn.append(t)
            assert off <= RC_BYTES, off
            mk = lambda n: [Buf(n + "0"), Buf(n + "1")]
            C.lnB = mk("ln"); C.rsB = C.lnB
            C.tmpfB = mk("tmpf"); C.t1B = C.tmpfB
            C.qnB, C.t2B = mk("qn"), mk("t2")
            C.qoutB, C.koutB, C.voutB = Buf("qout"), Buf("kout"), Buf("vout")
            C.csB = mk("cs")
            C.cs_stream_i = 0

        def two_h(C):
            h2 = xtmp[:, 4:8, :].rearrange("p a b -> p (a b)").bitcast(BF16).rearrange("p (c t) -> p c t", c=8)
            return [(C.h, C.hB), (h2, [Buf("hx%d" % i) for i in range(8)])]

        def new_banks(C, ps, tag):
            C.banks = [ps.enter_context(nc.psum_tensor("b%s_%d" % (tag, i), [128, 512], F32)) for i in range(8)]
            C.bankB = [Buf("bank%d" % i) for i in range(8)]

        def segs_of(w, ntok):
            return w if isinstance(w, list) else [(0, ntok, w)]

        def modap(l, v, ch, w):
            return mod[:, l, v * 8 + ch, w:w + 1]

        C.modB = Buf("mod")
        C.constB = Buf("const")

        def mod_vector(C, l, v, defer=False, cb=0):
            pb = C.banks[7]
            pB = C.bankB[7]
            for ch in range(8):
                j = v * 8 + ch
                slot, sB = C.stream.load(I["ada_h"][l * 72 + j], [], view=lambda t: t.bitcast(F32))
                w32 = slot.bitcast(F32).rearrange("p (c f) -> p c f", c=8)
                for dc in range(8):
                    mm(C, pb[:, cb + 2 * ch:cb + 2 * ch + 2], w32[:, dc, :], csil[:, dc, :], dc == 0, dc == 7, [sB], [pB])

            def evac():
                dst = mod[:, l, v * 8:(v + 1) * 8, :]
                tt(C, dst, pb[:, cb:cb + 16].rearrange("p (a b) -> p a b", a=8),
                   adab[:, l, v * 8:(v + 1) * 8].unsqueeze(2).broadcast_to([128, 8, 2]), ALU.add, [pB], [C.modB])
                if v in (1, 4, 7):
                    ts(C, dst, dst, 1.0, None, ALU.add, None, [C.modB], [C.modB])
                if v in (2, 8):
                    ts(C, dst, dst, 0.5, None, ALU.mult, None, [C.modB], [C.modB])
            if defer:
                return evac
            evac()
            return None

        def norm_mod(C, xap, xB, ntok, l, vsh, vsc, w, filler=None):
            ssb, ssB = C.banks[6], C.bankB[6]
            for ch in range(8):
                s = ch % 4
                if ch in (1, 4, 6):
                    tt(C, C.sq[s][:, :ntok], xap[:, ch, :], xap[:, ch, :], ALU.mult, [xB], [C.sqB[s]])
                else:
                    act(C, C.sq[s][:, :ntok], xap[:, ch, :], AF.Square, [xB], [C.sqB[s]])
                mm(C, ssb[:, :ntok], ones_bf[:], C.sq[s][:, :ntok], ch == 0, ch == 7, [C.sqB[s]], [ssB])
            after = filler() if filler is not None else None
            act(C, C.lnb[0][:, :ntok], ssb[:, :ntok], AF.Ln, [ssB], [C.lnB[0]], bias=EPS, scale=1.0 / D)
            act(C, C.rsb[0][:, :ntok], C.lnb[0][:, :ntok], AF.Exp, [C.lnB[0]], [C.rsB[0]], scale=-0.5)
            for ch in range(8):
                s = ch % 2
                for (lo, hi, ww) in segs_of(w, ntok):
                    stt(C, C.tmpf[s][:, lo:hi], xap[:, ch, lo:hi], modap(l, vsc, ch, ww), C.rsb[0][:, lo:hi], ALU.mult, ALU.mult,
                        [xB, C.rsB[0], C.modB], [C.tmpfB[s]])
                    act(C, C.h[:, ch, lo:hi], C.tmpf[s][:, lo:hi], AF.Identity, [C.tmpfB[s], C.modB], [C.hB[ch]], bias=modap(l, vsh, ch, ww))
            if after is not None:
                after()

        def ffn(C, xap, xB, ntok, f, l, vg, w):
            ffn_a(C, ntok, f)
            ffn_b(C, xap, xB, ntok, f, l, vg, w)

        def ffn_a(C, ntok, f):
            for j in range(NFC):
                slot, sB = C.stream.load(Wb["wi_h"][f, j * 128:(j + 1) * 128, :], C.wiB2(f, j))
                wv = slot.rearrange("p (c f) -> p c f", c=8)
                bg, bgB = C.banks[2 * (j % 2)], C.bankB[2 * (j % 2)]
                bu, buB = C.banks[2 * (j % 2) + 1], C.bankB[2 * (j % 2) + 1]
                for dc in range(8):
                    mm(C, bg[:, :ntok], wv[:, dc, 0:128], C.h[:, dc, :ntok], dc == 0, dc == 7, [sB, C.hB[dc]], [bgB])
                for dc in range(8):
                    mm(C, bu[:, :ntok], wv[:, dc, 128:256], C.h[:, dc, :ntok], dc == 0, dc == 7, [sB, C.hB[dc]], [buB])
                s = j % 2
                act(C, C.sg[s][:, :ntok], bg[:, :ntok], AF.Silu, [bgB], [C.sgB[s]])
                tt(C, C.G[:, j, :ntok], bu[:, :ntok], C.sg[s][:, :ntok], ALU.mult, [buB, C.sgB[s]], [C.GB])

        def ffn_b(C, xap, xB, ntok, f, l, vg, w, mid=None):
            for dc in range(8):
                if dc == 4 and mid is not None:
                    hsave = (C.h, C.hB)
                    mid()
                    C.h, C.hB = hsave
                slot, sB = C.wostream.load(Wb["wo_h"][f, dc * 128:(dc + 1) * 128, :], C.woB_(f, dc),
                                           view=lambda t: t.rearrange("p a b -> p (a b)"))
                by, byB = C.banks[4 + dc % 2], C.bankB[4 + dc % 2]
                for j in range(NFC):
                    mm(C, by[:, :ntok], slot[:, j, :], C.G[:, j, :ntok], j == 0, j == NFC - 1, [sB, C.GB], [byB])
                for (lo, hi, ww) in segs_of(w, ntok):
                    stt(C, xap[:, dc, lo:hi], by[:, lo:hi], modap(l, vg, dc, ww), xap[:, dc, lo:hi], ALU.mult, ALU.add,
                        [byB, xB, C.modB], [xB])

        def qk_chunks(C, ntok, chunk_specs, wsrc, wsrcB, rope_off, gcol_fn):
            n = len(chunk_specs)
            state = {}
            if rope_off is not None:
                k = C.cs_stream_i % 2
                C.cs_stream_i += 1
                dma(C, "sp", C.cs[k][:, :ntok], rope_off[0], [], [C.csB[k]], C.csB[k])
                dma(C, "sp", C.sn[k][:, :ntok], rope_off[1], [], [C.csB[k]], C.csB[k])
                cs, sn, csB = C.cs[k], C.sn[k], C.csB[k]

            def proj(c):
                if c % 2 == 0:
                    slot, sB = C.stream.load(wsrc[(c // 2) * 128:(c // 2 + 1) * 128, :], wsrcB)
                    state["slot"] = (slot, sB)
                slot, sB = state["slot"]
                wv = slot.rearrange("p (i c f) -> p i c f", i=2, c=8)
                pb, pB = C.banks[c % 4], C.bankB[c % 4]
                for dc in range(8):
                    mm(C, pb[:, :ntok], wv[:, c % 2, dc, :], C.h[:, dc, :ntok], dc == 0, dc == 7, [sB, C.hB[dc]], [pB])

            def sqr(c):
                s = c % 2
                pb, pB = C.banks[c % 4], C.bankB[c % 4]
                act(C, C.sq[s][:, :ntok], pb[:, :ntok], AF.Square, [pB], [C.sqB[s]])

            def ssmm(c):
                s = c % 2
                mm(C, C.banks[6][:, :ntok], bd_bf[:], C.sq[s][:, :ntok], True, True, [C.sqB[s]], [C.bankB[6]])

            def lnq(c):
                s = c % 2
                pb, pB = C.banks[c % 4], C.bankB[c % 4]
                act(C, C.lnb[s][:, :ntok], C.banks[6][:, :ntok], AF.Ln, [C.bankB[6]], [C.lnB[s]], bias=EPS, scale=1.0 / 64)
                act(C, C.rsb[s][:, :ntok], C.lnb[s][:, :ntok], AF.Exp, [C.lnB[s]], [C.rsB[s]], scale=-0.5)
                dst, dstB = chunk_specs[c]
                gc = gcol_fn(c)
                if rope_off is None:
                    stt(C, dst, pb[:, :ntok], smallp[:, gc:gc + 1], C.rsb[s][:, :ntok], ALU.mult, ALU.mult, [pB, C.rsB[s]], [dstB])
                else:
                    stt(C, C.qn[s][:, :ntok], pb[:, :ntok], smallp[:, gc:gc + 1], C.rsb[s][:, :ntok], ALU.mult, ALU.mult,
                        [pB, C.rsB[s]], [C.qnB[s]])

            def rotmm(c):
                if rope_off is None:
                    return
                s = c % 2
                mm(C, C.banks[7][:, :ntok], rm_bf[:], C.qn[s][:, :ntok], True, True, [C.qnB[s]], [C.bankB[7]])

            def fin(c):
                if rope_off is None:
                    return
                s = c % 2
                dst, dstB = chunk_specs[c]
                tt(C, C.t2[s][:, :ntok], C.banks[7][:, :ntok], sn[:, :ntok], ALU.mult, [C.bankB[7], csB], [C.t2B[s]])
                tt(C, C.t1[s][:, :ntok], C.qn[s][:, :ntok], cs[:, :ntok], ALU.mult, [C.qnB[s], csB], [C.t1B[s]])
                tt(C, dst, C.t1[s][:, :ntok], C.t2[s][:, :ntok], ALU.add, [C.t1B[s], C.t2B[s]], [dstB])

            pipeline([proj, sqr, ssmm, lnq, rotmm, fin], n)

        def v_proj(C, ntok, npieces, wsrc, wsrcB, vdst_fn, vB):
            nsub = ntok // 128
            cnt = 0
            for pc in range(npieces):
                slot, sB = C.stream.load(wsrc[pc * 128:(pc + 1) * 128, :], wsrcB)
                wv = slot.rearrange("p (c f) -> p c f", c=8)
                for sub in range(nsub):
                    pb, pB = C.banks[4 + cnt % 2], C.bankB[4 + cnt % 2]
                    cnt += 1
                    for dc in range(8):
                        mm(C, pb[:, 0:256], C.h[:, dc, sub * 128:(sub + 1) * 128], wv[:, dc, :], dc == 0, dc == 7, [sB, C.hB[dc]], [pB])
                    act(C, vdst_fn(sub, pc), pb[:, 0:256], AF.Copy, [pB], [vB])

        tilesA = []
        for i in range(4):
            tilesA.append(dict(kind="own", xcol=512 * i, ntok=512, src=("x", 512 * i), key=512 * i, rope=512 * i, q=True, w=0))
        HC = HALO + NCTX
        HCSEG = [(0, HALO, 0), (HALO, HC, 1)]
        tilesA.append(dict(kind="hc", xcol=OWN, ntok=HC, src=("hc", OWN), key=None, rope="hc", q=True, w=HCSEG))
        o = OWN + HALO
        while o < SEQ:
            n = min(512, SEQ - o)
            tilesA.append(dict(kind="rest", xcol=None, ntok=n, src=("x", o), key=o, rope=o, q=False, w=0))
            o += n

        pending_mods = [(0, 3), (0, 4), (0, 5), (0, 6), (0, 7), (0, 8)] + [(1, v) for v in range(9)]

        C.P = Prog(nc, "p1")
        with contextlib.ExitStack() as ps:
            new_banks(C, ps, "1")
            alloc_regA(C)
            alloc_regC_proj(C)
            kscrB = Buf("kscr", group=True)
            vscrB = Buf("vscr", group=True)
            qscrB = Buf("qscr", group=True)
            xtmpB = Buf("xtmp")
            allxB = []
            for v in range(3):
                mod_vector(C, 0, v)
            early_casts()
            for ti, T in enumerate(tilesA):
                ntok = T["ntok"]
                if T["xcol"] is not None:
                    xap = x_sb[:, :, T["xcol"]:T["xcol"] + ntok]
                else:
                    xap = xtmp[:, :, :ntok]
                xB = Buf("x_%d" % ti) if T["xcol"] is not None else xtmpB
                allxB.append(xB)
                if T["src"][0] == "x":
                    src = I["xT"][:, T["src"][1]:T["src"][1] + ntok]
                    dma(C, "sp", xap, src.rearrange("(c p) t -> p c t", p=128), [], [xB], xB)
                else:
                    dma(C, "sp", xap[:, :, 0:HALO], I["xT"][:, OWN:OWN + HALO].rearrange("(c p) t -> p c t", p=128), [], [xB], xB)
                    dma(C, "sp", xap[:, :, HALO:HC], I["ctxT"].rearrange("(c p) t -> p c t", p=128), [], [xB], xB)
                w = T["w"]
                def mod_filler(cb=0):
                    if pending_mods:
                        l_, v_ = pending_mods.pop(0)
                        return mod_vector(C, l_, v_, defer=True, cb=cb)
                    return None
                def mod_filler2():
                    e1 = mod_filler(0)
                    e2 = mod_filler(16)
                    return lambda: (e1(), e2())
                norm_mod(C, xap, xB, ntok, 0, 0, 1, w, filler=(mod_filler if ti > 0 else mod_filler2))
                ffn(C, xap, xB, ntok, 0, 0, 2, w)
                norm_mod(C, xap, xB, ntok, 0, 3, 4, w, filler=mod_filler)
                specs = [(C.kout[:, c, :ntok], C.koutB) for c in range(8)]
                if T["q"]:
                    specs += [(C.qout[:, c, :ntok], C.qoutB) for c in range(8)]
                if T["rope"] == "hc":
                    rsrc = (I["cs_hc"][:, 0, :], I["cs_hc"][:, 1, :])
                else:
                    rsrc = (I["cos_t"][:, T["rope"]:T["rope"] + ntok], I["sin_t"][:, T["rope"]:T["rope"] + ntok])
                qk_chunks(C, ntok, specs, Wb["wqkA_h"], [castB["wqkA_0"]], rsrc,
                          lambda c: 1 if c < 8 else 0)
                v_proj(C, ntok, 4, Wb["wvA_h"], [castB["wvA_0"]],
                       lambda sub, pc: C.vout[:, sub, pc * 256:(pc + 1) * 256], C.voutB)
                if T["kind"] == "hc":
                    kparts = [(OWN, 0, HALO), (SEQ, HALO, HC)]
                else:
                    kparts = [(T["key"], 0, ntok)]
                for (k0, lo, hi) in kparts:
                    dma(C, "pool", kT_scr[:, :, k0:k0 + hi - lo].rearrange("h p t -> p h t"), C.kout[:, :, lo:hi], [C.koutB], [kscrB], kscrB)
                    dma(C, "pool", v_scr[k0:k0 + hi - lo, :].rearrange("(s p) f -> p s f", p=128), C.vout[:, lo // 128:hi // 128, :],
                        [C.voutB], [vscrB], vscrB)
                if T["q"]:
                    q0 = T["xcol"]
                    dma(C, "pool", qT_scr[:, :, q0:q0 + ntok].rearrange("h p t -> p h t"), C.qout[:, :, :ntok], [C.qoutB], [qscrB], qscrB)
            while pending_mods:
                mod_vector(C, *pending_mods.pop(0))
            if debug:
                dB = Buf("dbg")
                allx = [Buf("xall")]
                dma(C, "sp", xdbg, x_sb[:], allxB, [dB], dB)
                dma(C, "sp", moddbg, mod[:].rearrange("p l j w -> p (l j w)"), [C.modB], [dB], dB)
            C.P.emit(C)
        if stop_after == 1:
            finish(C)
            return nc

        C.P = Prog(nc, "p2a")
        with contextlib.ExitStack() as ps:
            new_banks(C, ps, "2a")
            off = 0
            Kh, Vh, Qh = [], [], []
            for i in range(2):
                t, off = carve(regA, off, [NKA], BF16); Kh.append(t)
                t, off = carve(regA, off, [NKA // 128, 128], BF16); Vh.append(t)
                t, off = carve(regA, off, [XR], BF16); Qh.append(t)
            pbuf = [[None, None], [None, None]]
            for c in range(2):
                for s in range(2):
                    pbuf[c][s], off = carve(regA, off, [512], BF16)
            ep = []
            for i in range(4):
                t, off = carve(regA, off, [512], F32); ep.append(t)
            eo = [xtmp[:, 0, :], xtmp[:, 1, :]]
            ers = [xtmp[:, 2, :], xtmp[:, 3, :]]
            sqs = [xtmp[:, 4, :].bitcast(BF16)[:, 0:512], xtmp[:, 5, :].bitcast(BF16)[:, 0:512]]
            assert off <= RA_BYTES, off
            attn_out, o2 = carve(regC, 0, [8, XR], BF16)
            assert o2 <= RC_BYTES
            KhB = [Buf("Kh0"), Buf("Kh1")]
            VhB = [Buf("Vh0"), Buf("Vh1")]
            QhB = [Buf("Qh0"), Buf("Qh1")]
            pB = [[Buf("p00"), Buf("p01")], [Buf("p10"), Buf("p11")]]
            epB = [Buf("ep%d" % i) for i in range(4)]
            eoB, ersB, sqsB = [Buf("eo0"), Buf("eo1")], [Buf("ers0"), Buf("ers1")], [Buf("sqs0"), Buf("sqs1")]
            aoB = Buf("attn_out")
            sbank = [[C.banks[0], C.banks[2]], [C.banks[1], C.banks[3]]]
            sbankB = [[C.bankB[0], C.bankB[2]], [C.bankB[1], C.bankB[3]]]
            Ob, ObB = [C.banks[4], C.banks[5]], [C.bankB[4], C.bankB[5]]
            Lb, LbB = [C.banks[6], C.banks[7]], [C.bankB[6], C.bankB[7]]

            allk = list(range(NKA // 128))
            units = []
            for hd in range(8):
                for i in range(4):
                    units.append((hd, 512 * i, 512, allk))
                units.append((hd, OWN, HALO, allk))
                units.append((hd, OWN + HALO, NCTX, [32, 33]))
            items = [(u, i) for u in range(len(units)) for i in range(len(units[u][3]))]
            T = len(items)
            loaded = set()

            def load_head(hd):
                hs = hd % 2
                dma(C, "sp", Kh[hs], kT_scr[hd], [], [KhB[hs]], KhB[hs])
                dma(C, "sp", Vh[hs], v_scr[:, hd * 128:(hd + 1) * 128].rearrange("(c p) e -> p c e", p=128), [], [VhB[hs]], VhB[hs])
                dma(C, "sp", Qh[hs], qT_scr[hd], [], [QhB[hs]], QhB[hs])
                if hd == 1:
                    C.cast_queue = []
                    deferred_casts()

            def qk(t):
                u, i = items[t]
                hd, qcol, nq, kch = units[u]
                hs = hd % 2
                if hd not in loaded:
                    loaded.add(hd)
                    load_head(hd)
                kc = kch[i]
                st_ = t % 2
                for c in range(2):
                    mm(C, sbank[c][st_][:, :nq], Kh[hs][c * 64:(c + 1) * 64, kc * 128:(kc + 1) * 128],
                       Qh[hs][c * 64:(c + 1) * 64, qcol:qcol + nq], True, True, [KhB[hs], QhB[hs]], [sbankB[c][st_]])

            def ex(t):
                u, i = items[t]
                hd, qcol, nq, kch = units[u]
                st_ = t % 2
                for c in range(2):
                    act(C, pbuf[c][st_][:, :nq], sbank[c][st_][:, :nq], AF.Exp, [sbankB[c][st_]], [pB[c][st_]], scale=0.125)

            def pv(t):
                u, i = items[t]
                hd, qcol, nq, kch = units[u]
                hs = hd % 2
                n = len(kch)
                kc = kch[i]
                st_ = t % 2
                for c in range(2):
                    mm(C, Ob[c][:, :nq], Vh[hs][:, kc, :], pbuf[c][st_][:, :nq], i == 0, i == n - 1, [VhB[hs], pB[c][st_]], [ObB[c]])
                    last_pe[0] = mm(C, Lb[c][:, :nq], ones_bf[:], pbuf[c][st_][:, :nq], i == 0, i == n - 1, [pB[c][st_]], [LbB[c]])
                return i == n - 1

            def epi_a(u):
                hd, qcol, nq, kch = units[u]
                k = u % 2
                cp(C, ep[0][:, :nq], Lb[0][:, :nq], [LbB[0]], [epB[0]])
                cp(C, ep[1][:, :nq], Lb[1][:, :nq], [LbB[1]], [epB[1]])
                cp(C, ep[2][:, :nq], Ob[0][:, :nq], [ObB[0]], [epB[2]])
                cp(C, ep[3][:, :nq], Ob[1][:, :nq], [ObB[1]], [epB[3]])
                C.P.add("dve", lambda e: e.reciprocal(out=ep[0][:, :nq], in_=ep[0][:, :nq]), [epB[0]], [epB[0]])
                C.P.add("dve", lambda e: e.reciprocal(out=ep[1][:, :nq], in_=ep[1][:, :nq]), [epB[1]], [epB[1]])
                tt(C, ep[2][:, :nq], ep[2][:, :nq], ep[0][:, :nq], ALU.mult, [epB[2], epB[0]], [epB[2]])
                tt(C, ep[3][:, :nq], ep[3][:, :nq], ep[1][:, :nq], ALU.mult, [epB[3], epB[1]], [epB[3]])
                stt(C, eo[k][:, :nq], ep[3][:, :nq], neglam[:, 0:1], ep[2][:, :nq], ALU.mult, ALU.add, [epB[3], epB[2]], [eoB[k]])
                tt(C, sqs[k][:, :nq], eo[k][:, :nq], eo[k][:, :nq], ALU.mult, [eoB[k]], [sqsB[k]])

            def epi_b(u, bank, bankB_):
                hd, qcol, nq, kch = units[u]
                k = u % 2
                mm(C, bank[:, :nq], ones_bf[:], sqs[k][:, :nq], True, True, [sqsB[k]], [bankB_])
                act(C, ers[k][:, :nq], bank[:, :nq], AF.Ln, [bankB_], [ersB[k]], bias=EPS, scale=1.0 / 128)
                act(C, ers[k][:, :nq], ers[k][:, :nq], AF.Exp, [ersB[k]], [ersB[k]], scale=-0.5)
                stt(C, attn_out[:, hd, qcol:qcol + nq], eo[k][:, :nq], gain2[:, 0:1], ers[k][:, :nq], ALU.mult, ALU.mult,
                    [eoB[k], ersB[k]], [aoB])

            DELAY = 12
            pending = []
            last_pe = [None]
            for s_ in range(T + 2):
                if s_ - 2 >= 0:
                    if pv(s_ - 2):
                        u = items[s_ - 2][0]
                        epi_a(u)
                        pending.append((s_ + DELAY, u))
                        if C.cast_queue:
                            C.cast_queue.pop(0)(after=[last_pe[0]])
                if 0 <= s_ - 1 < T:
                    ex(s_ - 1)
                if s_ < T:
                    qk(s_)
                while pending and pending[0][0] <= s_:
                    _, u = pending.pop(0)
                    st_ = (s_ + 1) % 2
                    epi_b(u, sbank[0][st_], sbankB[0][st_])
            while pending:
                _, u = pending.pop(0)
                epi_b(u, sbank[0][0], sbankB[0][0])
            while C.cast_queue:
                C.cast_queue.pop(0)()
            C.cast_queue = None
            if debug:
                dB = Buf("dbg")
                dma(C, "sp", aodbg, attn_out, [aoB], [dB], dB)
            C.P.emit(C)
        if stop_after == 2:
            finish(C)
            return nc

        def out_proj(C, xap, xB, ntok, col, ao, aoB, wkey, wB, l, w):
            for dc in range(8):
                slot, sB = C.stream.load(Wb[wkey][dc * 128:(dc + 1) * 128, :], wB, view=lambda t: t[:, 0:1024])
                wv = slot[:, 0:1024].rearrange("p (h f) -> p h f", h=8)
                by, byB = C.banks[4 + dc % 2], C.bankB[4 + dc % 2]
                for hd in range(8):
                    mm(C, by[:, :ntok], wv[:, hd, :], ao[:, hd, col:col + ntok], hd == 0, hd == 7, [sB, aoB], [byB])
                for (lo, hi, ww) in segs_of(w, ntok):
                    stt(C, xap[:, dc, lo:hi], by[:, lo:hi], modap(l, 5, dc, ww), xap[:, dc, lo:hi], ALU.mult, ALU.add, [byB, xB, C.modB], [xB])

        tilesB = [dict(xcol=512 * i, ntok=512, w=0) for i in range(4)] + [dict(xcol=OWN, ntok=HC, w=HCSEG)]

        C.P = Prog(nc, "p2b")
        with contextlib.ExitStack() as ps:
            new_banks(C, ps, "2b")
            alloc_regA(C)
            C.lnb = [xtmp[:, 0, :]]
            C.rsb = C.lnb
            C.tmpf = [xtmp[:, 2, :], xtmp[:, 3, :]]
            C.lnB = [Buf("ln")]; C.rsB = C.lnB
            C.tmpfB = [Buf("tmpf0"), Buf("tmpf1")]
            attn_out, _ = carve(regC, 0, [8, XR], BF16)
            aoB = Buf("attn_out")
            allxB = []
            hb = two_h(C)
            tl = []
            for ti, T in enumerate(tilesB):
                ntok, col, w = T["ntok"], T["xcol"], T["w"]
                xB = Buf("x_%d" % ti)
                allxB.append(xB)
                tl.append((x_sb[:, :, col:col + ntok], xB, ntok, col, w))

            def proj2b(ti):
                xap, xB, ntok, col, w = tl[ti]
                out_proj(C, xap, xB, ntok, col, attn_out, aoB, "woA_h", [castB["woA_0"]], 0, w)

            def norm2b(ti):
                xap, xB, ntok, col, w = tl[ti]
                C.h, C.hB = hb[ti % 2]
                norm_mod(C, xap, xB, ntok, 0, 6, 7, w)
            proj2b(0)
            norm2b(0)
            for ti in range(len(tl)):
                xap, xB, ntok, col, w = tl[ti]
                C.h, C.hB = hb[ti % 2]
                ffn_a(C, ntok, 1)
                nxt = ti + 1 < len(tl)
                if nxt:
                    proj2b(ti + 1)
                ffn_b(C, xap, xB, ntok, 1, 0, 8, w, mid=((lambda ti=ti: norm2b(ti + 1)) if nxt else None))
            if debug:
                dB = Buf("dbg")
                dma(C, "sp", xdbg, x_sb[:], allxB, [dB], dB)
            C.P.emit(C)
        if stop_after == 3:
            finish(C)
            return nc

        C.P = Prog(nc, "p3")
        with contextlib.ExitStack() as ps:
            new_banks(C, ps, "3")
            alloc_regA(C)
            alloc_regC_proj(C)
            kscrB = Buf("kBscr", group=True)
            vscrB = Buf("vBscr", group=True)
            qscrB = Buf("qBscr", group=True)
            allxB = [Buf("x_%d" % ti) for ti in range(len(tilesB))]
            hb = two_h(C)

            def norm3a(ti):
                T = tilesB[ti]
                C.h, C.hB = hb[ti % 2]
                norm_mod(C, x_sb[:, :, T["xcol"]:T["xcol"] + T["ntok"]], allxB[ti], T["ntok"], 1, 0, 1, T["w"])
            norm3a(0)
            for ti, T in enumerate(tilesB):
                ntok, col, w = T["ntok"], T["xcol"], T["w"]
                xap = x_sb[:, :, col:col + ntok]
                xB = allxB[ti]
                C.h, C.hB = hb[ti % 2]
                ffn_a(C, ntok, 2)
                nxt = ti + 1 < len(tilesB)
                ffn_b(C, xap, xB, ntok, 2, 1, 2, w, mid=((lambda ti=ti: norm3a(ti + 1)) if nxt else None))
                C.h, C.hB = hb[ti % 2]
                norm_mod(C, xap, xB, ntok, 1, 3, 4, w)
                isq = col < OWN
                specs = [(C.kout[:, c, :ntok], C.koutB) for c in range(4)]
                if isq:
                    specs += [(C.qout[:, c, :ntok], C.qoutB) for c in range(8)]
                if isq:
                    rsrc = (I["cos_t"][:, col:col + ntok], I["sin_t"][:, col:col + ntok])
                else:
                    rsrc = (I["cs_hc"][:, 0, :], I["cs_hc"][:, 1, :])
                qk_chunks(C, ntok, specs, Wb["wqkB_h"], [castB["wqkB_0"]], rsrc,
                          lambda c: 3 if c < 4 else 2)
                v_proj(C, ntok, 1, Wb["wvB_h"], [castB["wvB_0"]], lambda sub, pc: C.vout[:, sub, 0:256], C.voutB)
                dma(C, "pool", kB_scr[:, :, col:col + ntok].rearrange("h p t -> p h t"), C.kout[:, 0:4, :ntok], [C.koutB], [kscrB], kscrB)
                dma(C, "pool", vB_scr[col:col + ntok, :].rearrange("(s p) f -> p s f", p=128), C.vout[:, :ntok // 128, 0:256], [C.voutB], [vscrB], vscrB)
                if isq:
                    dma(C, "pool", qB_scr[:, :, col:col + ntok].rearrange("h p t -> p h t"), C.qout[:, :, :ntok], [C.qoutB], [qscrB], qscrB)
            if debug:
                dB = Buf("dbg")
                dma(C, "sp", xdbg, x_sb[:], allxB, [dB], dB)
                dma(C, "sp", moddbg, mod[:].rearrange("p l j w -> p (l j w)"), [], [dB], dB)
            C.P.emit(C)
        if stop_after == 4:
            finish(C)
            return nc

        C.P = Prog(nc, "p4a")
        with contextlib.ExitStack() as ps:
            new_banks(C, ps, "4a")
            off = 0
            KB, VB, QB = [], [], []
            NKB = XR // 128
            for i in range(2):
                t, off = carve(regA, off, [XR], BF16); KB.append(t)
                t, off = carve(regA, off, [NKB, 64], BF16); VB.append(t)
                t, off = carve(regA, off, [2, OWN], BF16); QB.append(t)
            pT = []
            for i in range(3):
                t, off = carve(regA, off, [512], BF16); pT.append(t)
            es, off = carve(regA, off, [4, 256], F32)
            lbuf, rbuf = [], []
            for i in range(2):
                t, off = carve(regA, off, [256], F32); lbuf.append(t)
                t, off = carve(regA, off, [256], F32); rbuf.append(t)
            assert off <= RA_BYTES, off
            attn_out, _ = carve(regC, 0, [8, XR], BF16)
            aoB = Buf("attn_out")
            KBB, VBB, QBB = [Buf("KB0"), Buf("KB1")], [Buf("VB0"), Buf("VB1")], [Buf("QB0"), Buf("QB1")]
            pTB = [[Buf("pT%d0" % i), Buf("pT%d1" % i)] for i in range(3)]
            esB = Buf("es")
            lB, rB = [Buf("l0"), Buf("l1")], [Buf("r0"), Buf("r1")]
            se = [C.banks[0], C.banks[1]]; seB = [C.bankB[0], C.bankB[1]]
            so = [C.banks[2], C.banks[3]]; soB = [C.bankB[2], C.bankB[3]]
            OB = [C.banks[4], C.banks[5]]; OBB = [C.bankB[4], C.bankB[5]]
            LB = [C.banks[6], C.banks[7]]; LBB = [C.bankB[6], C.bankB[7]]
            for kvh in range(4):
                for j in range(2):
                    for r in range(2):
                        g = 2 * j + r
                        cp(C, es[r * 64:(r + 1) * 64, kvh, j * 128:(j + 1) * 128],
                           esink[r * 64:(r + 1) * 64, kvh * 4 + g:kvh * 4 + g + 1].broadcast_to([64, 128]), [], [esB])
            unitsB = [(kvh, n) for kvh in range(4) for n in range(16)]

            def klist_of(n):
                return [((n - 1) if n > 0 else 16, 0 if n > 0 else 2), (n, None), ((n + 1) if n < 15 else 16, 1 if n < 15 else 3),
                        (17, None), (18, None)]
            itemsB = [(u, i) for u in range(len(unitsB)) for i in range(5)]
            TB = len(itemsB)
            loadedB = set()

            def load_kvh(kvh):
                hs = kvh % 2
                dma(C, "sp", KB[hs], kB_scr[kvh], [], [KBB[hs]], KBB[hs])
                dma(C, "sp", VB[hs], vB_scr[:, kvh * 64:(kvh + 1) * 64].rearrange("(c p) e -> p c e", p=128), [], [VBB[hs]], VBB[hs])
                dma(C, "sp", QB[hs], qB_scr[2 * kvh:2 * kvh + 2].rearrange("j p t -> p j t"), [], [QBB[hs]], QBB[hs])

            def qkB(t):
                u, i = itemsB[t]
                kvh, n = unitsB[u]
                hs = kvh % 2
                if kvh not in loadedB:
                    loadedB.add(kvh)
                    load_kvh(kvh)
                kc = klist_of(n)[i][0]
                st_ = t % 2
                for j in range(2):
                    mm(C, se[st_][:, j * 128:(j + 1) * 128], KB[hs][0:64, kc * 128:(kc + 1) * 128],
                       QB[hs][0:64, j, n * 128:(n + 1) * 128], True, True, [KBB[hs], QBB[hs]], [seB[st_]])
                    mm(C, so[st_][:, j * 128:(j + 1) * 128], KB[hs][64:128, kc * 128:(kc + 1) * 128],
                       QB[hs][64:128, j, n * 128:(n + 1) * 128], True, True, [KBB[hs], QBB[hs]], [soB[st_]])

            def exB(t):
                u, i = itemsB[t]
                st_ = t % 2
                pt_ = t % 3
                act(C, pT[pt_][:, 0:256], se[st_][:, 0:256], AF.Exp, [seB[st_]], [pTB[pt_][0]], scale=0.125)
                act(C, pT[pt_][:, 256:512], so[st_][:, 0:256], AF.Exp, [soB[st_]], [pTB[pt_][1]], scale=0.125)

            def maskB(t):
                u, i = itemsB[t]
                kvh, n = unitsB[u]
                pt_ = t % 3
                m = klist_of(n)[i][1]
                if m is not None:
                    tt(C, pT[pt_].rearrange("p (a b) -> p a b", a=4), pT[pt_].rearrange("p (a b) -> p a b", a=4),
                       masks_bf[:, m, :].unsqueeze(1).broadcast_to([128, 4, 128]), ALU.mult, list(pTB[pt_]), list(pTB[pt_]), eng="pool")

            def pvB(t):
                u, i = itemsB[t]
                kvh, n = unitsB[u]
                hs = kvh % 2
                ob = u % 2
                kc = klist_of(n)[i][0]
                pt_ = t % 3
                for r in range(2):
                    mm(C, OB[ob][r * 64:(r + 1) * 64, 0:256], VB[hs][:, kc, :], pT[pt_][:, r * 256:(r + 1) * 256],
                       i == 0, i == 4, [VBB[hs], pTB[pt_][r]], [OBB[ob]])
                    mm(C, LB[ob][r * 64:(r + 1) * 64, 0:256], ones_bf[:, 0:64], pT[pt_][:, r * 256:(r + 1) * 256],
                       i == 0, i == 4, [pTB[pt_][r]], [LBB[ob]])
                return i == 4

            def epiB(u):
                kvh, n = unitsB[u]
                ob = u % 2
                tt(C, lbuf[ob], LB[ob][:, 0:256], es[:, kvh, :], ALU.add, [LBB[ob], esB], [lB[ob]])
                C.P.add("dve", lambda e: e.reciprocal(out=rbuf[ob], in_=lbuf[ob]), [lB[ob]], [rB[ob]])
                tt(C, attn_out[:, 2 * kvh:2 * kvh + 2, n * 128:(n + 1) * 128], OB[ob][:, 0:256].rearrange("p (j q) -> p j q", j=2),
                   rbuf[ob].rearrange("p (j q) -> p j q", j=2), ALU.mult, [OBB[ob], rB[ob]], [aoB])

            for s_ in range(TB + 3):
                if 0 <= s_ - 3 < TB:
                    if pvB(s_ - 3):
                        epiB(itemsB[s_ - 3][0])
                if 0 <= s_ - 2 < TB:
                    maskB(s_ - 2)
                if 0 <= s_ - 1 < TB:
                    exB(s_ - 1)
                if s_ < TB:
                    qkB(s_)
            C.P.emit(C)
        if stop_after == 5:
            finish(C)
            return nc

        C.P = Prog(nc, "p4b")
        with contextlib.ExitStack() as ps:
            new_banks(C, ps, "4b")
            alloc_regA(C)
            C.lnb = [xtmp[:, 0, :]]
            C.rsb = C.lnb
            C.tmpf = [xtmp[:, 2, :], xtmp[:, 3, :]]
            C.lnB = [Buf("ln")]; C.rsB = C.lnB
            C.tmpfB = [Buf("tmpf0"), Buf("tmpf1")]
            attn_out, _ = carve(regC, 0, [8, XR], BF16)
            aoB = Buf("attn_out")
            yB = Buf("y", group=True)
            hb = two_h(C)
            tl = []
            for ti in range(4):
                col, ntok = 512 * ti, 512
                tl.append((x_sb[:, :, col:col + ntok], Buf("x_%d" % ti), ntok, col, 0))

            def proj4b(ti):
                xap, xB, ntok, col, w = tl[ti]
                out_proj(C, xap, xB, ntok, col, attn_out, aoB, "woB_h", [castB["woB_0"]], 1, 0)

            def norm4b(ti):
                xap, xB, ntok, col, w = tl[ti]
                C.h, C.hB = hb[ti % 2]
                norm_mod(C, xap, xB, ntok, 1, 6, 7, 0)
            proj4b(0)
            norm4b(0)
            for ti in range(4):
                xap, xB, ntok, col, w = tl[ti]
                C.h, C.hB = hb[ti % 2]
                ffn_a(C, ntok, 3)
                nxt = ti + 1 < 4
                if nxt:
                    proj4b(ti + 1)
                ffn_b(C, xap, xB, ntok, 3, 1, 8, 0, mid=((lambda ti=ti: norm4b(ti + 1)) if nxt else None))
                dma(C, "sp", yT[:, col:col + ntok].rearrange("(c p) t -> p c t", p=128), xap, [xB], [yB], yB)
            C.P.emit(C)
        finish(C)
    return nc


_CACHE = {}


def kernel(**inputs):
    shared = prep_shared(inputs)
    in_maps = []
    for core in range(8):
        b, h = core // 2, core % 2
        m = dict(shared)
        m.update(prep_core(inputs, b, h))
        in_maps.append(m)
    if "nc" not in _CACHE:
        _CACHE["nc"] = build()
    nc = _CACHE["nc"]
    res = run_bass_kernel_spmd(nc, in_maps, core_ids=list(range(8)))
    out = np.empty((4, SEQ, D), np.float32)
    for core in range(8):
        b, h = core // 2, core % 2
        out[b, h * OWN:(h + 1) * OWN, :] = res.results[core]["yT"].T
    return out
```

```python
import contextlib
import math
import numpy as np
import ml_dtypes
import concourse.bass as bass
import concourse.mybir as mybir
from concourse.bass_utils import run_bass_kernel_spmd

F32 = mybir.dt.float32
BF16 = mybir.dt.bfloat16
AF = mybir.ActivationFunctionType
ALU = mybir.AluOpType
AX = mybir.AxisListType

D = 1024
FF = 2816
NFC = 22
SEQ = 4096
NCTX = 256
OWN = 2048
HALO = 128
XR = OWN + HALO + NCTX
NKA = SEQ + NCTX
EPS = 1e-6
LAM_INIT0 = 0.8 - 0.6 * math.exp(-0.3 * 0)

ENGS = ("pe", "act", "dve", "pool", "sp")


class Buf:
    __slots__ = ("name", "writers", "readers", "sem", "dma_count", "group", "persist")

    def __init__(self, name, group=False, persist=False):
        self.name = name
        self.writers = []
        self.readers = {}
        self.sem = None
        self.dma_count = 0
        self.group = group
        self.persist = persist


class Op:
    __slots__ = ("eng", "fn", "deps", "is_dma", "signal", "semval", "buf", "prog")

    def __init__(self, eng, fn, is_dma, prog):
        self.eng = eng
        self.fn = fn
        self.deps = []
        self.is_dma = is_dma
        self.signal = False
        self.semval = None
        self.buf = None
        self.prog = prog


class Prog:
    def __init__(self, nc, name):
        self.nc = nc
        self.name = name
        self.ops = {e: [] for e in ENGS}
        self.all_ops = []

    def _key(self, op):
        return ("dma", id(op)) if op.is_dma else op.eng

    def _live(self, d):
        return d.prog is self or (d.is_dma and d.buf.persist)

    def add(self, eng, fn, reads=(), writes=(), dma_dst=None, after=()):
        op = Op(eng, fn, dma_dst is not None, self)
        op.buf = dma_dst
        deps = list(after)
        for b in reads:
            for w in b.writers:
                if w.is_dma or op.is_dma or w.eng != eng or eng != "pe":
                    deps.append(w)
        for b in writes:
            for r in b.readers.values():
                if r.is_dma or op.is_dma or r.eng != eng or eng != "pe":
                    deps.append(r)
            if not b.group:
                for w in b.writers:
                    if w.is_dma or op.is_dma or w.eng != eng or eng != "pe":
                        deps.append(w)
        seen = set()
        for d in deps:
            if id(d) not in seen and d is not op and self._live(d):
                seen.add(id(d))
                op.deps.append(d)
                d.signal = True
        for b in writes:
            if b.group:
                b.writers.append(op)
            else:
                b.writers = [op]
                b.readers = {}
        for b in reads:
            b.readers[self._key(op)] = op
        if op.is_dma:
            dma_dst.dma_count += 1
            op.semval = 16 * dma_dst.dma_count
        self.ops[eng].append(op)
        self.all_ops.append(op)
        return op

    def emit(self, C):
        nc = self.nc
        for e in ENGS:
            comp = [o for o in self.ops[e] if not o.is_dma]
            if comp:
                comp[-1].signal = True
            c = 0
            for op in comp:
                if op.signal:
                    c += 1
                    op.semval = c
        dma_bufs = []
        for op in self.all_ops:
            if op.is_dma and (not op.buf.persist) and op.buf not in dma_bufs:
                dma_bufs.append(op.buf)
        k = C.phase_idx
        cur, other = C.semsets[k % 2], C.semsets[(k + 1) % 2]
        assert len(dma_bufs) <= len(cur["dma"]), (self.name, len(dma_bufs))
        esem = cur["eng"]
        bar = C.bar
        bar_target = len(ENGS) * (k + 1)
        for i, b in enumerate(dma_bufs):
            b.sem = cur["dma"][i]
        with nc.Block() as block:

            def run(eng_name):
                def body(eng):
                    if eng_name == "pool" and k >= 1:
                        for sname in ENGS:
                            eng.sem_clear(other["eng"][sname])
                        for sm in other["dma"]:
                            eng.sem_clear(sm)
                    waited = {}
                    last_comp = None
                    my_dma = {}
                    for op in self.ops[eng_name]:
                        need = {}
                        for d in op.deps:
                            sem = d.buf.sem if d.is_dma else esem[d.eng]
                            kk = id(sem)
                            if kk not in need or need[kk][1] < d.semval:
                                need[kk] = (sem, d.semval)
                        for kk, (sem, v) in need.items():
                            if waited.get(kk, 0) < v:
                                eng.wait_ge(sem, v)
                                waited[kk] = v
                        ins = op.fn(eng)
                        if op.is_dma:
                            ins.then_inc(op.buf.sem, 16)
                            if not op.buf.persist:
                                my_dma[id(op.buf)] = (op.buf, op.semval)
                        else:
                            last_comp = op
                            if op.signal:
                                ins.then_inc(esem[eng_name], 1)
                    for b, v in my_dma.values():
                        eng.wait_ge(b.sem, v)
                    if last_comp is not None:
                        eng.wait_ge(esem[eng_name], last_comp.semval)
                    eng.sem_inc(bar, 1)
                    eng.wait_ge(bar, bar_target)
                return body

            block.tensor(run("pe"))
            block.scalar(run("act"))
            block.vector(run("dve"))
            block.gpsimd(run("pool"))
            block.sync(run("sp"))
        C.phase_idx += 1
        for b in dma_bufs:
            b.dma_count = 0
            b.sem = None
            b.writers = []
            b.readers = {}


class Ctx:
    pass


def mm(C, out, lhsT, rhs, start, stop, reads, writes, **kw):
    return C.P.add("pe", lambda e: e.matmul(out, lhsT, rhs, start=start, stop=stop, **kw), reads, writes)


def act(C, out, in_, func, reads, writes, **kw):
    C.P.add("act", lambda e: e.activation(out=out, in_=in_, func=func, **kw), reads, writes)


def stt(C, out, in0, scalar, in1, op0, op1, reads, writes, eng="dve"):
    C.P.add(eng, lambda e: e.scalar_tensor_tensor(out=out, in0=in0, scalar=scalar, in1=in1, op0=op0, op1=op1), reads, writes)


def tt(C, out, in0, in1, op, reads, writes, eng="dve"):
    C.P.add(eng, lambda e: e.tensor_tensor(out=out, in0=in0, in1=in1, op=op), reads, writes)


def ts(C, out, in0, s1, s2, op0, op1, reads, writes, eng="dve"):
    if s2 is None:
        C.P.add(eng, lambda e: e.tensor_scalar(out=out, in0=in0, scalar1=s1, scalar2=None, op0=op0), reads, writes)
    else:
        C.P.add(eng, lambda e: e.tensor_scalar(out=out, in0=in0, scalar1=s1, scalar2=s2, op0=op0, op1=op1), reads, writes)


def cp(C, out, in_, reads, writes, eng="dve"):
    C.P.add(eng, lambda e: e.tensor_copy(out=out, in_=in_), reads, writes)


def dma(C, eng, out, in_, reads, writes, dst, after=()):
    return C.P.add(eng, lambda e: e.dma_start(out=out, in_=in_), reads, writes, dma_dst=dst, after=after)


def pipeline(stages, n):
    ns = len(stages)
    for s in range(n + ns - 1):
        for k in reversed(range(ns)):
            c = s - k
            if 0 <= c < n:
                stages[k](c)


class Stream:
    def __init__(self, C, tens, name):
        self.C = C
        self.tens = tens
        self.bufs = [Buf("%s%d" % (name, i)) for i in range(len(tens))]
        self.i = 0

    def load(self, src, src_bufs, view=None, eng="sp"):
        s = self.i % len(self.tens)
        self.i += 1
        t = self.tens[s]
        dst = view(t) if view is not None else t
        dma(self.C, eng, dst, src, list(src_bufs), [self.bufs[s]], self.bufs[s])
        return t, self.bufs[s]


def _perm(h):
    if h == 0:
        own = np.arange(0, 2048)
        halo = np.arange(2048, 2176)
        rest = np.arange(2176, 4096)
    else:
        own = np.arange(2048, 4096)
        halo = np.arange(1920, 2048)
        rest = np.arange(0, 1920)
    return np.concatenate([own, halo, rest])


def _rope_tables(pos):
    nf = 16
    inv = (10000.0 ** (-np.arange(nf, dtype=np.float32) / nf)).astype(np.float32)
    row = (pos // 64).astype(np.float32)
    col = (pos % 64).astype(np.float32)
    ang = np.zeros((64, pos.shape[0]), np.float32)
    for d in range(64):
        a = d // 32
        f = d % 16
        ang[d] = (row if a == 0 else col) * inv[f]
    ang = np.concatenate([ang, ang], 0)
    return np.cos(ang).astype(np.float32), np.sin(ang).astype(np.float32)


def _chunk_w(W, cols):
    return np.ascontiguousarray(W[:, cols].reshape(8, 128, len(cols)).transpose(1, 0, 2))


def prep_shared(inp):
    f = np.float32
    out = {}
    ada_w = np.asarray(inp["ada_w"], f)
    out["ada_h"] = np.ascontiguousarray(
        ada_w.reshape(2, 8, 128, 72, 128).transpose(0, 3, 2, 1, 4)).reshape(144, 128, 1024)
    ada_b = np.asarray(inp["ada_b"], f)
    out["ada_bT"] = np.ascontiguousarray(ada_b.reshape(2, 72, 128).transpose(2, 0, 1)).reshape(128, 144)
    wi_all, wo_all = [], []
    for l in range(2):
        for (wi_name, wo_name) in (("ffn_pre_wi", "ffn_pre_wo"), ("ffn_post_wi", "ffn_post_wo")):
            wi = np.asarray(inp[wi_name][l], f)
            wo = np.asarray(inp[wo_name][l], f)
            g = wi[:, :FF].reshape(8, 128, NFC, 128)
            u = wi[:, FF:].reshape(8, 128, NFC, 128)
            gu = np.stack([g, u], axis=3)
            wi_all.append(np.ascontiguousarray(gu.transpose(2, 1, 0, 3, 4)).reshape(NFC * 128, 2048))
            wo_all.append(np.ascontiguousarray(wo.reshape(NFC, 128, 8, 128).transpose(2, 1, 0, 3)).reshape(8 * 128, FF))
    out["wi_h"] = np.stack(wi_all)
    out["wo_h"] = np.stack(wo_all)
    wa = np.asarray(inp["a_w_qkv"][0], f)
    chunks = [1024 + h * 128 for h in range(8)] + [h * 128 for h in range(8)]
    qk = [_chunk_w(wa, np.arange(c0, c0 + 128)) for c0 in chunks]
    out["wqkA_h"] = np.ascontiguousarray(
        np.stack([np.stack([qk[2 * p], qk[2 * p + 1]], axis=1) for p in range(8)])).reshape(8 * 128, 2048)
    out["wvA_h"] = np.ascontiguousarray(
        np.stack([_chunk_w(wa, np.arange(2048 + pc * 256, 2048 + (pc + 1) * 256)) for pc in range(4)])).reshape(4 * 128, 2048)
    woa = np.asarray(inp["a_w_o"][0], f)
    out["woA_h"] = np.ascontiguousarray(woa.reshape(8, 128, 8, 128).transpose(2, 1, 0, 3)).reshape(8 * 128, 1024)
    wb = np.asarray(inp["b_w_qkv"][0], f)
    colsB = []
    for kvh in range(4):
        c = np.arange(1024 + kvh * 64, 1024 + (kvh + 1) * 64)
        colsB.append(np.concatenate([c, c]))
    for j in range(8):
        colsB.append(np.arange(j * 128, (j + 1) * 128))
    qkb = [_chunk_w(wb, c) for c in colsB]
    out["wqkB_h"] = np.ascontiguousarray(
        np.stack([np.stack([qkb[2 * p], qkb[2 * p + 1]], axis=1) for p in range(6)])).reshape(6 * 128, 2048)
    out["wvB_h"] = _chunk_w(wb, np.arange(1280, 1536)).reshape(128, 2048)
    wob = np.asarray(inp["b_w_o"][0], f)
    out["woB_h"] = np.ascontiguousarray(wob.reshape(8, 128, 8, 128).transpose(2, 1, 0, 3)).reshape(8 * 128, 1024)
    sp = np.zeros((128, 32), f)
    sp[:, 0] = np.tile(np.asarray(inp["a_q_gain"][0], f), 2)
    sp[:, 1] = np.tile(np.asarray(inp["a_k_gain"][0], f), 2)
    sp[:, 2] = np.tile(np.asarray(inp["b_q_gain"][0], f), 2)
    sp[:, 3] = np.tile(np.asarray(inp["b_k_gain"][0], f), 2)
    sp[:, 4] = np.asarray(inp["a_subln_gain"][0], f)
    sp[:, 5:21] = np.asarray(inp["b_sink"][0], f)[None, :]
    out["smallp"] = sp
    lam = np.asarray(inp["a_lambda"][0], f)
    out["lamv"] = np.concatenate([lam[0], lam[2], lam[1], lam[3]])[None, :].astype(f)
    rm = np.zeros((128, 128), f)
    for m in range(128):
        if (m % 32) < 16:
            rm[m + 16, m] = -1.0
        else:
            rm[m - 16, m] = 1.0
    out["rmat"] = rm
    return out


def prep_core(inp, b, h):
    f = np.float32
    out = {}
    perm = _perm(h)
    x = np.asarray(inp["x"], f)
    out["xT"] = np.ascontiguousarray(x[b][perm].T)
    out["ctxT"] = np.ascontiguousarray(np.asarray(inp["ctx"], f)[b].T)
    cv = np.stack([np.asarray(inp["c"], f)[b], np.asarray(inp["c_ctx"], f)], axis=1)
    out["cvec"] = np.ascontiguousarray(cv.reshape(8, 128, 2).transpose(1, 0, 2)).reshape(128, 16)
    cos, sin = _rope_tables(perm)
    out["cos_t"] = cos
    out["sin_t"] = sin
    cs_hc = np.zeros((128, 2, HALO + NCTX), f)
    cs_hc[:, 0, :HALO] = cos[:, OWN:OWN + HALO]
    cs_hc[:, 0, HALO:] = 1.0
    cs_hc[:, 1, :HALO] = sin[:, OWN:OWN + HALO]
    out["cs_hc"] = cs_hc
    jj = np.arange(128)[:, None]
    ii = np.arange(128)[None, :]
    mprev = (jj >= ii).astype(f)
    mnext = (jj <= ii).astype(f)
    out["masks"] = np.ascontiguousarray(
        np.stack([mprev, mnext, mprev * (1.0 if h == 1 else 0.0), mnext * (1.0 if h == 0 else 0.0)], axis=1)).reshape(128, 512)
    return out


SHARED_SHAPES = {
    "ada_h": [144, 128, 1024], "ada_bT": [128, 144], "wi_h": [4, NFC * 128, 2048], "wo_h": [4, 1024, FF],
    "wqkA_h": [1024, 2048], "wvA_h": [512, 2048], "woA_h": [1024, 1024], "wqkB_h": [768, 2048],
    "wvB_h": [128, 2048], "woB_h": [1024, 1024], "smallp": [128, 32], "lamv": [1, 256], "rmat": [128, 128],
}
CORE_SHAPES = {
    "xT": [D, SEQ], "ctxT": [D, NCTX], "cs_hc": [128, 2, HALO + NCTX], "cvec": [128, 16], "cos_t": [128, SEQ], "sin_t": [128, SEQ], "masks": [128, 512],
}


def finish(C):
    nc = C.nc
    nc.all_engine_barrier()
    nc.clear_and_free_semaphores(C.all_sems)
    nc.all_engine_barrier()


def build(stop_after=None, debug=False):
    nc = bass.Bass("TRN2", target_bir_lowering=False)
    C = Ctx()
    C.nc = nc
    I = {}
    for k, shp in list(SHARED_SHAPES.items()) + list(CORE_SHAPES.items()):
        I[k] = nc.dram_tensor(k, shp, F32, kind="ExternalInput").ap()
    yT = nc.dram_tensor("yT", [D, OWN], F32, kind="ExternalOutput").ap()
    dk = "ExternalOutput" if debug else "Internal"
    Wb = {}
    for k in ("wi_h", "wo_h", "wqkA_h", "wvA_h", "woA_h", "wqkB_h", "wvB_h", "woB_h"):
        Wb[k] = nc.dram_tensor(k + "_b", SHARED_SHAPES[k], BF16).ap()
    ada_b16 = nc.dram_tensor("ada_b16", [144, 128, 1024], BF16).ap()
    kT_scr = nc.dram_tensor("kT_scr", [8, 128, NKA], BF16, kind=dk).ap()
    v_scr = nc.dram_tensor("v_scr", [NKA, D], BF16, kind=dk).ap()
    qT_scr = nc.dram_tensor("qT_scr", [8, 128, XR], BF16, kind=dk).ap()
    kB_scr = nc.dram_tensor("kB_scr", [4, 128, XR], BF16, kind=dk).ap()
    vB_scr = nc.dram_tensor("vB_scr", [XR, 256], BF16, kind=dk).ap()
    qB_scr = nc.dram_tensor("qB_scr", [8, 128, OWN], BF16, kind=dk).ap()
    if debug:
        xdbg = nc.dram_tensor("xdbg", [128, 8, XR], F32, kind="ExternalOutput").ap()
        moddbg = nc.dram_tensor("moddbg", [128, 288], F32, kind="ExternalOutput").ap()
        aodbg = nc.dram_tensor("aodbg", [128, 8, XR], BF16, kind="ExternalOutput").ap()

    with contextlib.ExitStack() as gs:
        sb = lambda name, shape, dt: gs.enter_context(nc.sbuf_tensor(name, shape, dt))
        x_sb = sb("x_sb", [128, 8, XR], F32)
        mod = sb("mod", [128, 2, 72, 2], F32)
        adab = sb("adab", [128, 2, 72], F32)
        smallp = sb("smallp_s", [128, 32], F32)
        csil = sb("csil", [128, 8, 2], F32)
        csil_bf = sb("csil_bf", [128, 8, 2], BF16)
        ones_bf = sb("ones_bf", [128, 128], BF16)
        bd_bf = sb("bd_bf", [128, 128], BF16)
        rm_bf = sb("rm_bf", [128, 128], BF16)
        ones_f = sb("ones_f", [128, 128], F32)
        neglam = sb("neglam", [128, 2], F32)
        gain2 = sb("gain2", [128, 1], F32)
        esink = sb("esink", [128, 16], F32)
        masks_bf = sb("masks_bf", [128, 4, 128], BF16)
        RA_BYTES = 65 * 1024
        RC_BYTES = 46 * 1024
        regA = sb("regA", [128, RA_BYTES // 4], F32)
        regC = sb("regC", [128, RC_BYTES // 4], F32)
        xtmp = sb("xtmp", [128, 8, 512], F32)

        def carve(reg, off, shape, dt):
            n = int(np.prod(shape))
            bpe = 4 if dt == F32 else 2
            assert off % 4 == 0 and (n * bpe) % 4 == 0
            v = reg[:, off // 4:(off + n * bpe) // 4]
            if dt != F32:
                v = v.bitcast(dt)
            if len(shape) == 2:
                v = v.rearrange("p (a b) -> p a b", a=shape[0])
            elif len(shape) == 3:
                v = v.rearrange("p (a b c) -> p a b c", a=shape[0], b=shape[1])
            return v, off + n * bpe

        C.all_sems = []

        def new_sem(name):
            h = nc.alloc_semaphore(name=name)
            C.all_sems.append(h)
            return h
        NDMA_SEMS = 24
        C.semsets = [dict(eng={e: new_sem("s%d_%s" % (i, e)) for e in ENGS}, dma=[new_sem("d%d_%d" % (i, j)) for j in range(NDMA_SEMS)])
                     for i in range(2)]
        C.bar = new_sem("bar")
        C.phase_idx = 0
        castB = {}

        def cast_buf(name):
            b = Buf(name, persist=True, group=True)
            b.sem = new_sem("c_" + name)
            castB[name] = b
            return b

        C.P = Prog(nc, "p0")
        with contextlib.ExitStack() as ps:
            psb = lambda name, shape, dt: ps.enter_context(nc.sbuf_tensor(name, shape, dt))
            banks = [ps.enter_context(nc.psum_tensor("b0_%d" % i, [128, 512], F32)) for i in range(8)]
            bankB = [Buf("bank%d" % i) for i in range(8)]
            C.cast_queue = None

            def cast(key, idx_slices, tag):
                for i, sl in enumerate(idx_slices):
                    b = cast_buf("%s_%d" % (tag, i))
                    src = I[key]
                    dst = Wb[key]
                    for s in sl[:-1]:
                        src = src[s]
                        dst = dst[s]
                    r0, r1 = sl[-1]
                    ncol = src.shape[-1]
                    cols = [(0, ncol)] if ncol <= 2048 else [(0, 1408), (1408, 2816)]
                    rstep = 704 if ncol <= 2048 else 512
                    for (c0, c1) in cols:
                        for ra in range(r0, r1, rstep):
                            rb = min(r1, ra + rstep)

                            def issue(after=(), src=src, dst=dst, ra=ra, rb=rb, c0=c0, c1=c1, b=b):
                                dma(C, "pool", dst[ra:rb, c0:c1], src[ra:rb, c0:c1], [], [b], b, after=after)
                            if C.cast_queue is None:
                                issue()
                            else:
                                C.cast_queue.append(issue)

            def ada_cast(l, v):
                b = cast_buf("ada%d_%d" % (l, v))
                i0 = l * 72 + v * 8
                dma(C, "pool", ada_b16[i0:i0 + 8].rearrange("j p f -> (j p) f"), I["ada_h"][i0:i0 + 8].rearrange("j p f -> (j p) f"), [], [b], b)
            C.ada_cast = ada_cast
            for v_ in range(3):
                ada_cast(0, v_)

            C.wi_bounds = {0: [0, 2, 11, NFC], 1: [0, 11, NFC], 2: [0, 11, NFC], 3: [0, 11, NFC]}
            def ffn_cast(f):
                bnd = C.wi_bounds[f]
                cast("wi_h", [(f, (bnd[i] * 128, bnd[i + 1] * 128)) for i in range(len(bnd) - 1)], "wi%d" % f)
                cast("wo_h", [(f, (0, 1024))], "wo%d" % f)
            ffn_cast(0)

            def early_casts():
                cast("wqkA_h", [((0, 1024),)], "wqkA")
                cast("wvA_h", [((0, 512),)], "wvA")

            def deferred_casts():
                cast("woA_h", [((0, 1024),)], "woA")
                ffn_cast(1)
                ffn_cast(2)
                cast("wqkB_h", [((0, 768),)], "wqkB")
                cast("wvB_h", [((0, 128),)], "wvB")
                cast("woB_h", [((0, 1024),)], "woB")
                ffn_cast(3)

            def wiB2(f, j):
                bnd = C.wi_bounds[f]
                for i in range(len(bnd) - 1):
                    if bnd[i] <= j < bnd[i + 1]:
                        return [castB["wi%d_%d" % (f, i)]]

            def woB_(f, dc):
                return [castB["wo%d_0" % f]]
            C.wiB2 = wiB2
            C.woB_ = woB_

            t_cv = xtmp[:, 0, 0:16]
            t_lam = xtmp[0:1, 1, 0:256]
            t_lam2 = xtmp[0:1, 2, 0:8]
            t_rm = xtmp[:, 3, 0:128]
            t_mk = xtmp[:, 4, 0:512]
            Bs = {n: Buf(n) for n in ["cv", "lam", "lam2", "rm", "mk", "smallp", "adab", "csil", "consts", "neglam", "gain2", "esink", "masksbf", "mod"]}
            dma(C, "sp", t_cv, I["cvec"], [], [Bs["cv"]], Bs["cv"])
            dma(C, "sp", smallp[:], I["smallp"], [], [Bs["smallp"]], Bs["smallp"])
            dma(C, "sp", adab[:].rearrange("p l j -> p (l j)"), I["ada_bT"], [], [Bs["adab"]], Bs["adab"])
            dma(C, "sp", t_lam, I["lamv"], [], [Bs["lam"]], Bs["lam"])
            dma(C, "sp", t_rm, I["rmat"], [], [Bs["rm"]], Bs["rm"])
            dma(C, "sp", t_mk, I["masks"], [], [Bs["mk"]], Bs["mk"])
            C.P.add("dve", lambda e: e.memset(ones_bf[:], 1.0), [], [Bs["consts"]])
            C.P.add("dve", lambda e: e.memset(ones_f[:], 1.0), [], [Bs["consts"]])
            C.P.add("dve", lambda e: e.memset(bd_bf[:], 0.0), [], [Bs["consts"]])
            C.P.add("dve", lambda e: e.memset(bd_bf[0:64, 0:64], 1.0), [], [Bs["consts"]])
            C.P.add("dve", lambda e: e.memset(bd_bf[64:128, 64:128], 1.0), [], [Bs["consts"]])
            cp(C, rm_bf[:], t_rm, [Bs["rm"]], [Bs["consts"]])
            cp(C, masks_bf[:].rearrange("p a b -> p (a b)"), t_mk, [Bs["mk"]], [Bs["masksbf"]])
            act(C, csil[:].rearrange("p a b -> p (a b)"), t_cv, AF.Silu, [Bs["cv"]], [Bs["csil"]])
            cp(C, csil_bf[:].rearrange("p a b -> p (a b)"), csil[:].rearrange("p a b -> p (a b)"), [Bs["csil"]], [Bs["csil"]])
            tt(C, t_lam[:, 0:128], t_lam[:, 0:128], t_lam[:, 128:256], ALU.mult, [Bs["lam"]], [Bs["lam"]])
            C.P.add("dve", lambda e: e.tensor_reduce(out=t_lam2[:, 0:2], in_=t_lam[:, 0:128].rearrange("p (a b) -> p a b", a=2), axis=AX.X, op=ALU.add),
                    [Bs["lam"]], [Bs["lam2"]])
            act(C, t_lam2[:, 2:4], t_lam2[:, 0:2], AF.Exp, [Bs["lam2"]], [Bs["lam2"]])
            tt(C, t_lam2[:, 4:5], t_lam2[:, 2:3], t_lam2[:, 3:4], ALU.subtract, [Bs["lam2"]], [Bs["lam2"]])
            ts(C, t_lam2[:, 6:7], t_lam2[:, 4:5], -1.0, -LAM_INIT0, ALU.mult, ALU.add, [Bs["lam2"]], [Bs["lam2"]])
            cp(C, t_lam2[:, 7:8], t_lam2[:, 6:7], [Bs["lam2"]], [Bs["lam2"]])
            mm(C, banks[7][:, 0:2], ones_f[0:1, :], t_lam2[0:1, 6:8], True, True, [Bs["consts"], Bs["lam2"]], [bankB[7]])
            cp(C, neglam[:], banks[7][:, 0:2], [bankB[7]], [Bs["neglam"]])
            ts(C, gain2[:], smallp[:, 4:5], 1.0 - LAM_INIT0, None, ALU.mult, None, [Bs["smallp"]], [Bs["gain2"]])
            act(C, esink[:], smallp[:, 5:21], AF.Exp, [Bs["smallp"]], [Bs["esink"]])
            C.P.emit(C)
        if debug:
            pass

        def alloc_regA(C):
            off = 0
            C.h, off = carve(regA, off, [8, 512], BF16)
            C.G, off = carve(regA, off, [NFC, 512], BF16)
            C.sg = []
            for i in range(2):
                t, off = carve(regA, off, [512], F32)
                C.sg.append(t)
            C.slots = []
            for i in range(4):
                t, off = carve(regA, off, [2048], BF16)
                C.slots.append(t)
            C.woslots = []
            for i in range(2):
                t, off = carve(regA, off, [NFC, 128], BF16)
                C.woslots.append(t)
            C.sq = []
            for i in range(4):
                t, off = carve(regA, off, [512], BF16)
                C.sq.append(t)
            assert off <= RA_BYTES, off
            C.hB = [Buf("h%d" % i) for i in range(8)]
            C.GB = Buf("G")
            C.sgB = [Buf("sg0"), Buf("sg1")]
            C.sqB = [Buf("sq%d" % i) for i in range(4)]
            C.stream = Stream(C, C.slots, "slot")
            C.wostream = Stream(C, C.woslots, "woslot")

        def alloc_regC_proj(C):
            off = 0
            C.lnb, C.rsb, C.tmpf, C.qn, C.t1, C.t2 = [], [], [], [], [], []
            for i in range(2):
                t, off = carve(regC, off, [512], F32); C.lnb.append(t); C.rsb.append(t)
                t, off = carve(regC, off, [512], F32); C.tmpf.append(t); C.t1.append(t)
                t, off = carve(regC, off, [512], BF16); C.qn.append(t)
                t, off = carve(regC, off, [512], F32); C.t2.append(t)
            C.qout, off = carve(regC, off, [8, 512], BF16)
            C.kout, off = carve(regC, off, [8, 512], BF16)
            C.vout, off = carve(regC, off, [4, 1024], BF16)
            C.cs, C.sn = [], []
            for i in range(2):
                t, off = carve(regC, off, [512], F32); C.cs.append(t)
                t, off = carve(regC, off, [512], F32); C.sn.append(t)
            assert off <= RC_BYTES, off
            mk = lambda n: [Buf(n + "0"), Buf(n + "1")]
            C.lnB = mk("ln"); C.rsB = C.lnB
            C.tmpfB = mk("tmpf"); C.t1B = C.tmpfB
            C.qnB, C.t2B = mk("qn"), mk("t2")
            C.qoutB, C.koutB, C.voutB = Buf("qout"), Buf("kout"), Buf("vout")
            C.csB = mk("cs")
            C.cs_stream_i = 0

        def two_h(C):
            h2 = xtmp[:, 4:8, :].rearrange("p a b -> p (a b)").bitcast(BF16).rearrange("p (c t) -> p c t", c=8)
            return [(C.h, C.hB), (h2, [Buf("hx%d" % i) for i in range(8)])]

        def new_banks(C, ps, tag):
            C.banks = [ps.enter_context(nc.psum_tensor("b%s_%d" % (tag, i), [128, 512], F32)) for i in range(8)]
            C.bankB = [Buf("bank%d" % i) for i in range(8)]

        def segs_of(w, ntok):
            return w if isinstance(w, list) else [(0, ntok, w)]

        def modap(l, v, ch, w):
            return mod[:, l, v * 8 + ch, w:w + 1]

        C.modB = Buf("mod")
        C.constB = Buf("const")

        def mod_vector(C, l, v, defer=False, cb=0):
            pb = C.banks[7]
            pB = C.bankB[7]
            for ch in range(8):
                j = v * 8 + ch
                slot, sB = C.stream.load(ada_b16[l * 72 + j], [castB["ada%d_%d" % (l, v)]], view=lambda t: t[:, 0:1024])
                w16 = slot[:, 0:1024].rearrange("p (c f) -> p c f", c=8)
                for dc in range(8):
                    mm(C, pb[:, cb + 2 * ch:cb + 2 * ch + 2], w16[:, dc, :], csil_bf[:, dc, :], dc == 0, dc == 7, [sB], [pB])

            def evac():
                dst = mod[:, l, v * 8:(v + 1) * 8, :]
                tt(C, dst, pb[:, cb:cb + 16].rearrange("p (a b) -> p a b", a=8),
                   adab[:, l, v * 8:(v + 1) * 8].unsqueeze(2).broadcast_to([128, 8, 2]), ALU.add, [pB], [C.modB])
                if v in (1, 4, 7):
                    ts(C, dst, dst, 1.0, None, ALU.add, None, [C.modB], [C.modB])
                if v in (2, 8):
                    ts(C, dst, dst, 0.5, None, ALU.mult, None, [C.modB], [C.modB])
            if defer:
                return evac
            evac()
            return None

        def norm_mod(C, xap, xB, ntok, l, vsh, vsc, w, filler=None):
            ssb, ssB = C.banks[6], C.bankB[6]
            for ch in range(8):
                s = ch % 4
                if ch in (1, 4, 6):
                    tt(C, C.sq[s][:, :ntok], xap[:, ch, :], xap[:, ch, :], ALU.mult, [xB], [C.sqB[s]])
                else:
                    act(C, C.sq[s][:, :ntok], xap[:, ch, :], AF.Square, [xB], [C.sqB[s]])
                mm(C, ssb[:, :ntok], ones_bf[:], C.sq[s][:, :ntok], ch == 0, ch == 7, [C.sqB[s]], [ssB])
            after = filler() if filler is not None else None
            act(C, C.lnb[0][:, :ntok], ssb[:, :ntok], AF.Ln, [ssB], [C.lnB[0]], bias=EPS, scale=1.0 / D)
            act(C, C.rsb[0][:, :ntok], C.lnb[0][:, :ntok], AF.Exp, [C.lnB[0]], [C.rsB[0]], scale=-0.5)
            for ch in range(8):
                s = ch % 2
                for (lo, hi, ww) in segs_of(w, ntok):
                    stt(C, C.tmpf[s][:, lo:hi], xap[:, ch, lo:hi], modap(l, vsc, ch, ww), C.rsb[0][:, lo:hi], ALU.mult, ALU.mult,
                        [xB, C.rsB[0], C.modB], [C.tmpfB[s]])
                    act(C, C.h[:, ch, lo:hi], C.tmpf[s][:, lo:hi], AF.Identity, [C.tmpfB[s], C.modB], [C.hB[ch]], bias=modap(l, vsh, ch, ww))
            if after is not None:
                after()

        def ffn(C, xap, xB, ntok, f, l, vg, w):
            ffn_a(C, ntok, f)
            ffn_b(C, xap, xB, ntok, f, l, vg, w)

        def ffn_a(C, ntok, f):
            for j in range(NFC):
                slot, sB = C.stream.load(Wb["wi_h"][f, j * 128:(j + 1) * 128, :], C.wiB2(f, j))
                wv = slot.rearrange("p (c f) -> p c f", c=8)
                bg, bgB = C.banks[2 * (j % 2)], C.bankB[2 * (j % 2)]
                bu, buB = C.banks[2 * (j % 2) + 1], C.bankB[2 * (j % 2) + 1]
                for dc in range(8):
                    mm(C, bg[:, :ntok], wv[:, dc, 0:128], C.h[:, dc, :ntok], dc == 0, dc == 7, [sB, C.hB[dc]], [bgB])
                for dc in range(8):
                    mm(C, bu[:, :ntok], wv[:, dc, 128:256], C.h[:, dc, :ntok], dc == 0, dc == 7, [sB, C.hB[dc]], [buB])
                s = j % 2
                act(C, C.sg[s][:, :ntok], bg[:, :ntok], AF.Silu, [bgB], [C.sgB[s]])
                tt(C, C.G[:, j, :ntok], bu[:, :ntok], C.sg[s][:, :ntok], ALU.mult, [buB, C.sgB[s]], [C.GB])

        def ffn_b(C, xap, xB, ntok, f, l, vg, w, mid=None):
            for dc in range(8):
                if dc == 4 and mid is not None:
                    hsave = (C.h, C.hB)
                    mid()
                    C.h, C.hB = hsave
                slot, sB = C.wostream.load(Wb["wo_h"][f, dc * 128:(dc + 1) * 128, :], C.woB_(f, dc),
                                           view=lambda t: t.rearrange("p a b -> p (a b)"))
                by, byB = C.banks[4 + dc % 2], C.bankB[4 + dc % 2]
                for j in range(NFC):
                    mm(C, by[:, :ntok], slot[:, j, :], C.G[:, j, :ntok], j == 0, j == NFC - 1, [sB, C.GB], [byB])
                for (lo, hi, ww) in segs_of(w, ntok):
                    stt(C, xap[:, dc, lo:hi], by[:, lo:hi], modap(l, vg, dc, ww), xap[:, dc, lo:hi], ALU.mult, ALU.add,
                        [byB, xB, C.modB], [xB])

        def qk_chunks(C, ntok, chunk_specs, wsrc, wsrcB, rope_off, gcol_fn):
            n = len(chunk_specs)
            state = {}
            if rope_off is not None:
                k = C.cs_stream_i % 2
                C.cs_stream_i += 1
                dma(C, "sp", C.cs[k][:, :ntok], rope_off[0], [], [C.csB[k]], C.csB[k])
                dma(C, "sp", C.sn[k][:, :ntok], rope_off[1], [], [C.csB[k]], C.csB[k])
                cs, sn, csB = C.cs[k], C.sn[k], C.csB[k]

            def proj(c):
                if c % 2 == 0:
                    slot, sB = C.stream.load(wsrc[(c // 2) * 128:(c // 2 + 1) * 128, :], wsrcB)
                    state["slot"] = (slot, sB)
                slot, sB = state["slot"]
                wv = slot.rearrange("p (i c f) -> p i c f", i=2, c=8)
                pb, pB = C.banks[c % 4], C.bankB[c % 4]
                for dc in range(8):
                    mm(C, pb[:, :ntok], wv[:, c % 2, dc, :], C.h[:, dc, :ntok], dc == 0, dc == 7, [sB, C.hB[dc]], [pB])

            def sqr(c):
                s = c % 2
                pb, pB = C.banks[c % 4], C.bankB[c % 4]
                act(C, C.sq[s][:, :ntok], pb[:, :ntok], AF.Square, [pB], [C.sqB[s]])

            def ssmm(c):
                s = c % 2
                mm(C, C.banks[6][:, :ntok], bd_bf[:], C.sq[s][:, :ntok], True, True, [C.sqB[s]], [C.bankB[6]])

            def lnq(c):
                s = c % 2
                pb, pB = C.banks[c % 4], C.bankB[c % 4]
                act(C, C.lnb[s][:, :ntok], C.banks[6][:, :ntok], AF.Ln, [C.bankB[6]], [C.lnB[s]], bias=EPS, scale=1.0 / 64)
                act(C, C.rsb[s][:, :ntok], C.lnb[s][:, :ntok], AF.Exp, [C.lnB[s]], [C.rsB[s]], scale=-0.5)
                dst, dstB = chunk_specs[c]
                gc = gcol_fn(c)
                if rope_off is None:
                    stt(C, dst, pb[:, :ntok], smallp[:, gc:gc + 1], C.rsb[s][:, :ntok], ALU.mult, ALU.mult, [pB, C.rsB[s]], [dstB])
                else:
                    stt(C, C.qn[s][:, :ntok], pb[:, :ntok], smallp[:, gc:gc + 1], C.rsb[s][:, :ntok], ALU.mult, ALU.mult,
                        [pB, C.rsB[s]], [C.qnB[s]])

            def rotmm(c):
                if rope_off is None:
                    return
                s = c % 2
                mm(C, C.banks[7][:, :ntok], rm_bf[:], C.qn[s][:, :ntok], True, True, [C.qnB[s]], [C.bankB[7]])

            def fin(c):
                if rope_off is None:
                    return
                s = c % 2
                dst, dstB = chunk_specs[c]
                tt(C, C.t2[s][:, :ntok], C.banks[7][:, :ntok], sn[:, :ntok], ALU.mult, [C.bankB[7], csB], [C.t2B[s]])
                tt(C, C.t1[s][:, :ntok], C.qn[s][:, :ntok], cs[:, :ntok], ALU.mult, [C.qnB[s], csB], [C.t1B[s]])
                tt(C, dst, C.t1[s][:, :ntok], C.t2[s][:, :ntok], ALU.add, [C.t1B[s], C.t2B[s]], [dstB])

            pipeline([proj, sqr, ssmm, lnq, rotmm, fin], n)

        def v_proj(C, ntok, npieces, wsrc, wsrcB, vdst_fn, vB):
            nsub = ntok // 128
            cnt = 0
            for pc in range(npieces):
                slot, sB = C.stream.load(wsrc[pc * 128:(pc + 1) * 128, :], wsrcB)
                wv = slot.rearrange("p (c f) -> p c f", c=8)
                for sub in range(nsub):
                    pb, pB = C.banks[4 + cnt % 2], C.bankB[4 + cnt % 2]
                    cnt += 1
                    for dc in range(8):
                        mm(C, pb[:, 0:256], C.h[:, dc, sub * 128:(sub + 1) * 128], wv[:, dc, :], dc == 0, dc == 7, [sB, C.hB[dc]], [pB])
                    act(C, vdst_fn(sub, pc), pb[:, 0:256], AF.Copy, [pB], [vB])

        tilesA = []
        for i in range(4):
            tilesA.append(dict(kind="own", xcol=512 * i, ntok=512, src=("x", 512 * i), key=512 * i, rope=512 * i, q=True, w=0))
        HC = HALO + NCTX
        HCSEG = [(0, HALO, 0), (HALO, HC, 1)]
        tilesA.append(dict(kind="hc", xcol=OWN, ntok=HC, src=("hc", OWN), key=None, rope="hc", q=True, w=HCSEG))
        o = OWN + HALO
        while o < SEQ:
            n = min(512, SEQ - o)
            tilesA.append(dict(kind="rest", xcol=None, ntok=n, src=("x", o), key=o, rope=o, q=False, w=0))
            o += n

        pending_mods = [(0, 3), (0, 4), (0, 5), (0, 6), (0, 7), (0, 8)] + [(1, v) for v in range(9)]

        C.P = Prog(nc, "p1")
        with contextlib.ExitStack() as ps:
            new_banks(C, ps, "1")
            alloc_regA(C)
            alloc_regC_proj(C)
            kscrB = Buf("kscr", group=True)
            vscrB = Buf("vscr", group=True)
            qscrB = Buf("qscr", group=True)
            xtmpB = Buf("xtmp")
            allxB = []
            mod_order = [(0, 3), (0, 4), (0, 5), (0, 6), (0, 7), (0, 8)] + [(1, v) for v in range(9)]
            cast_ahead = list(mod_order)
            for _ in range(4):
                C.ada_cast(*cast_ahead.pop(0))
            for v in range(3):
                mod_vector(C, 0, v)
            early_casts()
            for ti, T in enumerate(tilesA):
                ntok = T["ntok"]
                if T["xcol"] is not None:
                    xap = x_sb[:, :, T["xcol"]:T["xcol"] + ntok]
                else:
                    xap = xtmp[:, :, :ntok]
                xB = Buf("x_%d" % ti) if T["xcol"] is not None else xtmpB
                allxB.append(xB)
                if T["src"][0] == "x":
                    src = I["xT"][:, T["src"][1]:T["src"][1] + ntok]
                    dma(C, "sp", xap, src.rearrange("(c p) t -> p c t", p=128), [], [xB], xB)
                else:
                    dma(C, "sp", xap[:, :, 0:HALO], I["xT"][:, OWN:OWN + HALO].rearrange("(c p) t -> p c t", p=128), [], [xB], xB)
                    dma(C, "sp", xap[:, :, HALO:HC], I["ctxT"].rearrange("(c p) t -> p c t", p=128), [], [xB], xB)
                w = T["w"]
                def mod_filler(cb=0):
                    if cast_ahead:
                        C.ada_cast(*cast_ahead.pop(0))
                    if pending_mods:
                        l_, v_ = pending_mods.pop(0)
                        return mod_vector(C, l_, v_, defer=True, cb=cb)
                    return None
                def mod_filler2():
                    e1 = mod_filler(0)
                    e2 = mod_filler(16)
                    return lambda: (e1(), e2())
                norm_mod(C, xap, xB, ntok, 0, 0, 1, w, filler=(mod_filler if ti > 0 else mod_filler2))
                ffn(C, xap, xB, ntok, 0, 0, 2, w)
                norm_mod(C, xap, xB, ntok, 0, 3, 4, w, filler=mod_filler)
                specs = [(C.kout[:, c, :ntok], C.koutB) for c in range(8)]
                if T["q"]:
                    specs += [(C.qout[:, c, :ntok], C.qoutB) for c in range(8)]
                if T["rope"] == "hc":
                    rsrc = (I["cs_hc"][:, 0, :], I["cs_hc"][:, 1, :])
                else:
                    rsrc = (I["cos_t"][:, T["rope"]:T["rope"] + ntok], I["sin_t"][:, T["rope"]:T["rope"] + ntok])
                qk_chunks(C, ntok, specs, Wb["wqkA_h"], [castB["wqkA_0"]], rsrc,
                          lambda c: 1 if c < 8 else 0)
                v_proj(C, ntok, 4, Wb["wvA_h"], [castB["wvA_0"]],
                       lambda sub, pc: C.vout[:, sub, pc * 256:(pc + 1) * 256], C.voutB)
                if T["kind"] == "hc":
                    kparts = [(OWN, 0, HALO), (SEQ, HALO, HC)]
                else:
                    kparts = [(T["key"], 0, ntok)]
                for (k0, lo, hi) in kparts:
                    dma(C, "pool", kT_scr[:, :, k0:k0 + hi - lo].rearrange("h p t -> p h t"), C.kout[:, :, lo:hi], [C.koutB], [kscrB], kscrB)
                    dma(C, "pool", v_scr[k0:k0 + hi - lo, :].rearrange("(s p) f -> p s f", p=128), C.vout[:, lo // 128:hi // 128, :],
                        [C.voutB], [vscrB], vscrB)
                if T["q"]:
                    q0 = T["xcol"]
                    dma(C, "pool", qT_scr[:, :, q0:q0 + ntok].rearrange("h p t -> p h t"), C.qout[:, :, :ntok], [C.qoutB], [qscrB], qscrB)
            while cast_ahead:
                C.ada_cast(*cast_ahead.pop(0))
            while pending_mods:
                mod_vector(C, *pending_mods.pop(0))
            if debug:
                dB = Buf("dbg")
                allx = [Buf("xall")]
                dma(C, "sp", xdbg, x_sb[:], allxB, [dB], dB)
                dma(C, "sp", moddbg, mod[:].rearrange("p l j w -> p (l j w)"), [C.modB], [dB], dB)
            C.P.emit(C)
        if stop_after == 1:
            finish(C)
            return nc

        C.P = Prog(nc, "p2a")
        with contextlib.ExitStack() as ps:
            new_banks(C, ps, "2a")
            off = 0
            Kh, Vh, Qh = [], [], []
            for i in range(2):
                t, off = carve(regA, off, [NKA], BF16); Kh.append(t)
                t, off = carve(regA, off, [NKA // 128, 128], BF16); Vh.append(t)
                t, off = carve(regA, off, [XR], BF16); Qh.append(t)
            pbuf = [[None, None], [None, None]]
            for c in range(2):
                for s in range(2):
                    pbuf[c][s], off = carve(regA, off, [512], BF16)
            ep = []
            for i in range(4):
                t, off = carve(regA, off, [512], F32); ep.append(t)
            eo = [xtmp[:, 0, :], xtmp[:, 1, :]]
            ers = [xtmp[:, 2, :], xtmp[:, 3, :]]
            sqs = [xtmp[:, 4, :].bitcast(BF16)[:, 0:512], xtmp[:, 5, :].bitcast(BF16)[:, 0:512]]
            assert off <= RA_BYTES, off
            attn_out, o2 = carve(regC, 0, [8, XR], BF16)
            assert o2 <= RC_BYTES
            KhB = [Buf("Kh0"), Buf("Kh1")]
            VhB = [Buf("Vh0"), Buf("Vh1")]
            QhB = [Buf("Qh0"), Buf("Qh1")]
            pB = [[Buf("p00"), Buf("p01")], [Buf("p10"), Buf("p11")]]
            epB = [Buf("ep%d" % i) for i in range(4)]
            eoB, ersB, sqsB = [Buf("eo0"), Buf("eo1")], [Buf("ers0"), Buf("ers1")], [Buf("sqs0"), Buf("sqs1")]
            aoB = Buf("attn_out")
            sbank = [[C.banks[0], C.banks[2]], [C.banks[1], C.banks[3]]]
            sbankB = [[C.bankB[0], C.bankB[2]], [C.bankB[1], C.bankB[3]]]
            Ob, ObB = [C.banks[4], C.banks[5]], [C.bankB[4], C.bankB[5]]
            Lb, LbB = [C.banks[6], C.banks[7]], [C.bankB[6], C.bankB[7]]

            allk = list(range(NKA // 128))
            units = []
            for hd in range(8):
                for i in range(4):
                    units.append((hd, 512 * i, 512, allk))
                units.append((hd, OWN, HALO, allk))
                units.append((hd, OWN + HALO, NCTX, [32, 33]))
            items = [(u, i) for u in range(len(units)) for i in range(len(units[u][3]))]
            T = len(items)
            loaded = set()

            def load_head(hd):
                hs = hd % 2
                dma(C, "sp", Kh[hs], kT_scr[hd], [], [KhB[hs]], KhB[hs])
                dma(C, "sp", Vh[hs], v_scr[:, hd * 128:(hd + 1) * 128].rearrange("(c p) e -> p c e", p=128), [], [VhB[hs]], VhB[hs])
                dma(C, "sp", Qh[hs], qT_scr[hd], [], [QhB[hs]], QhB[hs])
                if hd == 1:
                    C.cast_queue = []
                    deferred_casts()

            def qk(t):
                u, i = items[t]
                hd, qcol, nq, kch = units[u]
                hs = hd % 2
                if hd not in loaded:
                    loaded.add(hd)
                    load_head(hd)
                kc = kch[i]
                st_ = t % 2
                for c in range(2):
                    mm(C, sbank[c][st_][:, :nq], Kh[hs][c * 64:(c + 1) * 64, kc * 128:(kc + 1) * 128],
                       Qh[hs][c * 64:(c + 1) * 64, qcol:qcol + nq], True, True, [KhB[hs], QhB[hs]], [sbankB[c][st_]])

            def ex(t):
                u, i = items[t]
                hd, qcol, nq, kch = units[u]
                st_ = t % 2
                for c in range(2):
                    act(C, pbuf[c][st_][:, :nq], sbank[c][st_][:, :nq], AF.Exp, [sbankB[c][st_]], [pB[c][st_]], scale=0.125)

            def pv(t):
                u, i = items[t]
                hd, qcol, nq, kch = units[u]
                hs = hd % 2
                n = len(kch)
                kc = kch[i]
                st_ = t % 2
                for c in range(2):
                    mm(C, Ob[c][:, :nq], Vh[hs][:, kc, :], pbuf[c][st_][:, :nq], i == 0, i == n - 1, [VhB[hs], pB[c][st_]], [ObB[c]])
                    last_pe[0] = mm(C, Lb[c][:, :nq], ones_bf[:], pbuf[c][st_][:, :nq], i == 0, i == n - 1, [pB[c][st_]], [LbB[c]])
                return i == n - 1

            def epi_a(u):
                hd, qcol, nq, kch = units[u]
                k = u % 2
                cp(C, ep[0][:, :nq], Lb[0][:, :nq], [LbB[0]], [epB[0]])
                cp(C, ep[1][:, :nq], Lb[1][:, :nq], [LbB[1]], [epB[1]])
                cp(C, ep[2][:, :nq], Ob[0][:, :nq], [ObB[0]], [epB[2]])
                cp(C, ep[3][:, :nq], Ob[1][:, :nq], [ObB[1]], [epB[3]])
                C.P.add("dve", lambda e: e.reciprocal(out=ep[0][:, :nq], in_=ep[0][:, :nq]), [epB[0]], [epB[0]])
                C.P.add("dve", lambda e: e.reciprocal(out=ep[1][:, :nq], in_=ep[1][:, :nq]), [epB[1]], [epB[1]])
                tt(C, ep[2][:, :nq], ep[2][:, :nq], ep[0][:, :nq], ALU.mult, [epB[2], epB[0]], [epB[2]])
                tt(C, ep[3][:, :nq], ep[3][:, :nq], ep[1][:, :nq], ALU.mult, [epB[3], epB[1]], [epB[3]])
                stt(C, eo[k][:, :nq], ep[3][:, :nq], neglam[:, 0:1], ep[2][:, :nq], ALU.mult, ALU.add, [epB[3], epB[2]], [eoB[k]])
                tt(C, sqs[k][:, :nq], eo[k][:, :nq], eo[k][:, :nq], ALU.mult, [eoB[k]], [sqsB[k]])

            def epi_b(u, bank, bankB_):
                hd, qcol, nq, kch = units[u]
                k = u % 2
                mm(C, bank[:, :nq], ones_bf[:], sqs[k][:, :nq], True, True, [sqsB[k]], [bankB_])
                act(C, ers[k][:, :nq], bank[:, :nq], AF.Ln, [bankB_], [ersB[k]], bias=EPS, scale=1.0 / 128)
                act(C, ers[k][:, :nq], ers[k][:, :nq], AF.Exp, [ersB[k]], [ersB[k]], scale=-0.5)
                stt(C, attn_out[:, hd, qcol:qcol + nq], eo[k][:, :nq], gain2[:, 0:1], ers[k][:, :nq], ALU.mult, ALU.mult,
                    [eoB[k], ersB[k]], [aoB])

            DELAY = 12
            pending = []
            last_pe = [None]
            for s_ in range(T + 2):
                if s_ - 2 >= 0:
                    if pv(s_ - 2):
                        u = items[s_ - 2][0]
                        epi_a(u)
                        pending.append((s_ + DELAY, u))
                        if C.cast_queue:
                            C.cast_queue.pop(0)(after=[last_pe[0]])
                if 0 <= s_ - 1 < T:
                    ex(s_ - 1)
                if s_ < T:
                    qk(s_)
                while pending and pending[0][0] <= s_:
                    _, u = pending.pop(0)
                    st_ = (s_ + 1) % 2
                    epi_b(u, sbank[0][st_], sbankB[0][st_])
            while pending:
                _, u = pending.pop(0)
                epi_b(u, sbank[0][0], sbankB[0][0])
            while C.cast_queue:
                C.cast_queue.pop(0)()
            C.cast_queue = None
            if debug:
                dB = Buf("dbg")
                dma(C, "sp", aodbg, attn_out, [aoB], [dB], dB)
            C.P.emit(C)
        if stop_after == 2:
            finish(C)
            return nc

        def out_proj(C, xap, xB, ntok, col, ao, aoB, wkey, wB, l, w):
            for dc in range(8):
                slot, sB = C.stream.load(Wb[wkey][dc * 128:(dc + 1) * 128, :], wB, view=lambda t: t[:, 0:1024])
                wv = slot[:, 0:1024].rearrange("p (h f) -> p h f", h=8)
                by, byB = C.banks[4 + dc % 2], C.bankB[4 + dc % 2]
                for hd in range(8):
                    mm(C, by[:, :ntok], wv[:, hd, :], ao[:, hd, col:col + ntok], hd == 0, hd == 7, [sB, aoB], [byB])
                for (lo, hi, ww) in segs_of(w, ntok):
                    stt(C, xap[:, dc, lo:hi], by[:, lo:hi], modap(l, 5, dc, ww), xap[:, dc, lo:hi], ALU.mult, ALU.add, [byB, xB, C.modB], [xB])

        tilesB = [dict(xcol=512 * i, ntok=512, w=0) for i in range(4)] + [dict(xcol=OWN, ntok=HC, w=HCSEG)]

        C.P = Prog(nc, "p2b")
        with contextlib.ExitStack() as ps:
            new_banks(C, ps, "2b")
            alloc_regA(C)
            C.lnb = [xtmp[:, 0, :]]
            C.rsb = C.lnb
            C.tmpf = [xtmp[:, 2, :], xtmp[:, 3, :]]
            C.lnB = [Buf("ln")]; C.rsB = C.lnB
            C.tmpfB = [Buf("tmpf0"), Buf("tmpf1")]
            attn_out, _ = carve(regC, 0, [8, XR], BF16)
            aoB = Buf("attn_out")
            allxB = []
            hb = two_h(C)
            tl = []
            for ti, T in enumerate(tilesB):
                ntok, col, w = T["ntok"], T["xcol"], T["w"]
                xB = Buf("x_%d" % ti)
                allxB.append(xB)
                tl.append((x_sb[:, :, col:col + ntok], xB, ntok, col, w))

            def proj2b(ti):
                xap, xB, ntok, col, w = tl[ti]
                out_proj(C, xap, xB, ntok, col, attn_out, aoB, "woA_h", [castB["woA_0"]], 0, w)

            def norm2b(ti):
                xap, xB, ntok, col, w = tl[ti]
                C.h, C.hB = hb[ti % 2]
                norm_mod(C, xap, xB, ntok, 0, 6, 7, w)
            proj2b(0)
            norm2b(0)
            for ti in range(len(tl)):
                xap, xB, ntok, col, w = tl[ti]
                C.h, C.hB = hb[ti % 2]
                ffn_a(C, ntok, 1)
                nxt = ti + 1 < len(tl)
                if nxt:
                    proj2b(ti + 1)
                ffn_b(C, xap, xB, ntok, 1, 0, 8, w, mid=((lambda ti=ti: norm2b(ti + 1)) if nxt else None))
            if debug:
                dB = Buf("dbg")
                dma(C, "sp", xdbg, x_sb[:], allxB, [dB], dB)
            C.P.emit(C)
        if stop_after == 3:
            finish(C)
            return nc

        C.P = Prog(nc, "p3")
        with contextlib.ExitStack() as ps:
            new_banks(C, ps, "3")
            alloc_regA(C)
            alloc_regC_proj(C)
            kscrB = Buf("kBscr", group=True)
            vscrB = Buf("vBscr", group=True)
            qscrB = Buf("qBscr", group=True)
            allxB = [Buf("x_%d" % ti) for ti in range(len(tilesB))]
            hb = two_h(C)

            def norm3a(ti):
                T = tilesB[ti]
                C.h, C.hB = hb[ti % 2]
                norm_mod(C, x_sb[:, :, T["xcol"]:T["xcol"] + T["ntok"]], allxB[ti], T["ntok"], 1, 0, 1, T["w"])
            norm3a(0)
            for ti, T in enumerate(tilesB):
                ntok, col, w = T["ntok"], T["xcol"], T["w"]
                xap = x_sb[:, :, col:col + ntok]
                xB = allxB[ti]
                C.h, C.hB = hb[ti % 2]
                ffn_a(C, ntok, 2)
                nxt = ti + 1 < len(tilesB)
                ffn_b(C, xap, xB, ntok, 2, 1, 2, w, mid=((lambda ti=ti: norm3a(ti + 1)) if nxt else None))
                C.h, C.hB = hb[ti % 2]
                norm_mod(C, xap, xB, ntok, 1, 3, 4, w)
                isq = col < OWN
                specs = [(C.kout[:, c, :ntok], C.koutB) for c in range(4)]
                if isq:
                    specs += [(C.qout[:, c, :ntok], C.qoutB) for c in range(8)]
                if isq:
                    rsrc = (I["cos_t"][:, col:col + ntok], I["sin_t"][:, col:col + ntok])
                else:
                    rsrc = (I["cs_hc"][:, 0, :], I["cs_hc"][:, 1, :])
                qk_chunks(C, ntok, specs, Wb["wqkB_h"], [castB["wqkB_0"]], rsrc,
                          lambda c: 3 if c < 4 else 2)
                v_proj(C, ntok, 1, Wb["wvB_h"], [castB["wvB_0"]], lambda sub, pc: C.vout[:, sub, 0:256], C.voutB)
                dma(C, "pool", kB_scr[:, :, col:col + ntok].rearrange("h p t -> p h t"), C.kout[:, 0:4, :ntok], [C.koutB], [kscrB], kscrB)
                dma(C, "pool", vB_scr[col:col + ntok, :].rearrange("(s p) f -> p s f", p=128), C.vout[:, :ntok // 128, 0:256], [C.voutB], [vscrB], vscrB)
                if isq:
                    dma(C, "pool", qB_scr[:, :, col:col + ntok].rearrange("h p t -> p h t"), C.qout[:, :, :ntok], [C.qoutB], [qscrB], qscrB)
            if debug:
                dB = Buf("dbg")
                dma(C, "sp", xdbg, x_sb[:], allxB, [dB], dB)
                dma(C, "sp", moddbg, mod[:].rearrange("p l j w -> p (l j w)"), [], [dB], dB)
            C.P.emit(C)
        if stop_after == 4:
            finish(C)
            return nc

        C.P = Prog(nc, "p4a")
        with contextlib.ExitStack() as ps:
            new_banks(C, ps, "4a")
            off = 0
            KB, VB, QB = [], [], []
            NKB = XR // 128
            for i in range(2):
                t, off = carve(regA, off, [XR], BF16); KB.append(t)
                t, off = carve(regA, off, [NKB, 64], BF16); VB.append(t)
                t, off = carve(regA, off, [2, OWN], BF16); QB.append(t)
            pT = []
            for i in range(3):
                t, off = carve(regA, off, [512], BF16); pT.append(t)
            es, off = carve(regA, off, [4, 256], F32)
            lbuf, rbuf = [], []
            for i in range(2):
                t, off = carve(regA, off, [256], F32); lbuf.append(t)
                t, off = carve(regA, off, [256], F32); rbuf.append(t)
            assert off <= RA_BYTES, off
            attn_out, _ = carve(regC, 0, [8, XR], BF16)
            aoB = Buf("attn_out")
            KBB, VBB, QBB = [Buf("KB0"), Buf("KB1")], [Buf("VB0"), Buf("VB1")], [Buf("QB0"), Buf("QB1")]
            pTB = [[Buf("pT%d0" % i), Buf("pT%d1" % i)] for i in range(3)]
            esB = Buf("es")
            lB, rB = [Buf("l0"), Buf("l1")], [Buf("r0"), Buf("r1")]
            se = [C.banks[0], C.banks[1]]; seB = [C.bankB[0], C.bankB[1]]
            so = [C.banks[2], C.banks[3]]; soB = [C.bankB[2], C.bankB[3]]
            OB = [C.banks[4], C.banks[5]]; OBB = [C.bankB[4], C.bankB[5]]
            LB = [C.banks[6], C.banks[7]]; LBB = [C.bankB[6], C.bankB[7]]
            for kvh in range(4):
                for j in range(2):
                    for r in range(2):
                        g = 2 * j + r
                        cp(C, es[r * 64:(r + 1) * 64, kvh, j * 128:(j + 1) * 128],
                           esink[r * 64:(r + 1) * 64, kvh * 4 + g:kvh * 4 + g + 1].broadcast_to([64, 128]), [], [esB])
            unitsB = [(kvh, n) for kvh in range(4) for n in range(16)]

            def klist_of(n):
                return [((n - 1) if n > 0 else 16, 0 if n > 0 else 2), (n, None), ((n + 1) if n < 15 else 16, 1 if n < 15 else 3),
                        (17, None), (18, None)]
            itemsB = [(u, i) for u in range(len(unitsB)) for i in range(5)]
            TB = len(itemsB)
            loadedB = set()

            def load_kvh(kvh):
                hs = kvh % 2
                dma(C, "sp", KB[hs], kB_scr[kvh], [], [KBB[hs]], KBB[hs])
                dma(C, "sp", VB[hs], vB_scr[:, kvh * 64:(kvh + 1) * 64].rearrange("(c p) e -> p c e", p=128), [], [VBB[hs]], VBB[hs])
                dma(C, "sp", QB[hs], qB_scr[2 * kvh:2 * kvh + 2].rearrange("j p t -> p j t"), [], [QBB[hs]], QBB[hs])

            def qkB(t):
                u, i = itemsB[t]
                kvh, n = unitsB[u]
                hs = kvh % 2
                if kvh not in loadedB:
                    loadedB.add(kvh)
                    load_kvh(kvh)
                kc = klist_of(n)[i][0]
                st_ = t % 2
                for j in range(2):
                    mm(C, se[st_][:, j * 128:(j + 1) * 128], KB[hs][0:64, kc * 128:(kc + 1) * 128],
                       QB[hs][0:64, j, n * 128:(n + 1) * 128], True, True, [KBB[hs], QBB[hs]], [seB[st_]])
                    mm(C, so[st_][:, j * 128:(j + 1) * 128], KB[hs][64:128, kc * 128:(kc + 1) * 128],
                       QB[hs][64:128, j, n * 128:(n + 1) * 128], True, True, [KBB[hs], QBB[hs]], [soB[st_]])

            def exB(t):
                u, i = itemsB[t]
                st_ = t % 2
                pt_ = t % 3
                act(C, pT[pt_][:, 0:256], se[st_][:, 0:256], AF.Exp, [seB[st_]], [pTB[pt_][0]], scale=0.125)
                act(C, pT[pt_][:, 256:512], so[st_][:, 0:256], AF.Exp, [soB[st_]], [pTB[pt_][1]], scale=0.125)

            def maskB(t):
                u, i = itemsB[t]
                kvh, n = unitsB[u]
                pt_ = t % 3
                m = klist_of(n)[i][1]
                if m is not None:
                    tt(C, pT[pt_].rearrange("p (a b) -> p a b", a=4), pT[pt_].rearrange("p (a b) -> p a b", a=4),
                       masks_bf[:, m, :].unsqueeze(1).broadcast_to([128, 4, 128]), ALU.mult, list(pTB[pt_]), list(pTB[pt_]), eng="pool")

            def pvB(t):
                u, i = itemsB[t]
                kvh, n = unitsB[u]
                hs = kvh % 2
                ob = u % 2
                kc = klist_of(n)[i][0]
                pt_ = t % 3
                for r in range(2):
                    mm(C, OB[ob][r * 64:(r + 1) * 64, 0:256], VB[hs][:, kc, :], pT[pt_][:, r * 256:(r + 1) * 256],
                       i == 0, i == 4, [VBB[hs], pTB[pt_][r]], [OBB[ob]])
                    mm(C, LB[ob][r * 64:(r + 1) * 64, 0:256], ones_bf[:, 0:64], pT[pt_][:, r * 256:(r + 1) * 256],
                       i == 0, i == 4, [pTB[pt_][r]], [LBB[ob]])
                return i == 4

            def epiB(u):
                kvh, n = unitsB[u]
                ob = u % 2
                tt(C, lbuf[ob], LB[ob][:, 0:256], es[:, kvh, :], ALU.add, [LBB[ob], esB], [lB[ob]])
                C.P.add("dve", lambda e: e.reciprocal(out=rbuf[ob], in_=lbuf[ob]), [lB[ob]], [rB[ob]])
                tt(C, attn_out[:, 2 * kvh:2 * kvh + 2, n * 128:(n + 1) * 128], OB[ob][:, 0:256].rearrange("p (j q) -> p j q", j=2),
                   rbuf[ob].rearrange("p (j q) -> p j q", j=2), ALU.mult, [OBB[ob], rB[ob]], [aoB])

            for s_ in range(TB + 3):
                if 0 <= s_ - 3 < TB:
                    if pvB(s_ - 3):
                        epiB(itemsB[s_ - 3][0])
                if 0 <= s_ - 2 < TB:
                    maskB(s_ - 2)
                if 0 <= s_ - 1 < TB:
                    exB(s_ - 1)
                if s_ < TB:
                    qkB(s_)
            C.P.emit(C)
        if stop_after == 5:
            finish(C)
            return nc

        C.P = Prog(nc, "p4b")
        with contextlib.ExitStack() as ps:
            new_banks(C, ps, "4b")
            alloc_regA(C)
            C.lnb = [xtmp[:, 0, :]]
            C.rsb = C.lnb
            C.tmpf = [xtmp[:, 2, :], xtmp[:, 3, :]]
            C.lnB = [Buf("ln")]; C.rsB = C.lnB
            C.tmpfB = [Buf("tmpf0"), Buf("tmpf1")]
            attn_out, _ = carve(regC, 0, [8, XR], BF16)
            aoB = Buf("attn_out")
            yB = Buf("y", group=True)
            hb = two_h(C)
            tl = []
            for ti in range(4):
                col, ntok = 512 * ti, 512
                tl.append((x_sb[:, :, col:col + ntok], Buf("x_%d" % ti), ntok, col, 0))

            def proj4b(ti):
                xap, xB, ntok, col, w = tl[ti]
                out_proj(C, xap, xB, ntok, col, attn_out, aoB, "woB_h", [castB["woB_0"]], 1, 0)

            def norm4b(ti):
                xap, xB, ntok, col, w = tl[ti]
                C.h, C.hB = hb[ti % 2]
                norm_mod(C, xap, xB, ntok, 1, 6, 7, 0)
            proj4b(0)
            norm4b(0)
            for ti in range(4):
                xap, xB, ntok, col, w = tl[ti]
                C.h, C.hB = hb[ti % 2]
                ffn_a(C, ntok, 3)
                nxt = ti + 1 < 4
                if nxt:
                    proj4b(ti + 1)
                ffn_b(C, xap, xB, ntok, 3, 1, 8, 0, mid=((lambda ti=ti: norm4b(ti + 1)) if nxt else None))
                dma(C, "sp", yT[:, col:col + ntok].rearrange("(c p) t -> p c t", p=128), xap, [xB], [yB], yB)
            C.P.emit(C)
        finish(C)
    return nc


_CACHE = {}


def kernel(**inputs):
    shared = prep_shared(inputs)
    in_maps = []
    for core in range(8):
        b, h = core // 2, core % 2
        m = dict(shared)
        m.update(prep_core(inputs, b, h))
        in_maps.append(m)
    if "nc" not in _CACHE:
        _CACHE["nc"] = build()
    nc = _CACHE["nc"]
    res = run_bass_kernel_spmd(nc, in_maps, core_ids=list(range(8)))
    out = np.empty((4, SEQ, D), np.float32)
    for core in range(8):
        b, h = core // 2, core % 2
        out[b, h * OWN:(h + 1) * OWN, :] = res.results[core]["yT"].T
    return out
```
